# Optimizing a Trainium2 kernel written in Bass

```python
import math
import jax
import jax.numpy as jnp
from jax import lax
import numpy as np

D_MODEL = 2048
BATCH = 1
SEQ = 16384
DEPTH = 4

N_A_LAYERS = DEPTH // 2
N_B_LAYERS = DEPTH - N_A_LAYERS
RMS_EPS = 1e-6

GDN_HEADS = 16
GDN_DK = 128
GDN_DV = 128
GDN_CONV = 4
GDN_CHUNK = 64
GDN_QK_W = GDN_HEADS * GDN_DK
GDN_V_W = GDN_HEADS * GDN_DV
GDN_CONV_W = 2 * GDN_QK_W + GDN_V_W
GDN_PROJ = GDN_CONV_W + GDN_V_W + 2 * GDN_HEADS

DSA_GROUPS = ((128, 1), (512, 4), (2048, 16))
N_GROUPS = len(DSA_GROUPS)
DSA_HEADS = 16
DSA_DH = 128
DSA_SPAN = 128
DSA_GROUP_W = DSA_HEADS * DSA_DH

NUM_BUCKETS = 32
MAX_DISTANCE = 2048

D_FF = -(-8 * D_MODEL // (3 * 256)) * 256

kernel_name = 'yoco_gdn_dilated_swa_hybrid'


def rms_norm(x, gain):
    xf = x.astype(jnp.float32)
    y = xf * lax.rsqrt(jnp.mean(xf * xf, axis=-1, keepdims=True) + RMS_EPS)
    return (y * gain.astype(jnp.float32)).astype(x.dtype)


def l2norm(x):
    return x * lax.rsqrt(jnp.sum(x * x, axis=-1, keepdims=True) + RMS_EPS)


def swiglu_ffn(h, w_gate_up, w_down):
    gate, up = jnp.split(h @ w_gate_up, 2, axis=-1)
    return (jax.nn.silu(gate) * up) @ w_down


def causal_conv_silu(x, w):
    K, C = w.shape
    y = lax.conv_general_dilated(x, w[:, None, :].astype(x.dtype), window_strides=(1,),
                                 padding=((K - 1, 0),), dimension_numbers=('NWC', 'WIO', 'NWC'),
                                 feature_group_count=C)
    return jax.nn.silu(y)


def chunk_gated_delta_rule(q, k, v, g, beta):
    B, S, H, Dk = q.shape
    Dv = v.shape[-1]
    C = GDN_CHUNK
    N = S // C

    def to_chunks(t):
        t = t.reshape((B, N, C, H) + t.shape[3:])
        return jnp.moveaxis(t, (1, 3), (0, 2))

    qc = to_chunks(q * (Dk ** -0.5))
    kc = to_chunks(k)
    vc = to_chunks(v)
    gc = to_chunks(g)
    bc = to_chunks(beta)
    gcum = jnp.cumsum(gc, axis=-1)
    idx = jnp.arange(C)
    causal = idx[:, None] >= idx[None, :]
    strict = idx[:, None] > idx[None, :]
    decay = jnp.exp(jnp.where(causal, gcum[..., :, None] - gcum[..., None, :], -jnp.inf))
    kb = kc * bc[..., None]
    low = jnp.where(strict, jnp.einsum('nbhid,nbhjd->nbhij', kb, kc) * decay, 0.0)
    a_mat = low + jnp.eye(C, dtype=jnp.float32)
    rhs = jnp.concatenate([vc * bc[..., None], kb * jnp.exp(gcum)[..., None]], axis=-1)
    sol = lax.linalg.triangular_solve(a_mat, rhs, left_side=True, lower=True, unit_diagonal=True)
    u = sol[..., :Dv]
    w = sol[..., Dv:]
    intra = jnp.einsum('nbhid,nbhjd->nbhij', qc, kc) * decay
    q_dec = qc * jnp.exp(gcum)[..., None]
    k_dec = kc * jnp.exp(gcum[..., -1:] - gcum)[..., None]
    g_last = jnp.exp(gcum[..., -1])

    def step(state, inp):
        q_i, k_i, u_i, w_i, a_i, gl = inp
        v_new = u_i - jnp.einsum('bhck,bhkv->bhcv', w_i, state)
        o = jnp.einsum('bhck,bhkv->bhcv', q_i, state) + jnp.einsum('bhij,bhjv->bhiv', a_i, v_new)
        state = state * gl[..., None, None] + jnp.einsum('bhck,bhcv->bhkv', k_i, v_new)
        return state, o

    state0 = jnp.zeros((B, H, Dk, Dv), jnp.float32)
    _, o = lax.scan(step, state0, (q_dec, k_dec, u, w, intra, g_last))
    return jnp.moveaxis(o, (0, 2), (1, 3)).reshape(B, S, H, Dv)


def gdn_mixer(h, w_in, conv_w, a_log, dt_bias, out_norm, w_out):
    B, S, _ = h.shape
    proj = h @ w_in
    qkv = causal_conv_silu(proj[..., :GDN_CONV_W], conv_w).astype(jnp.float32)
    z = proj[..., GDN_CONV_W:GDN_CONV_W + GDN_V_W]
    b = proj[..., GDN_CONV_W + GDN_V_W:GDN_CONV_W + GDN_V_W + GDN_HEADS]
    a = proj[..., GDN_CONV_W + GDN_V_W + GDN_HEADS:]
    q = l2norm(qkv[..., :GDN_QK_W].reshape(B, S, GDN_HEADS, GDN_DK))
    k = l2norm(qkv[..., GDN_QK_W:2 * GDN_QK_W].reshape(B, S, GDN_HEADS, GDN_DK))
    v = qkv[..., 2 * GDN_QK_W:].reshape(B, S, GDN_HEADS, GDN_DV)
    beta = jax.nn.sigmoid(b.astype(jnp.float32))
    g = -jnp.exp(a_log.astype(jnp.float32)) * jax.nn.softplus(a.astype(jnp.float32) + dt_bias.astype(jnp.float32))
    o = chunk_gated_delta_rule(q, k, v, g, beta)
    zf = z.astype(jnp.float32).reshape(B, S, GDN_HEADS, GDN_DV)
    o = rms_norm(o, out_norm) * jax.nn.silu(zf)
    return o.reshape(B, S, GDN_V_W).astype(h.dtype) @ w_out


def t5_bucket(dist):
    max_exact = NUM_BUCKETS // 2
    d_f = jnp.maximum(dist, 1).astype(jnp.float32)
    large = max_exact + (jnp.log(d_f / max_exact) / math.log(MAX_DISTANCE / max_exact)
                         * (NUM_BUCKETS - max_exact)).astype(jnp.int32)
    large = jnp.minimum(large, NUM_BUCKETS - 1)
    return jnp.where(dist < max_exact, dist, large)


def dilated_attention(q, k, v, bias_table, dilation):
    B, S, H, Dh = q.shape
    W = DSA_SPAN
    L = S // dilation
    nb = -(-L // W)
    Lp = nb * W

    def to_sub(t):
        t = t.reshape(B, L, dilation, H, Dh).transpose(0, 2, 1, 3, 4)
        t = jnp.pad(t, ((0, 0), (0, 0), (0, Lp - L), (0, 0), (0, 0)))
        return t.reshape(B, dilation, nb, W, H, Dh)

    def with_prev(t):
        prev = jnp.pad(t, ((0, 0), (0, 0), (1, 0), (0, 0), (0, 0), (0, 0)))[:, :, :-1]
        return jnp.concatenate([prev, t], axis=3)

    qb = to_sub(q)
    kk = with_prev(to_sub(k))
    vv = with_prev(to_sub(v))
    s = jnp.einsum('brnqhd,brnkhd->brnhqk', qb, kk, preferred_element_type=jnp.float32) * (Dh ** -0.5)
    qi = jnp.arange(W)[:, None]
    kj = jnp.arange(2 * W)[None, :]
    rel = qi + W - kj
    band = (rel >= 0) & (rel <= W)
    bucket = t5_bucket(jnp.clip(rel, 0, W) * dilation)
    bias = jnp.take(bias_table, bucket, axis=0).transpose(2, 0, 1).astype(jnp.float32)
    key_pos = jnp.arange(nb)[:, None] * W - W + kj
    valid = band[None] & (key_pos >= 0)[:, None, :]
    s = jnp.where(valid[None, None, :, None], s + bias[None, None, None], -jnp.inf)
    lse = jax.nn.logsumexp(s, axis=-1)
    p = jnp.exp(s - lse[..., None])
    o = jnp.einsum('brnhqk,brnkhd->brnqhd', p, vv.astype(jnp.float32))
    o = o.reshape(B, dilation, Lp, H, Dh)[:, :, :L].transpose(0, 2, 1, 3, 4).reshape(B, S, H, Dh)
    lse = jnp.swapaxes(lse, -1, -2).reshape(B, dilation, Lp, H)[:, :, :L]
    lse = lse.transpose(0, 2, 1, 3).reshape(B, S, H)
    return o, lse


def dsa_mixer(h, w_q, k_all, v_all, rel_bias, w_out):
    B, S, _ = h.shape
    q_all = (h @ w_q).reshape(B, S, N_GROUPS, DSA_HEADS, DSA_DH)
    outs = []
    lses = []
    for gi, (window, dilation) in enumerate(DSA_GROUPS):
        o, lse = dilated_attention(q_all[:, :, gi], k_all[:, :, gi], v_all[:, :, gi],
                                   rel_bias[:, gi * DSA_HEADS:(gi + 1) * DSA_HEADS], dilation)
        outs.append(o)
        lses.append(lse)
    wts = jax.nn.softmax(jnp.stack(lses, axis=0), axis=0)
    o = jnp.sum(wts[..., None] * jnp.stack(outs, axis=0), axis=0)
    return o.reshape(B, S, DSA_GROUP_W).astype(h.dtype) @ w_out


def shared_kv(x, kv_norm, kv_w):
    B, S, _ = x.shape
    kv = (rms_norm(x, kv_norm) @ kv_w).reshape(B, S, 2, N_GROUPS, DSA_HEADS, DSA_DH)
    return kv[:, :, 0], kv[:, :, 1]


def setup_inputs(seed: int = 0) -> dict:
    key = jax.random.key(seed)
    ks = jax.random.split(key, 16)
    f32 = jnp.float32

    def dense(k, shape):
        return jax.random.normal(k, shape, f32) * shape[-2] ** -0.5

    x = jax.random.normal(ks[0], (BATCH, SEQ, D_MODEL), f32)
    norm_gains = 1.0 + 0.02 * jax.random.normal(ks[1], (DEPTH, 4, D_MODEL), f32)
    ffn_w_gate_up = dense(ks[2], (DEPTH, D_MODEL, 2 * D_FF))
    ffn_w_down = dense(ks[3], (DEPTH, D_FF, D_MODEL))
    gdn_w_in = dense(ks[4], (N_A_LAYERS, D_MODEL, GDN_PROJ))
    gdn_conv_w = jax.random.normal(ks[5], (N_A_LAYERS, GDN_CONV, GDN_CONV_W), f32) * GDN_CONV ** -0.5
    gdn_a_log = jnp.log(jax.random.uniform(ks[6], (N_A_LAYERS, GDN_HEADS), f32, 1.0, 16.0))
    dt = jnp.exp(jax.random.uniform(ks[7], (N_A_LAYERS, GDN_HEADS), f32, math.log(1e-3), math.log(1e-1)))
    gdn_dt_bias = dt + jnp.log(-jnp.expm1(-dt))
    gdn_out_norm = 1.0 + 0.02 * jax.random.normal(ks[8], (N_A_LAYERS, GDN_DV), f32)
    gdn_w_out = dense(ks[9], (N_A_LAYERS, GDN_V_W, D_MODEL))
    kv_norm = 1.0 + 0.02 * jax.random.normal(ks[10], (D_MODEL,), f32)
    kv_w = dense(ks[11], (D_MODEL, 2 * N_GROUPS * DSA_GROUP_W))
    dsa_w_q = dense(ks[12], (N_B_LAYERS, D_MODEL, N_GROUPS * DSA_GROUP_W))
    dsa_w_out = dense(ks[13], (N_B_LAYERS, DSA_GROUP_W, D_MODEL))
    rel_bias = 0.1 * jax.random.normal(ks[14], (NUM_BUCKETS, N_GROUPS * DSA_HEADS), f32)
    return {'x': x, 'norm_gains': norm_gains, 'ffn_w_gate_up': ffn_w_gate_up, 'ffn_w_down': ffn_w_down,
            'gdn_w_in': gdn_w_in, 'gdn_conv_w': gdn_conv_w, 'gdn_a_log': gdn_a_log,
            'gdn_dt_bias': gdn_dt_bias, 'gdn_out_norm': gdn_out_norm, 'gdn_w_out': gdn_w_out,
            'kv_norm': kv_norm, 'kv_w': kv_w, 'dsa_w_q': dsa_w_q, 'dsa_w_out': dsa_w_out,
            'rel_bias': rel_bias}


def reference(x, norm_gains, ffn_w_gate_up, ffn_w_down, gdn_w_in, gdn_conv_w, gdn_a_log,
              gdn_dt_bias, gdn_out_norm, gdn_w_out, kv_norm, kv_w, dsa_w_q, dsa_w_out, rel_bias):
    k_all = None
    v_all = None
    for layer in range(DEPTH):
        gains = norm_gains[layer]
        h = rms_norm(x, gains[0])
        if layer < N_A_LAYERS:
            m = gdn_mixer(h, gdn_w_in[layer], gdn_conv_w[layer], gdn_a_log[layer], gdn_dt_bias[layer],
                          gdn_out_norm[layer], gdn_w_out[layer])
        else:
            j = layer - N_A_LAYERS
            m = dsa_mixer(h, dsa_w_q[j], k_all, v_all, rel_bias, dsa_w_out[j])
        x = x + rms_norm(m, gains[1])
        f = swiglu_ffn(rms_norm(x, gains[2]), ffn_w_gate_up[layer], ffn_w_down[layer])
        x = x + rms_norm(f, gains[3])
        if layer == N_A_LAYERS - 1:
            k_all, v_all = shared_kv(x, kv_norm, kv_w)
    return x
```

```python
import contextlib
import os
import numpy as np
import ml_dtypes
import concourse.bass as bass
import concourse.mybir as mybir
from concourse.bass_utils import run_bass_kernel_spmd

F32 = mybir.dt.float32
BF16 = mybir.dt.bfloat16
ALU = mybir.AluOpType
AF = mybir.ActivationFunctionType
AX = mybir.AxisListType

NCORES = 8
D = 2048
S = 16384
TPC = S // NCORES
DFF = 5632
EPS = 1e-6
NEG = -30000.0


class Buf:
    __slots__ = ("name", "last_writer", "readers", "sem", "dma_cnt", "last_dma", "excl")

    def __init__(self, name):
        self.name = name
        self.excl = False
        self.last_writer = None
        self.readers = []
        self.sem = None
        self.dma_cnt = 0
        self.last_dma = None


class Op:
    __slots__ = ("eng", "fn", "deps", "signal", "token", "is_dma", "dbuf")

    def __init__(self, eng, fn, is_dma=False, dbuf=None):
        self.eng = eng
        self.fn = fn
        self.deps = []
        self.signal = False
        self.token = None
        self.is_dma = is_dma
        self.dbuf = dbuf


class Prog:
    ENGS = ("pe", "act", "dve", "pool", "sp")

    NPROG = [0]

    def __init__(self, nc, stack):
        self.nc = nc
        self.stack = stack
        Prog.NPROG[0] += 1
        self.tag = f"_g{Prog.NPROG[0]}"
        self.ops = []
        self.bufs = []
        self.nbuf = 0

    def buf(self, name=None):
        b = Buf(name or f"b{self.nbuf}")
        self.nbuf += 1
        self.bufs.append(b)
        return b

    def sb(self, name, shape, dt):
        t = self.stack.enter_context(self.nc.sbuf_tensor(name + self.tag, list(shape), dt))
        return t, self.buf(name)

    def ps(self, name, shape, dt=F32):
        t = self.stack.enter_context(self.nc.psum_tensor(name + self.tag, list(shape), dt))
        b = self.buf(name)
        b.excl = True
        return t, b

    def _deps(self, op, reads, writes):
        deps = []
        for r in reads:
            if r.last_writer is not None:
                deps.append(r.last_writer)
        for w in writes:
            if w.last_writer is not None:
                deps.append(w.last_writer)
            deps.extend(w.readers)
        seen = set()
        for d in deps:
            if d is op or id(d) in seen:
                continue
            seen.add(id(d))
            if d.eng == "pe" and op.eng == "pe" and not d.is_dma and not op.is_dma:
                continue
            op.deps.append(d)
            d.signal = True
        for w in writes:
            w.last_writer = op
            w.readers = []
        for r in reads:
            if r not in writes:
                if op.is_dma:
                    r.readers.append(op)
                else:
                    r.readers = [x for x in r.readers if x.is_dma or x.eng != op.eng]
                    r.readers.append(op)

    def op(self, eng, fn, reads=(), writes=()):
        o = Op(eng, fn)
        reads = list(reads)
        writes = list(writes)
        for r in list(reads):
            if r.excl:
                reads.remove(r)
                if r not in writes:
                    writes.append(r)
        self._deps(o, reads, writes)
        self.ops.append(o)
        return o

    def dma(self, queue, fn, reads=(), writes=(), sbuf=None):
        o = Op(queue, fn, is_dma=True, dbuf=sbuf)
        self._deps(o, list(reads), list(writes))
        if sbuf.last_dma is not None and sbuf.last_dma not in o.deps:
            o.deps.append(sbuf.last_dma)
        sbuf.last_dma = o
        sbuf.dma_cnt += 16
        o.token = (sbuf, sbuf.dma_cnt)
        o.signal = True
        self.ops.append(o)
        return o

    def barrier(self):
        last = {}
        for o in self.ops:
            if o.is_dma:
                last[("dma", id(o.dbuf))] = o
            else:
                last[o.eng] = o
        for e in self.ENGS:
            o = Op(e, None)
            for k, d in last.items():
                if not d.is_dma and d.eng == e and e == "pe":
                    continue
                o.deps.append(d)
                d.signal = True
            self.ops.append(o)
        for b in self.bufs:
            b.last_writer = None
            b.readers = []

    def emit(self):
        nc = self.nc
        st = self.stack
        esem = {e: st.enter_context(nc.semaphore(f"sem_{e}{self.tag}")) for e in self.ENGS}
        cnt = {e: 0 for e in self.ENGS}
        for o in self.ops:
            if o.is_dma:
                if o.dbuf.sem is None:
                    o.dbuf.sem = st.enter_context(nc.semaphore(f"sd_{o.dbuf.name}{self.tag}"))
                o.token = (o.dbuf.sem, o.token[1])
            elif o.signal and o.fn is not None:
                cnt[o.eng] += 1
                o.token = (esem[o.eng], cnt[o.eng])
        block = st.enter_context(nc.Block())
        per = {e: [o for o in self.ops if o.eng == e] for e in self.ENGS}

        def run(e, eng):
            known = {}
            for o in per[e]:
                need = {}
                for d in o.deps:
                    if d.token is None:
                        continue
                    s, c = d.token
                    if known.get(id(s), 0) >= c:
                        continue
                    if need.get(id(s), (None, 0))[1] < c:
                        need[id(s)] = (s, c)
                for s, c in need.values():
                    eng.wait_ge(s, c)
                    known[id(s)] = c
                if o.fn is None:
                    continue
                ins = o.fn(eng)
                if o.is_dma:
                    ins.then_inc(o.token[0], 16)
                elif o.signal:
                    ins.then_inc(o.token[0], 1)

        @block.tensor
        def _(eng):
            run("pe", eng)

        @block.scalar
        def _(eng):
            run("act", eng)

        @block.vector
        def _(eng):
            run("dve", eng)

        @block.gpsimd
        def _(eng):
            run("pool", eng)

        @block.sync
        def _(eng):
            run("sp", eng)


TB = 512


def _rstd_from_sq(P, sqt, sqb, KC, ones, onesb, pss, pssb, rs, rsb, inv_n):
    for c in range(KC):
        P.op("pe", lambda e, c=c: e.matmul(pss[:], ones[:], sqt[:, c, :], start=(c == 0), stop=(c == KC - 1)),
             reads=[onesb, sqb], writes=[pssb])
    P.op("act", lambda e: e.activation(rs[:], pss[:], AF.Sqrt, bias=EPS, scale=inv_n), reads=[pssb], writes=[rsb])
    P.op("dve", lambda e: e.reciprocal(rs[:], rs[:]), reads=[rsb], writes=[rsb])


def piece_norm(nc, x_d, h_d, gains_d, gcol, T, f_d=None, gcol_post=None, x_out_d=None):
    KC = D // 128
    with contextlib.ExitStack() as st:
        P = Prog(nc, st)
        ones, onesb = P.sb("ones", [128, 128], BF16)
        gt, gtb = P.sb("gains", [128, gains_d.shape[1]], F32)
        P.op("pool", lambda e: e.memset(ones[:], 1.0), writes=[onesb])
        P.dma("sp", lambda e: e.dma_start(out=gt[:], in_=gains_d[:, :]), writes=[gtb], sbuf=gtb)
        xts = [P.sb(f"xt{i}", [128, KC, TB], F32) for i in range(2)]
        fts = [P.sb(f"ft{i}", [128, KC, TB], F32) for i in range(2)] if f_d is not None else None
        sqs = [P.sb(f"sq{i}", [128, KC, TB], BF16) for i in range(1)]
        hts = [P.sb(f"ht{i}", [128, KC, TB], BF16) for i in range(2)] if h_d is not None else None
        rss = [P.sb(f"rs{i}", [128, TB], F32) for i in range(2)]
        psl = [P.ps(f"pss{i}", [128, TB]) for i in range(2)]
        nb = T // TB
        for tb in range(nb):
            sl = slice(tb * TB, (tb + 1) * TB)
            xt, xb = xts[tb % 2]
            sq, sqb = sqs[0]
            P.dma("sp", lambda e, xt=xt, sl=sl: e.dma_start(out=xt[:], in_=x_d[:, :, sl].rearrange("c p t -> p c t")),
                  writes=[xb], sbuf=xb)
            if f_d is not None:
                ft, fb = fts[tb % 2]
                P.dma("sp", lambda e, ft=ft, sl=sl: e.dma_start(out=ft[:], in_=f_d[:, :, sl].rearrange("c p t -> p c t")),
                      writes=[fb], sbuf=fb)
                P.op("act", lambda e, ft=ft, sq=sq: e.activation(sq[:], ft[:], AF.Square), reads=[fb], writes=[sqb])
                rs, rsb = rss[0]
                pss, pssb = psl[0]
                _rstd_from_sq(P, sq, sqb, KC, ones, onesb, pss, pssb, rs, rsb, 1.0 / D)
                for c in range(KC):
                    eng = "dve"
                    P.op(eng, lambda e, c=c, ft=ft, rs=rs: e.scalar_tensor_tensor(
                        ft[:, c, :], ft[:, c, :], gt[:, gcol_post + c:gcol_post + c + 1], rs[:], ALU.mult, ALU.mult),
                        reads=[fb, rsb, gtb], writes=[fb])
                P.op("pool", lambda e, xt=xt, ft=ft: e.tensor_tensor(xt[:], xt[:], ft[:], ALU.add),
                     reads=[xb, fb], writes=[xb])
                P.dma("sp", lambda e, xt=xt, sl=sl: e.dma_start(out=x_out_d[:, :, sl].rearrange("c p t -> p c t"), in_=xt[:]),
                      reads=[xb], sbuf=xb)
            if h_d is not None:
                ht, hb = hts[tb % 2]
                P.op("act", lambda e, xt=xt, sq=sq: e.activation(sq[:], xt[:], AF.Square), reads=[xb], writes=[sqb])
                rs, rsb = rss[1]
                pss, pssb = psl[1]
                _rstd_from_sq(P, sq, sqb, KC, ones, onesb, pss, pssb, rs, rsb, 1.0 / D)
                for c in range(KC):
                    eng = "dve"
                    P.op(eng, lambda e, c=c, xt=xt, ht=ht, rs=rs: e.scalar_tensor_tensor(
                        ht[:, c, :], xt[:, c, :], gt[:, gcol + c:gcol + c + 1], rs[:], ALU.mult, ALU.mult),
                        reads=[xb, rsb, gtb], writes=[hb])
                P.dma("sp", lambda e, ht=ht, sl=sl: e.dma_start(out=h_d[:, :, sl].rearrange("c p t -> p c t"), in_=ht[:]),
                      reads=[hb], sbuf=hb)
        P.barrier()
        P.emit()


def piece_gemm(nc, act_d, KC, T, W_d, groups, evac_factory, TR=None, extra=None):
    TR = TR or T
    maxw = max(sum(n for _, n in g[0]) for g in groups)
    with contextlib.ExitStack() as st:
        P = Prog(nc, st)
        act, actb = P.sb("act", [128, KC, TR], BF16)
        wsl = [P.sb(f"w{i}", [128, KC, maxw], BF16) for i in range(3)]
        psl = [P.ps(f"pg{i}", [128, TB]) for i in range(6)]
        evac = evac_factory(P)
        if extra is not None:
            extra_fn = extra(P)
        pi = 0
        gi = 0
        for half in range(T // TR):
            P.dma("sp", lambda e, half=half: e.dma_start(
                out=act[:], in_=act_d[:, :, half * TR:(half + 1) * TR].rearrange("c p t -> p c t")),
                writes=[actb], sbuf=actb)
            if extra is not None:
                extra_fn(act, actb, half)
            ug = 0
            for cols, units in groups:
                w, wb = wsl[gi % 3]
                gi += 1
                off = 0
                for (c0, n) in cols:
                    P.dma("pool", lambda e, w=w, off=off, c0=c0, n=n: e.dma_start(
                        out=w[:, :, off:off + n], in_=W_d[:, c0:c0 + n].rearrange("(k p) n -> p k n", p=128)),
                        writes=[wb], sbuf=wb)
                    off += n
                for unit in units:
                    for tb in range(TR // TB):
                        pss = []
                        for uo in unit:
                            ps, psb = psl[pi % 6]
                            pi += 1
                            for k in range(KC):
                                P.op("pe", lambda e, ps=ps, w=w, uo=uo, k=k, tb=tb: e.matmul(
                                    ps[:], w[:, k, uo:uo + 128], act[:, k, tb * TB:(tb + 1) * TB],
                                    start=(k == 0), stop=(k == KC - 1)), reads=[wb, actb], writes=[psb])
                            pss.append((ps, psb))
                        evac(ug, half * (TR // TB) + tb, pss)
                    ug += 1
        P.barrier()
        P.emit()


def store_evac(out_ap_fn, TR, dt, nslots=3):
    def factory(P):
        tiles = [P.sb(f"ot{i}", [128, TR], dt) for i in range(nslots)]
        state = {"n": 0, "t": -1, "key": None}
        nb = TR // TB

        def evac(u, tb, pss):
            key = (u, tb // nb)
            if key != state["key"]:
                state["key"] = key
                state["t"] += 1
            ot, ob = tiles[state["t"] % nslots]
            ps, psb = pss[0]
            i = state["n"]
            state["n"] += 1
            sl = slice((tb % nb) * TB, (tb % nb + 1) * TB)
            if i % 2 == 0:
                P.op("act", lambda e: e.activation(ot[:, sl], ps[:], AF.Copy), reads=[psb], writes=[ob])
            else:
                P.op("dve", lambda e: e.tensor_copy(ot[:, sl], ps[:]), reads=[psb], writes=[ob])
            if tb % nb == nb - 1:
                dst = out_ap_fn(u, tb // nb)
                P.dma("sp", lambda e: e.dma_start(out=dst, in_=ot[:]), reads=[ob], sbuf=ob)
        return evac
    return factory


SEG = 2048
CH = 64


def gdn_consts():
    idx = np.arange(128)
    same = (idx[:, None] // CH) == (idx[None, :] // CH)
    U = (same & (idx[:, None] <= idx[None, :])).astype(np.float32)
    BM = same.astype(np.float32)
    negm = np.where(same & (idx[None, :] >= idx[:, None]), 0.0, NEG).astype(np.float32)
    strict = (same & (idx[None, :] > idx[:, None])).astype(np.float32)
    ident = np.eye(128, dtype=np.float32)
    cs = (idx % CH == 0).astype(np.float32)[:, None]
    half = np.stack([(idx < CH), (idx >= CH)], 1).astype(np.float32)
    sel = np.zeros((2, 128), np.float32)
    sel[0] = 1.0
    c = np.concatenate([U, BM, negm, strict, ident, cs, half, np.ones((128, 1), np.float32)], 1)
    return {"gc": np.ascontiguousarray(c), "gsel": sel}


GC_U, GC_BM, GC_NEG, GC_STR, GC_ID, GC_CS, GC_HALF, GC_ONE = 0, 128, 256, 384, 512, 640, 641, 643
GC_W = 644


class _Stop(Exception):
    pass


def stage_gdn_seq(nc, pj_d, ba_d, convw_d, hp_d, onorm_d, gc_d, gsel_d, out_d, SL):
    try:
        _stage_gdn_seq(nc, pj_d, ba_d, convw_d, hp_d, onorm_d, gc_d, gsel_d, out_d, SL)
    except _Stop:
        pass


def _stage_gdn_seq(nc, pj_d, ba_d, convw_d, hp_d, onorm_d, gc_d, gsel_d, out_d, SL):
    LVL = float(os.environ.get("GDN_DBG", "99"))

    def chk(P, k):
        if LVL <= k:
            P.barrier()
            P.emit()
            raise _Stop()
    NT = SL // 128
    NSEG = SL // SEG
    with contextlib.ExitStack() as st:
        P = Prog(nc, st)
        gc, gcb = P.sb("gc", [128, GC_W], F32)
        P.dma("sp", lambda e: e.dma_start(out=gc[:], in_=gc_d[:, :]), writes=[gcb], sbuf=gcb)
        gsel, gselb = P.sb("gsel", [2, 128], F32)
        P.dma("sp", lambda e: e.dma_start(out=gsel[:], in_=gsel_d[:, :]), writes=[gselb], sbuf=gselb)
        cw, cwb = P.sb("cw", [128, 6, 4], F32)
        P.dma("sp", lambda e: e.dma_start(out=cw[:], in_=convw_d[:, :, :]), writes=[cwb], sbuf=cwb)
        hp, hpb = P.sb("hp", [128, 4], F32)
        P.dma("sp", lambda e: e.dma_start(out=hp[:], in_=hp_d[:, :]), writes=[hpb], sbuf=hpb)
        onm, onmb = P.sb("onm", [128, 1], F32)
        P.dma("sp", lambda e: e.dma_start(out=onm[:], in_=onorm_d[:, :]), writes=[onmb], sbuf=onmb)
        cbf, cbfb = P.sb("cbf", [128, 3, 128], BF16)
        P.op("dve", lambda e: e.tensor_copy(cbf[:, 0, :], gc[:, GC_ID:GC_ID + 128]), reads=[gcb], writes=[cbfb])
        P.op("dve", lambda e: e.tensor_copy(cbf[:, 1, :], gc[:, GC_STR:GC_STR + 128]), reads=[gcb], writes=[cbfb])
        P.op("dve", lambda e: e.memset(cbf[:, 2, :], 1.0), writes=[cbfb])
        identb = cbf[:, 0, :]
        onesb16 = cbf[:, 2, :]
        U = gc[:, GC_U:GC_U + 128]
        pf = [P.ps(f"pf{i}", [128, 512]) for i in range(6)]
        pb = [P.ps(f"pb{i}", [128, 1024], BF16) for i in range(2)]
        ba, bab = P.sb("ba", [128, NT, 4], F32)
        P.dma("sp", lambda e: e.dma_start(out=ba[:], in_=ba_d.rearrange("(n p) x -> p n x", p=128)), writes=[bab], sbuf=bab)
        gt, gtb = P.sb("gt", [128, NT, 2], F32)
        beta, betab = P.sb("beta", [128, NT, 2], F32)
        gcum, gcumb = P.sb("gcum", [128, NT, 2], F32)
        eg, egb = P.sb("eg", [128, NT, 2], F32)
        edec, edecb = P.sb("edec", [128, NT, 2], F32)
        gh, ghb = P.sb("gh", [128, NT, 2, 2], F32)
        glx, glxb = P.sb("glx", [128, NT, 2, 2], F32)
        gm, gmb = P.sb("gm", [128, NT, 2, 4], F32)
        tmp, tmpb = P.sb("gtmp", [128, NT, 2], F32)
        ea, eab = P.sb("ea", [128, 2], F32)
        P.op("act", lambda e: e.activation(beta[:], ba[:, :, 0:2], AF.Exp, scale=-1.0), reads=[bab], writes=[betab])
        P.op("dve", lambda e: e.tensor_scalar(beta[:], beta[:], 1.0, 1.0, ALU.mult, ALU.add), reads=[betab], writes=[betab])
        P.op("dve", lambda e: e.reciprocal(beta[:], beta[:]), reads=[betab], writes=[betab])
        for e_ in range(2):
            P.op("act", lambda e, e_=e_: e.activation(tmp[:, :, e_], ba[:, :, 2 + e_], AF.Exp, bias=hp[:, 2 + e_:3 + e_]),
                 reads=[bab, hpb], writes=[tmpb])
        P.op("act", lambda e: e.activation(tmp[:], tmp[:], AF.Ln, bias=1.0), reads=[tmpb], writes=[tmpb])
        P.op("act", lambda e: e.activation(ea[:], hp[:, 0:2], AF.Exp), reads=[hpb], writes=[eab])
        for e_ in range(2):
            P.op("dve", lambda e, e_=e_: e.tensor_scalar(gt[:, :, e_], tmp[:, :, e_], ea[:, e_:e_ + 1], -1.0, ALU.mult, ALU.mult),
                 reads=[tmpb, eab], writes=[gtb])
        G2 = NT * 2
        def gflat(t):
            return t[:].rearrange("p n e -> p (n e)")
        for c0 in range(0, G2, 512):
            c1 = min(G2, c0 + 512)
            ps, psb = pf[0]
            P.op("pe", lambda e, c0=c0, c1=c1: e.matmul(ps[:, 0:c1 - c0], U, gflat(gt)[:, c0:c1], start=True, stop=True),
                 reads=[gcb, gtb], writes=[psb])
            P.op("dve", lambda e, c0=c0, c1=c1: e.tensor_copy(gflat(gcum)[:, c0:c1], ps[:, 0:c1 - c0]), reads=[psb], writes=[gcumb])
            ps2, ps2b = pf[1]
            P.op("pe", lambda e, c0=c0, c1=c1: e.matmul(ps2[:, 0:c1 - c0], gc[:, GC_BM:GC_BM + 128], gflat(gt)[:, c0:c1], start=True, stop=True),
                 reads=[gcb, gtb], writes=[ps2b])
            P.op("dve", lambda e, c0=c0, c1=c1: e.tensor_tensor(gflat(edec)[:, c0:c1], ps2[:, 0:c1 - c0], gflat(gcum)[:, c0:c1], ALU.subtract),
                 reads=[ps2b, gcumb], writes=[edecb])
        P.op("act", lambda e: e.activation(edec[:], edec[:], AF.Exp), reads=[edecb], writes=[edecb])
        P.op("act", lambda e: e.activation(eg[:], gcum[:], AF.Exp), reads=[gcumb], writes=[egb])
        for h_ in range(2):
            P.op("dve", lambda e, h_=h_: e.tensor_scalar_mul(gh[:, :, :, h_], gt[:], gc[:, GC_HALF + h_:GC_HALF + h_ + 1]),
                 reads=[gtb, gcb], writes=[ghb])
        G4 = NT * 4
        onesf, onesfb = P.sb("onesf", [128, 128], F32)
        P.op("pool", lambda e: e.memset(onesf[:], 1.0), writes=[onesfb])
        for c0 in range(0, G4, 512):
            c1 = min(G4, c0 + 512)
            ps, psb = pf[2]
            P.op("pe", lambda e, c0=c0, c1=c1: e.matmul(ps[:, 0:c1 - c0], onesf[:], gh[:].rearrange("p n e h -> p (n e h)")[:, c0:c1], start=True, stop=True),
                 reads=[onesfb, ghb], writes=[psb])
            P.op("act", lambda e, c0=c0, c1=c1: e.activation(glx[:].rearrange("p n e h -> p (n e h)")[:, c0:c1], ps[:, 0:c1 - c0], AF.Exp),
                 reads=[psb], writes=[glxb])
        P.op("dve", lambda e: e.tensor_copy(gm[:, :, :, 0], gt[:]), reads=[gtb], writes=[gmb])
        P.op("dve", lambda e: e.tensor_scalar_mul(gm[:, :, :, 3], gt[:], -1.0), reads=[gtb], writes=[gmb])
        for k_ in (1, 2):
            P.op("act", lambda e, k_=k_: e.activation(gm[:, :, :, k_], gt[:], AF.Identity, bias=gc[:, GC_CS:GC_CS + 1], scale=0.0),
                 reads=[gtb, gcb], writes=[gmb])
        Rr, Rrb = P.sb("Rr", [2, 2, SEG], F32)
        Lr, Lrb = P.sb("Lr", [2, 2, SEG], F32)
        chk(P, 1)
        Sf = [P.sb(f"Sf{e_}", [128, 128], F32) for e_ in range(2)]
        Sb = [P.sb(f"Sb{e_}", [128, 128], BF16) for e_ in range(2)]
        for e_ in range(2):
            P.op("pool", lambda e, e_=e_: e.memset(Sf[e_][0][:], 0.0), writes=[Sf[e_][1]])
            P.op("pool", lambda e, e_=e_: e.memset(Sb[e_][0][:], 0.0), writes=[Sb[e_][1]])
        xin = [P.sb(f"xin{i}", [128, 4 + SEG], BF16) for i in range(2)]
        acc, accb = P.sb("acc", [128, SEG], F32)
        ysil, ysilb = P.sb("ysil", [128, SEG], F32)
        sq, sqb = P.sb("sqs", [128, SEG], BF16)
        rn, rnb = P.sb("rn", [128, TB], F32)
        egr, egrb = P.sb("egr", [128, TB], F32)
        segq = [P.sb(f"segq{e_}", [128, SEG], BF16) for e_ in range(2)]
        segqd = [P.sb(f"segqd{e_}", [128, SEG], BF16) for e_ in range(2)]
        segk = [P.sb(f"segk{e_}", [128, SEG], BF16) for e_ in range(2)]
        segv = [P.sb(f"segv{e_}", [128, SEG], BF16) for e_ in range(2)]
        segz = [P.sb(f"segz{e_}", [128, SEG], BF16) for e_ in range(2)]
        sego = [P.sb(f"sego{e_}", [128, SEG], BF16) for e_ in range(2)]
        def mk(name, dt, w=128, n=2, p=128):
            return [P.sb(f"{name}{i}", [p, w], dt) for i in range(n)]
        dT = mk("dT", F32); dTs = mk("dTs", F32); Wt = mk("Wt", BF16); inT = mk("inT", BF16)
        Yp = [mk(f"Yp{k_}_", BF16) for k_ in range(6)]
        Wp = [mk(f"Wp{k_}_", BF16) for k_ in range(5)]
        Pm = [mk(f"Pm{k_}_", BF16) for k_ in range(2)]
        kg = mk("kg", BF16); kdec = mk("kdec", BF16); vtok = mk("vtok", BF16); nwT = mk("nwT", BF16)
        vnew = mk("vnew", BF16); otok = mk("otok", F32); onb = mk("onb", BF16)
        ssq = mk("ssq", F32, w=1); junk = mk("junk", F32)
        xi = 0
        pfi = [0]
        pbi = [0]

        def PF():
            i = pfi[0]
            pfi[0] += 1
            t, b = pf[i % 6]
            return t, b

        def PB():
            i = pbi[0]
            pbi[0] += 1
            t, b = pb[i % 2]
            return t, b

        kk = 0
        for sg in range(NSEG):
            t_lo = sg * SEG
            for e_ in range(2):
                for t0 in range(0, SEG // 128, 4):
                    for (dstt, dstb, cofs) in ((Rr, Rrb, 0), (Lr, Lrb, 2)):
                        ps, psb = pf[3 + (kk % 2)]
                        kk += 1
                        for q_ in range(4):
                            P.op("pe", lambda e, ps=ps, e_=e_, t=sg * (SEG // 128) + t0 + q_, q_=q_, cofs=cofs: e.matmul(
                                ps[0:2, q_ * 128:(q_ + 1) * 128], gm[:, t, e_, cofs:cofs + 2], U, start=True, stop=True),
                                reads=[gmb, gcb], writes=[psb])
                        P.op("act", lambda e, ps=ps, dstt=dstt, e_=e_, t0=t0: e.activation(dstt[:, e_, t0 * 128:(t0 + 4) * 128], ps[0:2, :], AF.Copy),
                             reads=[psb], writes=[dstb])
            chk(P, 2)
            for e_ in range(2):
                for kind in range(4):
                    xt, xb = xin[xi % 2]
                    xi += 1
                    P.dma("sp", lambda e, xt=xt, kind=kind, e_=e_, t_lo=t_lo: e.dma_start(out=xt[:, 4:4 + SEG], in_=pj_d[kind, e_, :, t_lo:t_lo + SEG]),
                          writes=[xb], sbuf=xb)
                    if kind == 3:
                        P.op("act", lambda e, xt=xt, e_=e_: e.activation(segz[e_][0][:], xt[:, 4:4 + SEG], AF.Silu), reads=[xb], writes=[segz[e_][1]])
                        continue
                    if sg == 0:
                        P.op("pool", lambda e, xt=xt: e.memset(xt[:, 0:4], 0.0), writes=[xb])
                    else:
                        P.dma("sp", lambda e, xt=xt, kind=kind, e_=e_, t_lo=t_lo: e.dma_start(out=xt[:, 1:4], in_=pj_d[kind, e_, :, t_lo - 3:t_lo]),
                              writes=[xb], sbuf=xb)
                    ci = kind * 2 + e_
                    P.op("dve", lambda e, xt=xt, ci=ci: e.tensor_scalar_mul(acc[:], xt[:, 1:1 + SEG], cw[:, ci, 0:1]),
                         reads=[xb, cwb], writes=[accb])
                    for s_ in range(1, 4):
                        P.op("dve", lambda e, xt=xt, ci=ci, s_=s_: e.scalar_tensor_tensor(acc[:], xt[:, 1 + s_:1 + s_ + SEG], cw[:, ci, s_:s_ + 1], acc[:], ALU.mult, ALU.add),
                             reads=[xb, cwb, accb], writes=[accb])
                    if kind == 2:
                        P.op("act", lambda e, e_=e_: e.activation(segv[e_][0][:], acc[:], AF.Silu), reads=[accb], writes=[segv[e_][1]])
                        continue
                    P.op("act", lambda e: e.activation(ysil[:], acc[:], AF.Silu), reads=[accb], writes=[ysilb])
                    P.op("act", lambda e: e.activation(sq[:], ysil[:], AF.Square), reads=[ysilb], writes=[sqb])
                    for tb in range(SEG // TB):
                        sl = slice(tb * TB, (tb + 1) * TB)
                        ps, psb = PF()
                        P.op("pe", lambda e, ps=ps, sl=sl: e.matmul(ps[:], onesb16, sq[:, sl], start=True, stop=True), reads=[cbfb, sqb], writes=[psb])
                        P.op("act", lambda e, ps=ps: e.activation(rn[:], ps[:], AF.Sqrt, bias=EPS, scale=1.0), reads=[psb], writes=[rnb])
                        P.op("dve", lambda e: e.reciprocal(rn[:], rn[:]), reads=[rnb], writes=[rnb])
                        if kind == 1:
                            P.op("dve", lambda e, sl=sl, e_=e_: e.tensor_tensor(segk[e_][0][:, sl], ysil[:, sl], rn[:], ALU.mult),
                                 reads=[ysilb, rnb], writes=[segk[e_][1]])
                        else:
                            P.op("dve", lambda e, sl=sl, e_=e_: e.scalar_tensor_tensor(segq[e_][0][:, sl], ysil[:, sl], float(128 ** -0.5), rn[:], ALU.mult, ALU.mult),
                                 reads=[ysilb, rnb], writes=[segq[e_][1]])
                            ps2, ps2b = PF()
                            P.op("pe", lambda e, ps2=ps2, sl=sl, e_=e_, t_lo=t_lo: e.matmul(ps2[:], gsel[:], Rr[:, e_, sl], start=True, stop=True),
                                 reads=[gselb, Rrb], writes=[ps2b])
                            P.op("act", lambda e, ps2=ps2: e.activation(egr[:], ps2[:], AF.Exp), reads=[ps2b], writes=[egrb])
                            P.op("dve", lambda e, sl=sl, e_=e_: e.tensor_tensor(segqd[e_][0][:, sl], segq[e_][0][:, sl], egr[:], ALU.mult),
                                 reads=[segq[e_][1], egrb], writes=[segqd[e_][1]])
            chk(P, 3)
            for tl in range(SEG // 128):
                tg = sg * (SEG // 128) + tl
                cs_ = slice(tl * 128, (tl + 1) * 128)
                gs_ = cs_
                for e_ in range(2):
                    w = e_
                    qT = segq[e_][0][:, cs_]; qdT = segqd[e_][0][:, cs_]; kT = segk[e_][0][:, cs_]; vT = segv[e_][0][:, cs_]
                    rb = [segq[e_][1], segqd[e_][1], segk[e_][1], segv[e_][1]]
                    psA, psAb = PF()
                    P.op("pe", lambda e, psA=psA, kT=kT: e.matmul(psA[:, 0:128], kT, kT, start=True, stop=True), reads=[rb[2]], writes=[psAb])
                    P.op("pe", lambda e, psA=psA, kT=kT, qT=qT: e.matmul(psA[:, 128:256], kT, qT, start=True, stop=True), reads=[rb[2], rb[0]], writes=[psAb])
                    P.op("pe", lambda e, psA=psA, e_=e_, gs_=gs_: e.matmul(psA[:, 256:384], Lr[:, e_, gs_], Rr[:, e_, gs_], start=True, stop=False),
                         reads=[Lrb, Rrb], writes=[psAb])
                    P.op("pe", lambda e, psA=psA: e.matmul(psA[:, 256:384], gc[:, GC_ID:GC_ID + 128], gc[:, GC_NEG:GC_NEG + 128], start=False, stop=True),
                         reads=[gcb], writes=[psAb])
                    P.op("act", lambda e, psA=psA, w=w: e.activation(dT[w][0][:], psA[:, 256:384], AF.Exp), reads=[psAb], writes=[dT[w][1]])
                    P.op("pool", lambda e, w=w: e.tensor_tensor(dTs[w][0][:], dT[w][0][:], gc[:, GC_STR:GC_STR + 128], ALU.mult),
                         reads=[dT[w][1], gcb], writes=[dTs[w][1]])
                    P.op("dve", lambda e, psA=psA, w=w, tg=tg, e_=e_: e.scalar_tensor_tensor(Wt[w][0][:], psA[:, 0:128], beta[:, tg, e_:e_ + 1], dTs[w][0][:], ALU.mult, ALU.mult),
                         reads=[psAb, betab, dTs[w][1]], writes=[Wt[w][1]])
                    P.op("dve", lambda e, psA=psA, w=w: e.tensor_tensor(inT[w][0][:], psA[:, 128:256], dT[w][0][:], ALU.mult),
                         reads=[psAb, dT[w][1]], writes=[inT[w][1]])
                    chk(P, 4)
                    pT, pTb = PB()
                    P.op("pe", lambda e, pT=pT, w=w: e.transpose(pT[:, 0:128], Wt[w][0][:], identb), reads=[Wt[w][1], cbfb], writes=[pTb])
                    P.op("act", lambda e, pT=pT, w=w: e.activation(Yp[0][w][0][:], pT[:, 0:128], AF.Copy), reads=[pTb], writes=[Yp[0][w][1]])
                    P.op("pe", lambda e, pT=pT, kT=kT: e.transpose(pT[:, 128:256], kT, identb), reads=[rb[2], cbfb], writes=[pTb])
                    P.op("pe", lambda e, pT=pT, vT=vT: e.transpose(pT[:, 256:384], vT, identb), reads=[rb[3], cbfb], writes=[pTb])
                    P.op("act", lambda e, pT=pT, w=w, tg=tg, e_=e_: e.activation(kg[w][0][:], pT[:, 128:256], AF.Copy, scale=eg[:, tg, e_:e_ + 1]),
                         reads=[pTb, egb], writes=[kg[w][1]])
                    P.op("act", lambda e, pT=pT, w=w, tg=tg, e_=e_: e.activation(kdec[w][0][:], pT[:, 128:256], AF.Copy, scale=edec[:, tg, e_:e_ + 1]),
                         reads=[pTb, edecb], writes=[kdec[w][1]])
                    P.op("dve", lambda e, pT=pT, w=w: e.tensor_copy(vtok[w][0][:], pT[:, 256:384]), reads=[pTb], writes=[vtok[w][1]])
                    chk(P, 5)
                    Wcur = Wt[w]
                    for k_ in range(5):
                        ps, psb = PF()
                        Ycur = Yp[k_][w]
                        P.op("pe", lambda e, ps=ps, Wc=Wcur, Yc=Ycur: e.matmul(ps[:, 0:128], Wc[0][:], Yc[0][:], start=True, stop=True),
                             reads=[Wcur[1], Ycur[1]], writes=[psb])
                        psq, psqb = ps, psb
                        if os.environ.get("GDN_VAR") == "2":
                            psq, psqb = PF()
                        if k_ < 4:
                            P.op("pe", lambda e, ps=psq, Wc=Wcur, Yc=Ycur: e.matmul(ps[:, 128:256], Yc[0][:], Wc[0][:], start=True, stop=True),
                                 reads=[Wcur[1], Ycur[1]], writes=[psqb])
                        Yn = Yp[k_ + 1][w]
                        P.op("act", lambda e, ps=ps, Yn=Yn: e.activation(Yn[0][:], ps[:, 0:128], AF.Copy), reads=[psb], writes=[Yn[1]])
                        if k_ < 4:
                            Wn = Wp[k_][w]
                            if os.environ.get("GDN_VAR") == "1":
                                P.op("act", lambda e, ps=ps, Wn=Wn: e.activation(Wn[0][:], ps[:, 128:256], AF.Copy), reads=[psb], writes=[Wn[1]])
                            else:
                                P.op("dve", lambda e, ps=psq, Wn=Wn: e.tensor_copy(Wn[0][:], ps[:, 128:256]), reads=[psqb], writes=[Wn[1]])
                            Wcur = Wn
                    chk(P, 5.3)
                    Pc = Pm[0][w]
                    P.op("pool", lambda e, Pc=Pc, w=w: e.tensor_tensor(Pc[0][:], identb, Wt[w][0][:], ALU.subtract), reads=[cbfb, Wt[w][1]], writes=[Pc[1]])
                    for k_ in range(1, 6):
                        ps, psb = PF()
                        Yk = Yp[k_][w]
                        P.op("pe", lambda e, ps=ps, Yk=Yk, Pc=Pc: e.matmul(ps[:, 0:128], Yk[0][:], Pc[0][:], start=True, stop=True),
                             reads=[Yk[1], Pc[1]], writes=[psb])
                        Pn = Pm[k_ % 2][w]
                        P.op("dve", lambda e, ps=ps, Pn=Pn, Pc=Pc: e.tensor_tensor(Pn[0][:], ps[:, 0:128], Pc[0][:], ALU.add), reads=[psb, Pc[1]], writes=[Pn[1]])
                        Pc = Pn
                    TT = Pc
                    chk(P, 5.6)
                    ps, psb = PF()
                    P.op("pe", lambda e, ps=ps, w=w, TT=TT: e.matmul(ps[:, 0:128], kg[w][0][:], TT[0][:], start=True, stop=True),
                         reads=[kg[w][1], TT[1]], writes=[psb])
                    P.op("act", lambda e, ps=ps, w=w: e.activation(nwT[w][0][:], ps[:, 0:128], AF.Copy, scale=-1.0), reads=[psb], writes=[nwT[w][1]])
                    chk(P, 6)
                    S_f, S_fb = Sf[e_]
                    S_b, S_bb = Sb[e_]
                    for h_ in range(2):
                        r = slice(h_ * CH, (h_ + 1) * CH)
                        ps, psb = PF()
                        P.op("pe", lambda e, ps=ps, TT=TT, w=w, r=r: e.matmul(ps[r, 0:128], TT[0][:, r], vtok[w][0][:], start=True, stop=False),
                             reads=[TT[1], vtok[w][1]], writes=[psb])
                        P.op("pe", lambda e, ps=ps, w=w, r=r, S_b=S_b: e.matmul(ps[r, 0:128], nwT[w][0][:, r], S_b[:], start=False, stop=True),
                             reads=[nwT[w][1], S_bb], writes=[psb])
                        P.op("act", lambda e, ps=ps, w=w, r=r, tg=tg, e_=e_: e.activation(vnew[w][0][r, :], ps[r, 0:128], AF.Copy, scale=beta[r, tg, e_:e_ + 1]),
                             reads=[psb, betab], writes=[vnew[w][1]])
                        P.op("pe", lambda e, ps=ps, r=r, qdT=qdT, S_b=S_b: e.matmul(ps[r, 128:256], qdT[:, r], S_b[:], start=True, stop=False),
                             reads=[rb[1], S_bb], writes=[psb])
                        P.op("pe", lambda e, ps=ps, r=r, w=w: e.matmul(ps[r, 128:256], inT[w][0][r, r], vnew[w][0][r, :], start=False, stop=True),
                             reads=[inT[w][1], vnew[w][1]], writes=[psb])
                        P.op("pe", lambda e, ps=ps, r=r, w=w: e.matmul(ps[:, 256:384], kdec[w][0][r, :], vnew[w][0][r, :], start=True, stop=True),
                             reads=[kdec[w][1], vnew[w][1]], writes=[psb])
                        P.op("act", lambda e, ps=ps, r=r, w=w: e.activation(otok[w][0][r, :], ps[r, 128:256], AF.Copy), reads=[psb], writes=[otok[w][1]])
                        gcol = glx[:, tg, e_, h_:h_ + 1]
                        P.op("dve", lambda e, ps=ps, S_b=S_b, S_f=S_f, gcol=gcol: e.scalar_tensor_tensor(S_b[:], S_f[:], gcol, ps[:, 256:384], ALU.mult, ALU.add),
                             reads=[psb, S_fb, glxb], writes=[S_bb])
                        P.op("dve", lambda e, ps=ps, S_f=S_f, gcol=gcol: e.scalar_tensor_tensor(S_f[:], S_f[:], gcol, ps[:, 256:384], ALU.mult, ALU.add),
                             reads=[psb, S_fb, glxb], writes=[S_fb])
                    chk(P, 7)
                    P.op("act", lambda e, w=w: e.activation(junk[w][0][:], otok[w][0][:], AF.Square, accum_out=ssq[w][0][:]), reads=[otok[w][1]], writes=[junk[w][1], ssq[w][1]])
                    P.op("act", lambda e, w=w: e.activation(ssq[w][0][:], ssq[w][0][:], AF.Sqrt, bias=EPS, scale=1.0 / 128), reads=[ssq[w][1]], writes=[ssq[w][1]])
                    P.op("dve", lambda e, w=w: e.reciprocal(ssq[w][0][:], ssq[w][0][:]), reads=[ssq[w][1]], writes=[ssq[w][1]])
                    P.op("dve", lambda e, w=w: e.tensor_scalar_mul(onb[w][0][:], otok[w][0][:], ssq[w][0][:, 0:1]), reads=[otok[w][1], ssq[w][1]], writes=[onb[w][1]])
                    pT2, pT2b = PB()
                    P.op("pe", lambda e, pT2=pT2, w=w: e.transpose(pT2[:, 0:128], onb[w][0][:], identb), reads=[onb[w][1], cbfb], writes=[pT2b])
                    P.op("dve", lambda e, pT2=pT2, e_=e_, cs_=cs_: e.scalar_tensor_tensor(sego[e_][0][:, cs_], pT2[:, 0:128], onm[:, 0:1], segz[e_][0][:, cs_], ALU.mult, ALU.mult),
                         reads=[pT2b, onmb, segz[e_][1]], writes=[sego[e_][1]])
            for e_ in range(2):
                P.dma("sp", lambda e, e_=e_, t_lo=t_lo: e.dma_start(out=out_d[e_, :, t_lo:t_lo + SEG], in_=sego[e_][0][:]), reads=[sego[e_][1]], sbuf=sego[e_][1])
        P.barrier()
        P.emit()


DILS = (1, 4, 16)


def t5_bucket_np(dist):
    max_exact = 16
    d_f = np.maximum(dist, 1).astype(np.float32)
    large = max_exact + (np.log(d_f / np.float32(max_exact)) / np.float32(np.log(2048 / max_exact))
                         * np.float32(32 - max_exact)).astype(np.int32)
    large = np.minimum(large, 31)
    return np.where(dist < max_exact, dist, large)


def dsa_bias_tiles(rel_bias, heads):
    k = np.arange(128)[:, None]
    j = np.arange(256)[None, :]
    rel = np.where(j < 128, j + 128 - k, j - 128 - k)
    valid = np.where(j < 128, j <= k, (j - 128) >= k)
    out = np.zeros((3, len(heads), 128, 256), np.float32)
    for g, d in enumerate(DILS):
        bucket = t5_bucket_np(np.clip(rel, 0, 128) * d)
        for e_, h in enumerate(heads):
            vals = rel_bias[:, g * 16 + h][bucket]
            out[g, e_] = np.where(valid, vals, np.float32(NEG))
    return out


def stage_dsa(nc, q_d, k_d, v_d, bias_d, ident_d, out_d, SL):
    NSEG = SL // SEG
    scale = float(128 ** -0.5)
    with contextlib.ExitStack() as st:
        P = Prog(nc, st)
        idb, idbb = P.sb("idb", [128, 128], BF16)
        P.dma("sp", lambda e: e.dma_start(out=idb[:], in_=ident_d[:, :]), writes=[idbb], sbuf=idbb)
        ones, onesb = P.sb("ones", [128, 128], BF16)
        P.op("pool", lambda e: e.memset(ones[:], 1.0), writes=[onesb])
        bias, biasb = P.sb("bias", [128, 6, 256], F32)
        P.dma("sp", lambda e: e.dma_start(out=bias[:], in_=bias_d.rearrange("g e k j -> k (g e) j")), writes=[biasb], sbuf=biasb)
        qs = [P.sb(f"qs{i}", [128, SEG], BF16) for i in range(2)]
        kw = [P.sb(f"kw{i}", [128, 2 * SEG], BF16) for i in range(2)]
        vw = [P.sb(f"vw{i}", [128, 2 * SEG], BF16) for i in range(2)]
        vt = [P.sb(f"vt{i}", [128, 32, 128], BF16) for i in range(2)]
        acc, accb = P.sb("acc", [128, 2, SEG], F32)
        rec, recb = P.sb("rec", [128, SEG], F32)
        ot = [P.sb(f"ot{i}", [128, SEG], BF16) for i in range(2)]
        tmp = [P.sb(f"tmp{i}", [128, 256], F32) for i in range(2)]
        pt = [P.sb(f"pt{i}", [128, 256], BF16) for i in range(2)]
        pf = [P.ps(f"pf{i}", [128, 512]) for i in range(6)]
        pb = [P.ps(f"pb{i}", [128, 1024], BF16) for i in range(2)]
        it = 0
        nblk = 0
        npf = 0
        npb = 0
        for sg in range(NSEG):
            t_lo = sg * SEG
            for e_ in range(2):
                for g, d in enumerate(DILS):
                    q, qb = qs[it % 2]
                    kk, kb = kw[it % 2]
                    vv, vb = vw[it % 2]
                    vtt, vtb = vt[it % 2]
                    it += 1
                    P.dma("sp", lambda e, q=q, g=g, e_=e_, t_lo=t_lo: e.dma_start(out=q[:], in_=q_d[g, e_, :, t_lo:t_lo + SEG]), writes=[qb], sbuf=qb)
                    w0 = 0 if sg > 0 else SEG
                    P.dma("sp", lambda e, kk=kk, g=g, e_=e_, t_lo=t_lo, w0=w0: e.dma_start(out=kk[:, w0:2 * SEG], in_=k_d[g, e_, :, t_lo - SEG + w0:t_lo + SEG]), writes=[kb], sbuf=kb)
                    P.dma("sp", lambda e, vv=vv, g=g, e_=e_, t_lo=t_lo, w0=w0: e.dma_start(out=vv[:, w0:2 * SEG], in_=v_d[g, e_, :, t_lo - SEG + w0:t_lo + SEG]), writes=[vb], sbuf=vb)
                    nper = SEG // (128 * d)
                    def kcols(r, n_):
                        start = SEG + 128 * n_ * d + r
                        return slice(start, start + 127 * d + 1, d)
                    slots = []
                    for r in range(d):
                        for n_ in range(-1, nper):
                            if n_ == -1 and sg == 0:
                                continue
                            slots.append((r, n_))
                    for s0 in range(0, len(slots), 8):
                        grp = slots[s0:s0 + 8]
                        pT, pTb = pb[npb % 2]
                        npb += 1
                        for i_, (r, n_) in enumerate(grp):
                            P.op("pe", lambda e, pT=pT, i_=i_, vv=vv, c=kcols(r, n_): e.transpose(pT[:, i_ * 128:(i_ + 1) * 128], vv[:, c], idb[:]),
                                 reads=[vb, idbb], writes=[pTb])
                        n_g = len(grp)
                        P.op("act", lambda e, pT=pT, vtt=vtt, s0=s0, n_g=n_g: e.activation(
                            vtt[:, s0:s0 + n_g, :], pT[:, 0:n_g * 128].rearrange("p (s c) -> p s c", c=128), AF.Copy),
                            reads=[pTb], writes=[vtb])
                    sidx = {s: i for i, s in enumerate(slots)}
                    for r in range(d):
                        for n_ in range(nper):
                            has_prev = (r, n_ - 1) in sidx
                            qc = slice(128 * n_ * d + r, 128 * n_ * d + r + 127 * d + 1, d)
                            ps, psb = pf[npf % 6]
                            npf += 1
                            lo = 0 if has_prev else 128
                            if has_prev:
                                P.op("pe", lambda e, ps=ps, kk=kk, q=q, c=kcols(r, n_ - 1), qc=qc: e.matmul(ps[:, 0:128], kk[:, c], q[:, qc], start=True, stop=True),
                                     reads=[kb, qb], writes=[psb])
                            P.op("pe", lambda e, ps=ps, kk=kk, q=q, c=kcols(r, n_), qc=qc: e.matmul(ps[:, 128:256], kk[:, c], q[:, qc], start=True, stop=True),
                                 reads=[kb, qb], writes=[psb])
                            tm, tmb = tmp[nblk % 2]
                            pp, ppb = pt[nblk % 2]
                            nblk += 1
                            P.op("dve", lambda e, ps=ps, tm=tm, lo=lo, g=g, e_=e_: e.scalar_tensor_tensor(
                                tm[:, lo:256], ps[:, lo:256], scale, bias[:, g * 2 + e_, lo:256], ALU.mult, ALU.add),
                                reads=[psb, biasb], writes=[tmb])
                            P.op("act", lambda e, tm=tm, pp=pp, lo=lo: e.activation(pp[:, lo:256], tm[:, lo:256], AF.Exp), reads=[tmb], writes=[ppb])
                            ps2, ps2b = pf[npf % 6]
                            npf += 1
                            halves = ([(n_ - 1, 0)] if has_prev else []) + [(n_, 128)]
                            for hi, (nn, c0) in enumerate(halves):
                                si = sidx[(r, nn)]
                                P.op("pe", lambda e, ps2=ps2, vtt=vtt, si=si, pp=pp, c0=c0, hi=hi, nh=len(halves): e.matmul(
                                    ps2[:, 0:128], vtt[:, si, :], pp[:, c0:c0 + 128], start=(hi == 0), stop=(hi == nh - 1)),
                                    reads=[vtb, ppb], writes=[ps2b])
                            for hi, (nn, c0) in enumerate(halves):
                                P.op("pe", lambda e, ps2=ps2, pp=pp, c0=c0, hi=hi, nh=len(halves): e.matmul(
                                    ps2[:, 128:256], ones[:], pp[:, c0:c0 + 128], start=(hi == 0), stop=(hi == nh - 1)),
                                    reads=[onesb, ppb], writes=[ps2b])
                            src = lambda ps2=ps2: ps2[:, 0:256].rearrange("p (a c) -> p a c", a=2)
                            if g == 0:
                                P.op("act", lambda e, src=src, qc=qc: e.activation(acc[:, :, qc], src(), AF.Copy), reads=[ps2b], writes=[accb])
                            else:
                                P.op("dve", lambda e, src=src, qc=qc: e.tensor_tensor(acc[:, :, qc], acc[:, :, qc], src(), ALU.add), reads=[ps2b, accb], writes=[accb])
                o, ob = ot[(sg * 2 + e_) % 2]
                P.op("dve", lambda e: e.reciprocal(rec[:], acc[:, 1, :]), reads=[accb], writes=[recb])
                P.op("dve", lambda e, o=o: e.tensor_tensor(o[:], acc[:, 0, :], rec[:], ALU.mult), reads=[accb, recb], writes=[ob])
                P.dma("sp", lambda e, o=o, e_=e_, t_lo=t_lo: e.dma_start(out=out_d[e_, :, t_lo:t_lo + SEG], in_=o[:]), reads=[ob], sbuf=ob)
        P.barrier()
        P.emit()


T_ = TPC
NG = 4 * 4 * 16 + 16


def ba_extra(W_d, ba_d):
    def mk(P):
        wba, wbab = P.sb("wba", [128, 16, 32], BF16)
        bat, batb = P.sb("bat", [128, T_ // 128, 32], F32)
        psb_t, psb_b = P.ps("psba", [128, 512])

        def fn(act, actb, half):
            P.dma("pool", lambda e: e.dma_start(out=wba[:], in_=W_d[:, 8192:8224].rearrange("(k p) n -> p k n", p=128)),
                  writes=[wbab], sbuf=wbab)
            for tt in range(T_ // 128):
                for k in range(16):
                    P.op("pe", lambda e, tt=tt, k=k: e.matmul(psb_t[:, 0:32], act[:, k, tt * 128:(tt + 1) * 128], wba[:, k, :],
                                                              start=(k == 0), stop=(k == 15)), reads=[actb, wbab], writes=[psb_b])
                P.op("dve", lambda e, tt=tt: e.tensor_copy(bat[:, tt, :], psb_t[:, 0:32]), reads=[psb_b], writes=[batb])
            P.dma("sp", lambda e: e.dma_start(out=ba_d.rearrange("(n p) x -> p n x", p=128), in_=bat[:]), reads=[batb], sbuf=batb)
        return fn
    return mk


def gu_evac(hid_d):
    def factory(P):
        sgs = [P.sb(f"sg{i}", [128, TB], F32) for i in range(2)]
        hts = [P.sb(f"hid{i}", [128, T_], BF16) for i in range(3)]
        state = {"n": 0}

        def evac(u, tb, pss):
            (psg, psgb), (psu, psub) = pss
            sg, sgb = sgs[state["n"] % 2]
            state["n"] += 1
            ht, hb = hts[u % 3]
            sl = slice(tb * TB, (tb + 1) * TB)
            P.op("act", lambda e: e.activation(sg[:], psg[:], AF.Silu), reads=[psgb], writes=[sgb])
            P.op("dve", lambda e: e.tensor_tensor(ht[:, sl], sg[:], psu[:], ALU.mult), reads=[sgb, psub], writes=[hb])
            if tb == T_ // TB - 1:
                P.dma("sp", lambda e: e.dma_start(out=hid_d[u, :, :], in_=ht[:]), reads=[hb], sbuf=hb)
        return evac
    return factory


def plain_groups(ncols, gw=256):
    return [([(c0, gw)], [[o] for o in range(0, gw, 128)]) for c0 in range(0, ncols, gw)]


def build_ts(layer, do_block, nxt):
    nc = bass.Bass("TRN2", target_bir_lowering=False)
    dt = lambda name, shape, dty, kind: nc.dram_tensor(name, list(shape), dty, kind=kind).ap()
    x_in = dt("x_in", [16, 128, T_], F32, "ExternalInput")
    gains = dt("gains", [128, NG], F32, "ExternalInput")
    h_d = dt("h_scr", [16, 128, T_], BF16, "Internal")
    gcol = lambda l, w: (l * 4 + w) * 16
    x_cur = x_in
    if do_block:
        o_in = dt("o_in", [16, 128, T_], BF16, "ExternalInput")
        w_out = dt("w_out", [D, D], F32, "ExternalInput")
        w_gu = dt("w_gu", [D, 2 * DFF], F32, "ExternalInput")
        w_dn = dt("w_dn", [DFF, D], F32, "ExternalInput")
        f_d = dt("f_scr", [16, 128, T_], F32, "Internal")
        x_mid = dt("x_mid", [16, 128, T_], F32, "Internal")
        x_out = dt("x_out", [16, 128, T_], F32, "ExternalOutput")
        hid_d = dt("hid_scr", [DFF // 128, 128, T_], BF16, "Internal")
        piece_gemm(nc, o_in, 16, T_, w_out, plain_groups(D), store_evac(lambda u, half: f_d[u, :, :], T_, F32))
        piece_norm(nc, x_in, h_d, gains, gcol(layer, 2), T_, f_d=f_d, gcol_post=gcol(layer, 1), x_out_d=x_mid)
        gu_groups = [([(j * 128, 128), (DFF + j * 128, 128)], [[0, 128]]) for j in range(DFF // 128)]
        piece_gemm(nc, h_d, 16, T_, w_gu, gu_groups, gu_evac(hid_d))
        piece_gemm(nc, hid_d, DFF // 128, T_, w_dn, plain_groups(D, 128),
                   store_evac(lambda u, half: f_d[u, :, half * 1024:(half + 1) * 1024], 1024, F32), TR=1024)
        nxt_gcol = gcol(layer + 1, 0) if nxt is not None else None
        piece_norm(nc, x_mid, h_d if nxt is not None else None, gains, nxt_gcol, T_, f_d=f_d, gcol_post=gcol(layer, 3), x_out_d=x_out)
        x_cur = x_out
    else:
        piece_norm(nc, x_in, h_d, gains, gcol(layer, 0), T_)
    if nxt == "gdn":
        w_in = dt("w_in", [D, 8224], F32, "ExternalInput")
        proj = dt("proj", [64, 128, T_], BF16, "ExternalOutput")
        ba = dt("ba", [T_, 32], F32, "ExternalOutput")
        piece_gemm(nc, h_d, 16, T_, w_in, plain_groups(8192), store_evac(lambda u, half: proj[u, :, :], T_, BF16),
                   extra=ba_extra(w_in, ba))
    elif nxt in ("dsa", "dsa_kv"):
        if nxt == "dsa_kv":
            kv_w = dt("kv_w", [D, 12288], F32, "ExternalInput")
            kv = dt("kv", [96, 128, T_], BF16, "ExternalOutput")
            hk_d = dt("hk_scr", [16, 128, T_], BF16, "Internal")
            piece_norm(nc, x_cur, hk_d, gains, 256, T_)
            piece_gemm(nc, hk_d, 16, T_, kv_w, plain_groups(12288), store_evac(lambda u, half: kv[u, :, :], T_, BF16))
        w_q = dt("w_q", [D, 6144], F32, "ExternalInput")
        qo = dt("q_out", [48, 128, T_], BF16, "ExternalOutput")
        piece_gemm(nc, h_d, 16, T_, w_q, plain_groups(6144), store_evac(lambda u, half: qo[u, :, :], T_, BF16))
    return nc


def build_gdn():
    nc = bass.Bass("TRN2", target_bir_lowering=False)
    dt = lambda name, shape, dty, kind: nc.dram_tensor(name, list(shape), dty, kind=kind).ap()
    pj = dt("pj", [4, 2, 128, S], BF16, "ExternalInput")
    ba = dt("ba", [S, 4], F32, "ExternalInput")
    cw = dt("cw", [128, 6, 4], F32, "ExternalInput")
    hp = dt("hp", [128, 4], F32, "ExternalInput")
    on = dt("on", [128, 1], F32, "ExternalInput")
    gc = dt("gc", [128, GC_W], F32, "ExternalInput")
    gs = dt("gs", [2, 128], F32, "ExternalInput")
    out = dt("out", [2, 128, S], BF16, "ExternalOutput")
    stage_gdn_seq(nc, pj, ba, cw, hp, on, gc, gs, out, S)
    return nc


def build_dsa():
    nc = bass.Bass("TRN2", target_bir_lowering=False)
    dt = lambda name, shape, dty, kind: nc.dram_tensor(name, list(shape), dty, kind=kind).ap()
    qd = dt("q", [3, 2, 128, S], BF16, "ExternalInput")
    kd = dt("k", [3, 2, 128, S], BF16, "ExternalInput")
    vd = dt("v", [3, 2, 128, S], BF16, "ExternalInput")
    bd = dt("bias", [3, 2, 128, 256], F32, "ExternalInput")
    idd = dt("ident", [128, 128], BF16, "ExternalInput")
    out = dt("out", [2, 128, S], BF16, "ExternalOutput")
    stage_dsa(nc, qd, kd, vd, bd, idd, out, S)
    return nc


def _run(nc, in_maps):
    res = run_bass_kernel_spmd(nc, in_maps, core_ids=list(range(NCORES)))
    return res.results


def _to_heads(chunks_per_core, sel):
    outs = []
    for j in range(NCORES):
        ids = sel(j)
        outs.append(np.concatenate([chunks_per_core[c][ids] for c in range(NCORES)], axis=2))
    return outs


def _to_tokens(outs_per_core):
    full = np.concatenate(outs_per_core, axis=0)
    return [np.ascontiguousarray(full[:, :, c * T_:(c + 1) * T_]) for c in range(NCORES)]


def kernel(x, norm_gains, ffn_w_gate_up, ffn_w_down, gdn_w_in, gdn_conv_w, gdn_a_log, gdn_dt_bias,
           gdn_out_norm, gdn_w_out, kv_norm, kv_w, dsa_w_q, dsa_w_out, rel_bias):
    f32 = lambda a: np.ascontiguousarray(np.asarray(a, dtype=np.float32))
    x = f32(x)
    xs = [np.ascontiguousarray(x[0, c * T_:(c + 1) * T_, :].T.reshape(16, 128, T_)) for c in range(NCORES)]
    gall = np.concatenate([f32(norm_gains).reshape(16, D), f32(kv_norm)[None]], 0)
    gains = np.ascontiguousarray(gall.reshape(17, 16, 128).transpose(2, 0, 1).reshape(128, NG))
    gcs = gdn_consts()
    identb = np.eye(128, dtype=np.float32).astype(ml_dtypes.bfloat16)
    w_gu = f32(ffn_w_gate_up)
    w_dn = f32(ffn_w_down)
    w_in = f32(gdn_w_in)
    conv_w = f32(gdn_conv_w)
    a_log = f32(gdn_a_log)
    dtb = f32(gdn_dt_bias)
    onorm = f32(gdn_out_norm)
    g_w_out = f32(gdn_w_out)
    kvw = f32(kv_w)
    wq = f32(dsa_w_q)
    d_w_out = f32(dsa_w_out)
    rb = f32(rel_bias)

    def gdn_inputs(l, projs, bas):
        ba_full = np.concatenate(bas, axis=0)
        pjs = _to_heads(projs, lambda j: [kind * 16 + 2 * j + e for kind in range(4) for e in range(2)])
        maps = []
        for j in range(NCORES):
            hs = [2 * j, 2 * j + 1]
            cwn = np.zeros((128, 6, 4), np.float32)
            for kind in range(3):
                for e_, hd in enumerate(hs):
                    cwn[:, kind * 2 + e_, :] = conv_w[l][:, kind * 2048 + hd * 128: kind * 2048 + (hd + 1) * 128].T
            hpn = np.tile(np.array([a_log[l][hs[0]], a_log[l][hs[1]], dtb[l][hs[0]], dtb[l][hs[1]]], np.float32)[None], (128, 1))
            maps.append({"pj": np.ascontiguousarray(pjs[j].reshape(4, 2, 128, S)),
                         "ba": np.ascontiguousarray(ba_full[:, [hs[0], hs[1], 16 + hs[0], 16 + hs[1]]]),
                         "cw": cwn, "hp": hpn, "on": np.ascontiguousarray(onorm[l][:, None]),
                         "gc": gcs["gc"], "gs": gcs["gsel"]})
        return maps

    def dsa_inputs(qs_, kvs):
        qh = _to_heads(qs_, lambda j: [g * 16 + 2 * j + e for g in range(3) for e in range(2)])
        kh = _to_heads(kvs, lambda j: [g * 16 + 2 * j + e for g in range(3) for e in range(2)])
        vh = _to_heads(kvs, lambda j: [48 + g * 16 + 2 * j + e for g in range(3) for e in range(2)])
        return [{"q": np.ascontiguousarray(qh[j].reshape(3, 2, 128, S)), "k": np.ascontiguousarray(kh[j].reshape(3, 2, 128, S)),
                 "v": np.ascontiguousarray(vh[j].reshape(3, 2, 128, S)),
                 "bias": dsa_bias_tiles(rb, [2 * j, 2 * j + 1]), "ident": identb} for j in range(NCORES)]

    r = _run(build_ts(0, False, "gdn"), [{"x_in": xs[c], "gains": gains, "w_in": w_in[0]} for c in range(NCORES)])
    projs, bas = [r[c]["proj"] for c in range(NCORES)], [r[c]["ba"] for c in range(NCORES)]
    nc_gdn = build_gdn()
    nc_dsa = None
    kvs = None
    for l in range(4):
        if l < 2:
            ro = _run(nc_gdn if l == 0 else build_gdn(), gdn_inputs(l, projs, bas))
        else:
            ro = _run(build_dsa(), dsa_inputs(qs_, kvs))
        o_tok = _to_tokens([ro[j]["out"] for j in range(NCORES)])
        nxt = ["gdn", "dsa_kv", "dsa", None][l]
        w_o = g_w_out[l] if l < 2 else d_w_out[l - 2]
        maps = []
        for c in range(NCORES):
            m = {"x_in": xs[c], "gains": gains, "o_in": o_tok[c], "w_out": w_o, "w_gu": w_gu[l], "w_dn": w_dn[l]}
            if nxt == "gdn":
                m["w_in"] = w_in[l + 1]
            if nxt == "dsa_kv":
                m["kv_w"] = kvw
            if nxt in ("dsa", "dsa_kv"):
                m["w_q"] = wq[l - 1]
            maps.append(m)
        r = _run(build_ts(l, True, nxt), maps)
        xs = [r[c]["x_out"] for c in range(NCORES)]
        if nxt == "gdn":
            projs, bas = [r[c]["proj"] for c in range(NCORES)], [r[c]["ba"] for c in range(NCORES)]
        if nxt == "dsa_kv":
            kvs = [r[c]["kv"] for c in range(NCORES)]
        if nxt in ("dsa", "dsa_kv"):
            qs_ = [r[c]["q_out"] for c in range(NCORES)]
    out = np.empty((1, S, D), np.float32)
    for c in range(NCORES):
        out[0, c * T_:(c + 1) * T_, :] = xs[c].reshape(D, T_).T
    return out
```

```python
import contextlib
import os
import numpy as np
import ml_dtypes
import concourse.bass as bass
import concourse.mybir as mybir
from concourse.bass_utils import run_bass_kernel_spmd

F32 = mybir.dt.float32
BF16 = mybir.dt.bfloat16
ALU = mybir.AluOpType
AF = mybir.ActivationFunctionType
AX = mybir.AxisListType

NCORES = 8
D = 2048
S = 16384
TPC = S // NCORES
DFF = 5632
EPS = 1e-6
NEG = -30000.0


class Buf:
    __slots__ = ("name", "last_writer", "readers", "sem", "dma_cnt", "last_dma", "excl")

    def __init__(self, name):
        self.name = name
        self.excl = False
        self.last_writer = None
        self.readers = []
        self.sem = None
        self.dma_cnt = 0
        self.last_dma = None


class Op:
    __slots__ = ("eng", "fn", "deps", "signal", "token", "is_dma", "dbuf", "inc")

    def __init__(self, eng, fn, is_dma=False, dbuf=None):
        self.eng = eng
        self.fn = fn
        self.deps = []
        self.signal = False
        self.token = None
        self.is_dma = is_dma
        self.dbuf = dbuf


class Prog:
    ENGS = ("pe", "act", "dve", "pool", "sp")

    NPROG = [0]

    def __init__(self, nc, stack):
        self.nc = nc
        self.stack = stack
        Prog.NPROG[0] += 1
        self.tag = f"_g{Prog.NPROG[0]}"
        self.ops = []
        self.bufs = []
        self.nbuf = 0

    def buf(self, name=None):
        b = Buf(name or f"b{self.nbuf}")
        self.nbuf += 1
        self.bufs.append(b)
        return b

    def sb(self, name, shape, dt):
        t = self.stack.enter_context(self.nc.sbuf_tensor(name + self.tag, list(shape), dt))
        return t, self.buf(name)

    def ps(self, name, shape, dt=F32):
        t = self.stack.enter_context(self.nc.psum_tensor(name + self.tag, list(shape), dt))
        b = self.buf(name)
        b.excl = True
        return t, b

    def _deps(self, op, reads, writes):
        deps = []
        for r in reads:
            if r.last_writer is not None:
                deps.append(r.last_writer)
        for w in writes:
            if w.last_writer is not None:
                deps.append(w.last_writer)
            deps.extend(w.readers)
        seen = set()
        for d in deps:
            if d is op or id(d) in seen:
                continue
            seen.add(id(d))
            if d.eng == "pe" and op.eng == "pe" and not d.is_dma and not op.is_dma:
                continue
            op.deps.append(d)
            d.signal = True
        for w in writes:
            w.last_writer = op
            w.readers = []
        for r in reads:
            if r not in writes:
                if op.is_dma:
                    r.readers.append(op)
                else:
                    r.readers = [x for x in r.readers if x.is_dma or x.eng != op.eng]
                    r.readers.append(op)

    def op(self, eng, fn, reads=(), writes=()):
        o = Op(eng, fn)
        reads = list(reads)
        writes = list(writes)
        for r in list(reads):
            if r.excl:
                reads.remove(r)
                if r not in writes:
                    writes.append(r)
        self._deps(o, reads, writes)
        self.ops.append(o)
        return o

    def dma(self, queue, fn, reads=(), writes=(), sbuf=None, inc=16):
        o = Op(queue, fn, is_dma=True, dbuf=sbuf)
        o.inc = inc
        self._deps(o, list(reads), list(writes))
        if sbuf.last_dma is not None and sbuf.last_dma not in o.deps:
            o.deps.append(sbuf.last_dma)
        sbuf.last_dma = o
        sbuf.dma_cnt += inc
        o.token = (sbuf, sbuf.dma_cnt)
        o.signal = True
        self.ops.append(o)
        return o

    def dma_parts(self, queue, fns, reads=(), writes=(), sbuf=None):
        first = self.dma(queue, fns[0], reads=reads, writes=writes, sbuf=sbuf)
        last = first
        for fn in fns[1:]:
            o = Op(queue, fn, is_dma=True, dbuf=sbuf)
            o.inc = 16
            o.deps = list(first.deps)
            sbuf.dma_cnt += 16
            o.token = (sbuf, sbuf.dma_cnt)
            o.signal = True
            self.ops.append(o)
            last = o
        for w in writes:
            w.last_writer = last
        sbuf.last_dma = last
        return last

    def barrier(self):
        last = {}
        for o in self.ops:
            if o.is_dma:
                last[("dma", id(o.dbuf))] = o
            else:
                last[o.eng] = o
        for e in self.ENGS:
            o = Op(e, None)
            for k, d in last.items():
                if not d.is_dma and d.eng == e and e == "pe":
                    continue
                o.deps.append(d)
                d.signal = True
            self.ops.append(o)
        for b in self.bufs:
            b.last_writer = None
            b.readers = []

    def emit(self):
        nc = self.nc
        st = self.stack
        esem = {e: st.enter_context(nc.semaphore(f"sem_{e}{self.tag}")) for e in self.ENGS}
        cnt = {e: 0 for e in self.ENGS}
        for o in self.ops:
            if o.is_dma:
                if o.dbuf.sem is None:
                    o.dbuf.sem = st.enter_context(nc.semaphore(f"sd_{o.dbuf.name}{self.tag}"))
                o.token = (o.dbuf.sem, o.token[1])
            elif o.signal and o.fn is not None:
                cnt[o.eng] += 1
                o.token = (esem[o.eng], cnt[o.eng])
        block = st.enter_context(nc.Block())
        per = {e: [o for o in self.ops if o.eng == e] for e in self.ENGS}

        def run(e, eng):
            known = {}
            for o in per[e]:
                need = {}
                for d in o.deps:
                    if d.token is None:
                        continue
                    s, c = d.token
                    if known.get(id(s), 0) >= c:
                        continue
                    if need.get(id(s), (None, 0))[1] < c:
                        need[id(s)] = (s, c)
                for s, c in need.values():
                    eng.wait_ge(s, c)
                    known[id(s)] = c
                if o.fn is None:
                    continue
                ins = o.fn(eng)
                if o.is_dma:
                    ins.then_inc(o.token[0], o.inc)
                elif o.signal:
                    ins.then_inc(o.token[0], 1)

        @block.tensor
        def _(eng):
            run("pe", eng)

        @block.scalar
        def _(eng):
            run("act", eng)

        @block.vector
        def _(eng):
            run("dve", eng)

        @block.gpsimd
        def _(eng):
            run("pool", eng)

        @block.sync
        def _(eng):
            run("sp", eng)


TB = 512


def _rstd_from_sq(P, sqt, sqb, KC, ones, onesb, pss, pssb, rs, rsb, inv_n):
    for c in range(KC):
        P.op("pe", lambda e, c=c: e.matmul(pss[:], ones[:], sqt[:, c, :], start=(c == 0), stop=(c == KC - 1)),
             reads=[onesb, sqb], writes=[pssb])
    P.op("act", lambda e: e.activation(rs[:], pss[:], AF.Sqrt, bias=EPS, scale=inv_n), reads=[pssb], writes=[rsb])
    P.op("dve", lambda e: e.reciprocal(rs[:], rs[:]), reads=[rsb], writes=[rsb])


def piece_norm(nc, x_d, h_d, gains_d, gcol, T, f_d=None, gcol_post=None, x_out_d=None):
    KC = D // 128
    with contextlib.ExitStack() as st:
        P = Prog(nc, st)
        ones, onesb = P.sb("ones", [128, 128], BF16)
        gt, gtb = P.sb("gains", [128, gains_d.shape[1]], F32)
        P.op("pool", lambda e: e.memset(ones[:], 1.0), writes=[onesb])
        P.dma("sp", lambda e: e.dma_start(out=gt[:], in_=gains_d[:, :]), writes=[gtb], sbuf=gtb)
        xts = [P.sb(f"xt{i}", [128, KC, TB], F32) for i in range(2)]
        fts = [P.sb(f"ft{i}", [128, KC, TB], F32) for i in range(2)] if f_d is not None else None
        sqs = [P.sb(f"sq{i}", [128, KC, TB], BF16) for i in range(1)]
        hts = [P.sb(f"ht{i}", [128, KC, TB], BF16) for i in range(2)] if h_d is not None else None
        rss = [P.sb(f"rs{i}", [128, TB], F32) for i in range(2)]
        psl = [P.ps(f"pss{i}", [128, TB]) for i in range(2)]
        nb = T // TB
        for tb in range(nb):
            sl = slice(tb * TB, (tb + 1) * TB)
            xt, xb = xts[tb % 2]
            sq, sqb = sqs[0]
            P.dma("sp", lambda e, xt=xt, sl=sl: e.dma_start(out=xt[:], in_=x_d[:, :, sl].rearrange("c p t -> p c t")),
                  writes=[xb], sbuf=xb)
            if f_d is not None:
                ft, fb = fts[tb % 2]
                P.dma("sp", lambda e, ft=ft, sl=sl: e.dma_start(out=ft[:], in_=f_d[:, :, sl].rearrange("c p t -> p c t")),
                      writes=[fb], sbuf=fb)
                P.op("act", lambda e, ft=ft, sq=sq: e.activation(sq[:], ft[:], AF.Square), reads=[fb], writes=[sqb])
                rs, rsb = rss[0]
                pss, pssb = psl[0]
                _rstd_from_sq(P, sq, sqb, KC, ones, onesb, pss, pssb, rs, rsb, 1.0 / D)
                for c in range(KC):
                    eng = "dve"
                    P.op(eng, lambda e, c=c, ft=ft, rs=rs: e.scalar_tensor_tensor(
                        ft[:, c, :], ft[:, c, :], gt[:, gcol_post + c:gcol_post + c + 1], rs[:], ALU.mult, ALU.mult),
                        reads=[fb, rsb, gtb], writes=[fb])
                P.op("pool", lambda e, xt=xt, ft=ft: e.tensor_tensor(xt[:], xt[:], ft[:], ALU.add),
                     reads=[xb, fb], writes=[xb])
                P.dma("sp", lambda e, xt=xt, sl=sl: e.dma_start(out=x_out_d[:, :, sl].rearrange("c p t -> p c t"), in_=xt[:]),
                      reads=[xb], sbuf=xb)
            if h_d is not None:
                ht, hb = hts[tb % 2]
                P.op("act", lambda e, xt=xt, sq=sq: e.activation(sq[:], xt[:], AF.Square), reads=[xb], writes=[sqb])
                rs, rsb = rss[1]
                pss, pssb = psl[1]
                _rstd_from_sq(P, sq, sqb, KC, ones, onesb, pss, pssb, rs, rsb, 1.0 / D)
                for c in range(KC):
                    eng = "dve"
                    P.op(eng, lambda e, c=c, xt=xt, ht=ht, rs=rs: e.scalar_tensor_tensor(
                        ht[:, c, :], xt[:, c, :], gt[:, gcol + c:gcol + c + 1], rs[:], ALU.mult, ALU.mult),
                        reads=[xb, rsb, gtb], writes=[hb])
                P.dma("sp", lambda e, ht=ht, sl=sl: e.dma_start(out=h_d[:, :, sl].rearrange("c p t -> p c t"), in_=ht[:]),
                      reads=[hb], sbuf=hb)
        P.barrier()
        P.emit()


def piece_gemm(nc, act_d, KC, T, W_d, groups, evac_factory, TR=None, extra=None):
    TR = TR or T
    maxw = max(sum(n for _, n in g[0]) for g in groups)
    with contextlib.ExitStack() as st:
        P = Prog(nc, st)
        act, actb = P.sb("act", [128, KC, TR], BF16)
        wsl = [P.sb(f"w{i}", [128, KC, maxw], BF16) for i in range(3)]
        psl = [P.ps(f"pg{i}", [128, TB]) for i in range(6)]
        evac = evac_factory(P)
        if extra is not None:
            extra_fn = extra(P)
        pi = 0
        gi = 0
        for half in range(T // TR):
            P.dma("sp", lambda e, half=half: e.dma_start(
                out=act[:], in_=act_d[:, :, half * TR:(half + 1) * TR].rearrange("c p t -> p c t")),
                writes=[actb], sbuf=actb)
            if extra is not None:
                extra_fn(act, actb, half)
            ug = 0
            for cols, units in groups:
                w, wb = wsl[gi % 3]
                gi += 1
                off = 0
                fns = []
                for (c0, n) in cols:
                    for k0 in range(0, KC, 4):
                        k1 = min(KC, k0 + 4)
                        fns.append(lambda e, w=w, off=off, c0=c0, n=n, k0=k0, k1=k1: e.dma_start(
                            out=w[:, k0:k1, off:off + n],
                            in_=W_d[k0 * 128:k1 * 128, c0:c0 + n].rearrange("(k p) n -> p k n", p=128)))
                    off += n
                P.dma_parts("pool", fns, writes=[wb], sbuf=wb)
                for unit in units:
                    for tb in range(TR // TB):
                        pss = []
                        for uo in unit:
                            ps, psb = psl[pi % 6]
                            pi += 1
                            for k in range(KC):
                                P.op("pe", lambda e, ps=ps, w=w, uo=uo, k=k, tb=tb: e.matmul(
                                    ps[:], w[:, k, uo:uo + 128], act[:, k, tb * TB:(tb + 1) * TB],
                                    start=(k == 0), stop=(k == KC - 1)), reads=[wb, actb], writes=[psb])
                            pss.append((ps, psb))
                        evac(ug, half * (TR // TB) + tb, pss)
                    ug += 1
        P.barrier()
        P.emit()


def store_evac(out_ap_fn, TR, dt, nslots=3):
    def factory(P):
        tiles = [P.sb(f"ot{i}", [128, TR], dt) for i in range(nslots)]
        state = {"n": 0, "t": -1, "key": None}
        nb = TR // TB

        def evac(u, tb, pss):
            key = (u, tb // nb)
            if key != state["key"]:
                state["key"] = key
                state["t"] += 1
            ot, ob = tiles[state["t"] % nslots]
            ps, psb = pss[0]
            i = state["n"]
            state["n"] += 1
            sl = slice((tb % nb) * TB, (tb % nb + 1) * TB)
            if i % 2 == 0:
                P.op("act", lambda e: e.activation(ot[:, sl], ps[:], AF.Copy), reads=[psb], writes=[ob])
            else:
                P.op("dve", lambda e: e.tensor_copy(ot[:, sl], ps[:]), reads=[psb], writes=[ob])
            if tb % nb == nb - 1:
                dst = out_ap_fn(u, tb // nb)
                P.dma("sp", lambda e: e.dma_start(out=dst, in_=ot[:]), reads=[ob], sbuf=ob)
        return evac
    return factory


SEG = 2048
CH = 64


def gdn_consts():
    idx = np.arange(128)
    same = (idx[:, None] // CH) == (idx[None, :] // CH)
    U = (same & (idx[:, None] <= idx[None, :])).astype(np.float32)
    BM = same.astype(np.float32)
    negm = np.where(same & (idx[None, :] >= idx[:, None]), 0.0, NEG).astype(np.float32)
    strict = (same & (idx[None, :] > idx[:, None])).astype(np.float32)
    ident = np.eye(128, dtype=np.float32)
    cs = (idx % CH == 0).astype(np.float32)[:, None]
    half = np.stack([(idx < CH), (idx >= CH)], 1).astype(np.float32)
    sel = np.zeros((2, 128), np.float32)
    sel[0] = 1.0
    c = np.concatenate([U, BM, negm, strict, ident, cs, half, np.ones((128, 1), np.float32)], 1)
    return {"gc": np.ascontiguousarray(c), "gsel": sel}


GC_U, GC_BM, GC_NEG, GC_STR, GC_ID, GC_CS, GC_HALF, GC_ONE = 0, 128, 256, 384, 512, 640, 641, 643
GC_W = 644


class _Stop(Exception):
    pass


def stage_gdn_seq(nc, pj_d, ba_d, convw_d, hp_d, onorm_d, gc_d, gsel_d, out_d, SL):
    try:
        _stage_gdn_seq(nc, pj_d, ba_d, convw_d, hp_d, onorm_d, gc_d, gsel_d, out_d, SL)
    except _Stop:
        pass


def _stage_gdn_seq(nc, pj_d, ba_d, convw_d, hp_d, onorm_d, gc_d, gsel_d, out_d, SL):
    LVL = float(os.environ.get("GDN_DBG", "99"))

    def chk(P, k):
        if LVL <= k:
            P.barrier()
            P.emit()
            raise _Stop()
    NT = SL // 128
    NSEG = SL // SEG
    with contextlib.ExitStack() as st:
        P = Prog(nc, st)
        gc, gcb = P.sb("gc", [128, GC_W], F32)
        P.dma("sp", lambda e: e.dma_start(out=gc[:], in_=gc_d[:, :]), writes=[gcb], sbuf=gcb)
        gsel, gselb = P.sb("gsel", [2, 128], F32)
        P.dma("sp", lambda e: e.dma_start(out=gsel[:], in_=gsel_d[:, :]), writes=[gselb], sbuf=gselb)
        cw, cwb = P.sb("cw", [128, 6, 4], F32)
        P.dma("sp", lambda e: e.dma_start(out=cw[:], in_=convw_d[:, :, :]), writes=[cwb], sbuf=cwb)
        hp, hpb = P.sb("hp", [128, 4], F32)
        P.dma("sp", lambda e: e.dma_start(out=hp[:], in_=hp_d[:, :]), writes=[hpb], sbuf=hpb)
        onm, onmb = P.sb("onm", [128, 1], F32)
        P.dma("sp", lambda e: e.dma_start(out=onm[:], in_=onorm_d[:, :]), writes=[onmb], sbuf=onmb)
        cbf, cbfb = P.sb("cbf", [128, 3, 128], BF16)
        P.op("dve", lambda e: e.tensor_copy(cbf[:, 0, :], gc[:, GC_ID:GC_ID + 128]), reads=[gcb], writes=[cbfb])
        P.op("dve", lambda e: e.tensor_copy(cbf[:, 1, :], gc[:, GC_STR:GC_STR + 128]), reads=[gcb], writes=[cbfb])
        P.op("dve", lambda e: e.memset(cbf[:, 2, :], 1.0), writes=[cbfb])
        identb = cbf[:, 0, :]
        onesb16 = cbf[:, 2, :]
        dg, dgb = P.sb("dg", [128, 24, 128], BF16)
        for ci_ in range(6):
            for s_ in range(4):
                P.op("dve", lambda e, ci_=ci_, s_=s_: e.tensor_scalar_mul(dg[:, ci_ * 4 + s_, :], gc[:, GC_ID:GC_ID + 128], cw[:, ci_, s_:s_ + 1]),
                     reads=[gcb, cwb], writes=[dgb])
        U = gc[:, GC_U:GC_U + 128]
        pf = [P.ps(f"pf{i}", [128, 512]) for i in range(6)]
        pb = [P.ps(f"pb{i}", [128, 1024], BF16) for i in range(2)]
        ba, bab = P.sb("ba", [128, NT, 4], F32)
        P.dma("sp", lambda e: e.dma_start(out=ba[:], in_=ba_d.rearrange("(n p) x -> p n x", p=128)), writes=[bab], sbuf=bab)
        gt, gtb = P.sb("gt", [128, NT, 2], F32)
        beta, betab = P.sb("beta", [128, NT, 2], F32)
        gcum, gcumb = P.sb("gcum", [128, NT, 2], F32)
        eg, egb = P.sb("eg", [128, NT, 2], F32)
        edec, edecb = P.sb("edec", [128, NT, 2], F32)
        gh, ghb = P.sb("gh", [128, NT, 2, 2], F32)
        glx, glxb = P.sb("glx", [128, NT, 2, 2], F32)
        gm, gmb = P.sb("gm", [128, NT, 2, 4], F32)
        tmp, tmpb = P.sb("gtmp", [128, NT, 2], F32)
        ea, eab = P.sb("ea", [128, 2], F32)
        P.op("act", lambda e: e.activation(beta[:], ba[:, :, 0:2], AF.Exp, scale=-1.0), reads=[bab], writes=[betab])
        P.op("dve", lambda e: e.tensor_scalar(beta[:], beta[:], 1.0, 1.0, ALU.mult, ALU.add), reads=[betab], writes=[betab])
        P.op("dve", lambda e: e.reciprocal(beta[:], beta[:]), reads=[betab], writes=[betab])
        for e_ in range(2):
            P.op("act", lambda e, e_=e_: e.activation(tmp[:, :, e_], ba[:, :, 2 + e_], AF.Exp, bias=hp[:, 2 + e_:3 + e_]),
                 reads=[bab, hpb], writes=[tmpb])
        P.op("act", lambda e: e.activation(tmp[:], tmp[:], AF.Ln, bias=1.0), reads=[tmpb], writes=[tmpb])
        P.op("act", lambda e: e.activation(ea[:], hp[:, 0:2], AF.Exp), reads=[hpb], writes=[eab])
        for e_ in range(2):
            P.op("dve", lambda e, e_=e_: e.tensor_scalar(gt[:, :, e_], tmp[:, :, e_], ea[:, e_:e_ + 1], -1.0, ALU.mult, ALU.mult),
                 reads=[tmpb, eab], writes=[gtb])
        G2 = NT * 2
        def gflat(t):
            return t[:].rearrange("p n e -> p (n e)")
        for c0 in range(0, G2, 512):
            c1 = min(G2, c0 + 512)
            ps, psb = pf[0]
            P.op("pe", lambda e, c0=c0, c1=c1, ps=ps: e.matmul(ps[:, 0:c1 - c0], U, gflat(gt)[:, c0:c1], start=True, stop=True),
                 reads=[gcb, gtb], writes=[psb])
            P.op("dve", lambda e, c0=c0, c1=c1, ps=ps: e.tensor_copy(gflat(gcum)[:, c0:c1], ps[:, 0:c1 - c0]), reads=[psb], writes=[gcumb])
            ps2, ps2b = pf[1]
            P.op("pe", lambda e, c0=c0, c1=c1, ps2=ps2: e.matmul(ps2[:, 0:c1 - c0], gc[:, GC_BM:GC_BM + 128], gflat(gt)[:, c0:c1], start=True, stop=True),
                 reads=[gcb, gtb], writes=[ps2b])
            P.op("dve", lambda e, c0=c0, c1=c1, ps2=ps2: e.tensor_tensor(gflat(edec)[:, c0:c1], ps2[:, 0:c1 - c0], gflat(gcum)[:, c0:c1], ALU.subtract),
                 reads=[ps2b, gcumb], writes=[edecb])
        P.op("act", lambda e: e.activation(edec[:], edec[:], AF.Exp), reads=[edecb], writes=[edecb])
        P.op("act", lambda e: e.activation(eg[:], gcum[:], AF.Exp), reads=[gcumb], writes=[egb])
        for h_ in range(2):
            P.op("dve", lambda e, h_=h_: e.tensor_scalar_mul(gh[:, :, :, h_], gt[:], gc[:, GC_HALF + h_:GC_HALF + h_ + 1]),
                 reads=[gtb, gcb], writes=[ghb])
        G4 = NT * 4
        onesf, onesfb = P.sb("onesf", [128, 128], F32)
        P.op("pool", lambda e: e.memset(onesf[:], 1.0), writes=[onesfb])
        for c0 in range(0, G4, 512):
            c1 = min(G4, c0 + 512)
            ps, psb = pf[2]
            P.op("pe", lambda e, c0=c0, c1=c1, ps=ps: e.matmul(ps[:, 0:c1 - c0], onesf[:], gh[:].rearrange("p n e h -> p (n e h)")[:, c0:c1], start=True, stop=True),
                 reads=[onesfb, ghb], writes=[psb])
            P.op("act", lambda e, c0=c0, c1=c1, ps=ps: e.activation(glx[:].rearrange("p n e h -> p (n e h)")[:, c0:c1], ps[:, 0:c1 - c0], AF.Exp),
                 reads=[psb], writes=[glxb])
        P.op("dve", lambda e: e.tensor_copy(gm[:, :, :, 0], gt[:]), reads=[gtb], writes=[gmb])
        P.op("dve", lambda e: e.tensor_scalar_mul(gm[:, :, :, 3], gt[:], -1.0), reads=[gtb], writes=[gmb])
        for k_ in (1, 2):
            P.op("act", lambda e, k_=k_: e.activation(gm[:, :, :, k_], gt[:], AF.Identity, bias=gc[:, GC_CS:GC_CS + 1], scale=0.0),
                 reads=[gtb, gcb], writes=[gmb])
        Rr, Rrb = P.sb("Rr", [2, 2, SEG], F32)
        Lr, Lrb = P.sb("Lr", [2, 2, SEG], F32)
        chk(P, 1)
        Sf = [P.sb(f"Sf{e_}", [128, 128], F32) for e_ in range(2)]
        Sb = [P.sb(f"Sb{e_}", [128, 128], BF16) for e_ in range(2)]
        for e_ in range(2):
            P.op("pool", lambda e, e_=e_: e.memset(Sf[e_][0][:], 0.0), writes=[Sf[e_][1]])
            P.op("pool", lambda e, e_=e_: e.memset(Sb[e_][0][:], 0.0), writes=[Sb[e_][1]])
        xin = [P.sb(f"xin{i}", [128, 4 + SEG], BF16) for i in range(2)]
        acc, accb = P.sb("acc", [128, SEG], F32)
        ysil, ysilb = P.sb("ysil", [128, SEG], F32)
        sq, sqb = P.sb("sqs", [128, SEG], BF16)
        rn, rnb = P.sb("rn", [128, TB], F32)
        egr, egrb = P.sb("egr", [128, TB], F32)
        segq = [P.sb(f"segq{e_}", [128, SEG], BF16) for e_ in range(2)]
        segqd = [P.sb(f"segqd{e_}", [128, SEG], BF16) for e_ in range(2)]
        segk = [P.sb(f"segk{e_}", [128, SEG], BF16) for e_ in range(2)]
        segv = [P.sb(f"segv{e_}", [128, SEG], BF16) for e_ in range(2)]
        segz = [P.sb(f"segz{e_}", [128, SEG], BF16) for e_ in range(2)]
        sego = [P.sb(f"sego{e_}", [128, SEG], BF16) for e_ in range(2)]
        def mk(name, dt, w=128, n=8, p=128):
            return [P.sb(f"{name}{i}", [p, w], dt) for i in range(n)]
        dT = mk("dT", F32); dTs = mk("dTs", F32); Wt = mk("Wt", BF16); inT = mk("inT", BF16)
        Yp = [mk(f"Yp{k_}_", BF16) for k_ in range(6)]
        Wp = [mk(f"Wp{k_}_", BF16) for k_ in range(5)]
        Pm = [mk(f"Pm{k_}_", BF16) for k_ in range(2)]
        kg = mk("kg", BF16); kdec = mk("kdec", BF16); vtok = mk("vtok", BF16); nwT = mk("nwT", BF16)
        vnew = mk("vnew", BF16); otok = mk("otok", F32); onb = mk("onb", BF16)
        ssq = mk("ssq", F32, w=1); junk = mk("junk", F32)
        xi = 0
        pfi = [0]
        pbi = [0]

        def PF():
            i = pfi[0]
            pfi[0] += 1
            t, b = pf[i % 6]
            return t, b

        def PB():
            i = pbi[0]
            pbi[0] += 1
            t, b = pb[i % 2]
            return t, b

        kk = 0
        for sg in range(NSEG):
            t_lo = sg * SEG
            for e_ in range(2):
                for t0 in range(0, SEG // 128, 4):
                    for (dstt, dstb, cofs) in ((Rr, Rrb, 0), (Lr, Lrb, 2)):
                        ps, psb = pf[3 + (kk % 2)]
                        kk += 1
                        for q_ in range(4):
                            P.op("pe", lambda e, ps=ps, e_=e_, t=sg * (SEG // 128) + t0 + q_, q_=q_, cofs=cofs: e.matmul(
                                ps[0:2, q_ * 128:(q_ + 1) * 128], gm[:, t, e_, cofs:cofs + 2], U, start=True, stop=True),
                                reads=[gmb, gcb], writes=[psb])
                        P.op("act", lambda e, ps=ps, dstt=dstt, e_=e_, t0=t0: e.activation(dstt[:, e_, t0 * 128:(t0 + 4) * 128], ps[0:2, :], AF.Copy),
                             reads=[psb], writes=[dstb])
            chk(P, 2)
            for e_ in range(2):
                for kind in range(4):
                    xt, xb = xin[xi % 2]
                    xi += 1
                    P.dma("sp", lambda e, xt=xt, kind=kind, e_=e_, t_lo=t_lo: e.dma_start(out=xt[:, 4:4 + SEG], in_=pj_d[kind, e_, :, t_lo:t_lo + SEG]),
                          writes=[xb], sbuf=xb)
                    if kind == 3:
                        P.op("act", lambda e, xt=xt, e_=e_: e.activation(segz[e_][0][:], xt[:, 4:4 + SEG], AF.Silu), reads=[xb], writes=[segz[e_][1]])
                        continue
                    if sg == 0:
                        P.op("pool", lambda e, xt=xt: e.memset(xt[:, 0:4], 0.0), writes=[xb])
                    else:
                        P.dma("sp", lambda e, xt=xt, kind=kind, e_=e_, t_lo=t_lo: e.dma_start(out=xt[:, 1:4], in_=pj_d[kind, e_, :, t_lo - 3:t_lo]),
                              writes=[xb], sbuf=xb)
                    ci = kind * 2 + e_
                    dst, dstb = (segv[e_] if kind == 2 else (ysil, ysilb))
                    for tb in range(SEG // TB):
                        ps, psb = PF()
                        for s_ in range(4):
                            P.op("pe", lambda e, ps=ps, xt=xt, ci=ci, s_=s_, tb=tb: e.matmul(
                                ps[:], dg[:, ci * 4 + s_, :], xt[:, 1 + s_ + tb * TB:1 + s_ + (tb + 1) * TB],
                                start=(s_ == 0), stop=(s_ == 3)), reads=[dgb, xb], writes=[psb])
                        P.op("act", lambda e, ps=ps, dst=dst, tb=tb: e.activation(dst[:, tb * TB:(tb + 1) * TB], ps[:], AF.Silu),
                             reads=[psb], writes=[dstb])
                    if kind == 2:
                        continue
                    P.op("act", lambda e: e.activation(sq[:], ysil[:], AF.Square), reads=[ysilb], writes=[sqb])
                    for tb in range(SEG // TB):
                        sl = slice(tb * TB, (tb + 1) * TB)
                        ps, psb = PF()
                        P.op("pe", lambda e, ps=ps, sl=sl: e.matmul(ps[:], onesb16, sq[:, sl], start=True, stop=True), reads=[cbfb, sqb], writes=[psb])
                        P.op("act", lambda e, ps=ps: e.activation(rn[:], ps[:], AF.Sqrt, bias=EPS, scale=1.0), reads=[psb], writes=[rnb])
                        P.op("dve", lambda e: e.reciprocal(rn[:], rn[:]), reads=[rnb], writes=[rnb])
                        if kind == 1:
                            P.op("dve", lambda e, sl=sl, e_=e_: e.tensor_tensor(segk[e_][0][:, sl], ysil[:, sl], rn[:], ALU.mult),
                                 reads=[ysilb, rnb], writes=[segk[e_][1]])
                        else:
                            P.op("dve", lambda e, sl=sl, e_=e_: e.scalar_tensor_tensor(segq[e_][0][:, sl], ysil[:, sl], float(128 ** -0.5), rn[:], ALU.mult, ALU.mult),
                                 reads=[ysilb, rnb], writes=[segq[e_][1]])
                            ps2, ps2b = PF()
                            P.op("pe", lambda e, ps2=ps2, sl=sl, e_=e_, t_lo=t_lo: e.matmul(ps2[:], gsel[:], Rr[:, e_, sl], start=True, stop=True),
                                 reads=[gselb, Rrb], writes=[ps2b])
                            P.op("act", lambda e, ps2=ps2: e.activation(egr[:], ps2[:], AF.Exp), reads=[ps2b], writes=[egrb])
                            P.op("dve", lambda e, sl=sl, e_=e_: e.tensor_tensor(segqd[e_][0][:, sl], segq[e_][0][:, sl], egr[:], ALU.mult),
                                 reads=[segq[e_][1], egrb], writes=[segqd[e_][1]])
            chk(P, 3)
            def unit(tg, cs_, e_, w):
                gs_ = cs_
                ui = w % 4
                PF = lambda: pf[ui]
                PB = lambda: pb[ui % 2]
                pTo = (ui // 2) * 384
                qT = segq[e_][0][:, cs_]; qdT = segqd[e_][0][:, cs_]; kT = segk[e_][0][:, cs_]; vT = segv[e_][0][:, cs_]
                rb = [segq[e_][1], segqd[e_][1], segk[e_][1], segv[e_][1]]
                psA, psAb = PF()
                P.op("pe", lambda e, psA=psA, kT=kT: e.matmul(psA[:, 0:128], kT, kT, start=True, stop=True), reads=[rb[2]], writes=[psAb])
                P.op("pe", lambda e, psA=psA, kT=kT, qT=qT: e.matmul(psA[:, 128:256], kT, qT, start=True, stop=True), reads=[rb[2], rb[0]], writes=[psAb])
                P.op("pe", lambda e, psA=psA, e_=e_, gs_=gs_: e.matmul(psA[:, 256:384], Lr[:, e_, gs_], Rr[:, e_, gs_], start=True, stop=False),
                     reads=[Lrb, Rrb], writes=[psAb])
                P.op("pe", lambda e, psA=psA: e.matmul(psA[:, 256:384], gc[:, GC_ID:GC_ID + 128], gc[:, GC_NEG:GC_NEG + 128], start=False, stop=True),
                     reads=[gcb], writes=[psAb])
                P.op("act", lambda e, psA=psA, w=w: e.activation(dT[w][0][:], psA[:, 256:384], AF.Exp), reads=[psAb], writes=[dT[w][1]])
                yield None
                P.op("pool", lambda e, w=w: e.tensor_tensor(dTs[w][0][:], dT[w][0][:], gc[:, GC_STR:GC_STR + 128], ALU.mult),
                     reads=[dT[w][1], gcb], writes=[dTs[w][1]])
                yield None
                P.op("dve", lambda e, psA=psA, w=w, tg=tg, e_=e_: e.scalar_tensor_tensor(Wt[w][0][:], psA[:, 0:128], beta[:, tg, e_:e_ + 1], dTs[w][0][:], ALU.mult, ALU.mult),
                     reads=[psAb, betab, dTs[w][1]], writes=[Wt[w][1]])
                yield None
                P.op("dve", lambda e, psA=psA, w=w: e.tensor_tensor(inT[w][0][:], psA[:, 128:256], dT[w][0][:], ALU.mult),
                     reads=[psAb, dT[w][1]], writes=[inT[w][1]])
                yield None
                pT, pTb = PB()
                P.op("pe", lambda e, pT=pT, w=w: e.transpose(pT[:, pTo:pTo + 128], Wt[w][0][:], identb), reads=[Wt[w][1], cbfb], writes=[pTb])
                P.op("act", lambda e, pT=pT, w=w: e.activation(Yp[0][w][0][:], pT[:, pTo:pTo + 128], AF.Copy), reads=[pTb], writes=[Yp[0][w][1]])
                yield None
                P.op("pe", lambda e, pT=pT, kT=kT: e.transpose(pT[:, pTo + 128:pTo + 256], kT, identb), reads=[rb[2], cbfb], writes=[pTb])
                P.op("pe", lambda e, pT=pT, vT=vT: e.transpose(pT[:, pTo + 256:pTo + 384], vT, identb), reads=[rb[3], cbfb], writes=[pTb])
                P.op("act", lambda e, pT=pT, w=w, tg=tg, e_=e_: e.activation(kg[w][0][:], pT[:, pTo + 128:pTo + 256], AF.Copy, scale=eg[:, tg, e_:e_ + 1]),
                     reads=[pTb, egb], writes=[kg[w][1]])
                yield None
                P.op("act", lambda e, pT=pT, w=w, tg=tg, e_=e_: e.activation(kdec[w][0][:], pT[:, pTo + 128:pTo + 256], AF.Copy, scale=edec[:, tg, e_:e_ + 1]),
                     reads=[pTb, edecb], writes=[kdec[w][1]])
                yield None
                P.op("dve", lambda e, pT=pT, w=w: e.tensor_copy(vtok[w][0][:], pT[:, pTo + 256:pTo + 384]), reads=[pTb], writes=[vtok[w][1]])
                yield None
                Wcur = Wt[w]
                for k_ in range(5):
                    ps, psb = PF()
                    Ycur = Yp[k_][w]
                    P.op("pe", lambda e, ps=ps, Wc=Wcur, Yc=Ycur: e.matmul(ps[:, 0:128], Wc[0][:], Yc[0][:], start=True, stop=True),
                         reads=[Wcur[1], Ycur[1]], writes=[psb])
                    psq, psqb = ps, psb
                    if os.environ.get("GDN_VAR") == "2":
                        psq, psqb = PF()
                    if k_ < 4:
                        P.op("pe", lambda e, ps=psq, Wc=Wcur, Yc=Ycur: e.matmul(ps[:, 128:256], Yc[0][:], Wc[0][:], start=True, stop=True),
                             reads=[Wcur[1], Ycur[1]], writes=[psqb])
                    Yn = Yp[k_ + 1][w]
                    P.op("act", lambda e, ps=ps, Yn=Yn: e.activation(Yn[0][:], ps[:, 0:128], AF.Copy), reads=[psb], writes=[Yn[1]])
                    yield None
                    if k_ < 4:
                        Wn = Wp[k_][w]
                        if os.environ.get("GDN_VAR") == "1":
                            P.op("act", lambda e, ps=ps, Wn=Wn: e.activation(Wn[0][:], ps[:, 128:256], AF.Copy), reads=[psb], writes=[Wn[1]])
                            yield None
                        else:
                            P.op("dve", lambda e, ps=psq, Wn=Wn: e.tensor_copy(Wn[0][:], ps[:, 128:256]), reads=[psqb], writes=[Wn[1]])
                            yield None
                        Wcur = Wn
                Pc = Pm[0][w]
                P.op("pool", lambda e, Pc=Pc, w=w: e.tensor_tensor(Pc[0][:], identb, Wt[w][0][:], ALU.subtract), reads=[cbfb, Wt[w][1]], writes=[Pc[1]])
                yield None
                for k_ in range(1, 6):
                    ps, psb = PF()
                    Yk = Yp[k_][w]
                    P.op("pe", lambda e, ps=ps, Yk=Yk, Pc=Pc: e.matmul(ps[:, 0:128], Yk[0][:], Pc[0][:], start=True, stop=True),
                         reads=[Yk[1], Pc[1]], writes=[psb])
                    Pn = Pm[k_ % 2][w]
                    P.op("dve", lambda e, ps=ps, Pn=Pn, Pc=Pc: e.tensor_tensor(Pn[0][:], ps[:, 0:128], Pc[0][:], ALU.add), reads=[psb, Pc[1]], writes=[Pn[1]])
                    yield None
                    Pc = Pn
                TT = Pc
                ps, psb = PF()
                P.op("pe", lambda e, ps=ps, w=w, TT=TT: e.matmul(ps[:, 0:128], kg[w][0][:], TT[0][:], start=True, stop=True),
                     reads=[kg[w][1], TT[1]], writes=[psb])
                P.op("act", lambda e, ps=ps, w=w: e.activation(nwT[w][0][:], ps[:, 0:128], AF.Copy, scale=-1.0), reads=[psb], writes=[nwT[w][1]])
                yield None
                yield "P2"
                PF = lambda: pf[4 + e_]
                S_f, S_fb = Sf[e_]
                S_b, S_bb = Sb[e_]
                for h_ in range(2):
                    r = slice(h_ * CH, (h_ + 1) * CH)
                    ps, psb = PF()
                    P.op("pe", lambda e, ps=ps, TT=TT, w=w, r=r: e.matmul(ps[r, 0:128], TT[0][:, r], vtok[w][0][:], start=True, stop=False),
                         reads=[TT[1], vtok[w][1]], writes=[psb])
                    P.op("pe", lambda e, ps=ps, w=w, r=r, S_b=S_b: e.matmul(ps[r, 0:128], nwT[w][0][:, r], S_b[:], start=False, stop=True),
                         reads=[nwT[w][1], S_bb], writes=[psb])
                    P.op("act", lambda e, ps=ps, w=w, r=r, tg=tg, e_=e_: e.activation(vnew[w][0][r, :], ps[r, 0:128], AF.Copy, scale=beta[r, tg, e_:e_ + 1]),
                         reads=[psb, betab], writes=[vnew[w][1]])
                    yield None
                    P.op("pe", lambda e, ps=ps, r=r, qdT=qdT, S_b=S_b: e.matmul(ps[r, 128:256], qdT[:, r], S_b[:], start=True, stop=False),
                         reads=[rb[1], S_bb], writes=[psb])
                    P.op("pe", lambda e, ps=ps, r=r, w=w: e.matmul(ps[r, 128:256], inT[w][0][r, r], vnew[w][0][r, :], start=False, stop=True),
                         reads=[inT[w][1], vnew[w][1]], writes=[psb])
                    P.op("pe", lambda e, ps=ps, r=r, w=w: e.matmul(ps[:, 256:384], kdec[w][0][r, :], vnew[w][0][r, :], start=True, stop=True),
                         reads=[kdec[w][1], vnew[w][1]], writes=[psb])
                    P.op("act", lambda e, ps=ps, r=r, w=w: e.activation(otok[w][0][r, :], ps[r, 128:256], AF.Copy), reads=[psb], writes=[otok[w][1]])
                    yield None
                    gcol = glx[:, tg, e_, h_:h_ + 1]
                    P.op("dve", lambda e, ps=ps, S_b=S_b, S_f=S_f, gcol=gcol: e.scalar_tensor_tensor(S_b[:], S_f[:], gcol, ps[:, 256:384], ALU.mult, ALU.add),
                         reads=[psb, S_fb, glxb], writes=[S_bb])
                    yield None
                    P.op("dve", lambda e, ps=ps, S_f=S_f, gcol=gcol: e.scalar_tensor_tensor(S_f[:], S_f[:], gcol, ps[:, 256:384], ALU.mult, ALU.add),
                         reads=[psb, S_fb, glxb], writes=[S_fb])
                    yield None
                P.op("act", lambda e, w=w: e.activation(junk[w][0][:], otok[w][0][:], AF.Square, accum_out=ssq[w][0][:]), reads=[otok[w][1]], writes=[junk[w][1], ssq[w][1]])
                yield None
                P.op("act", lambda e, w=w: e.activation(ssq[w][0][:], ssq[w][0][:], AF.Sqrt, bias=EPS, scale=1.0 / 128), reads=[ssq[w][1]], writes=[ssq[w][1]])
                yield None
                P.op("dve", lambda e, w=w: e.reciprocal(ssq[w][0][:], ssq[w][0][:]), reads=[ssq[w][1]], writes=[ssq[w][1]])
                yield None
                P.op("dve", lambda e, w=w: e.tensor_scalar_mul(onb[w][0][:], otok[w][0][:], ssq[w][0][:, 0:1]), reads=[otok[w][1], ssq[w][1]], writes=[onb[w][1]])
                yield None
                pT2, pT2b = pb[e_]
                P.op("pe", lambda e, pT2=pT2, w=w: e.transpose(pT2[:, 768:896], onb[w][0][:], identb), reads=[onb[w][1], cbfb], writes=[pT2b])
                P.op("dve", lambda e, pT2=pT2, e_=e_, cs_=cs_: e.scalar_tensor_tensor(sego[e_][0][:, cs_], pT2[:, 768:896], onm[:, 0:1], segz[e_][0][:, cs_], ALU.mult, ALU.mult),
                     reads=[pT2b, onmb, segz[e_][1]], writes=[sego[e_][1]])
                yield None

            NU = 4
            ulist = []
            for tl in range(SEG // 128):
                for e_ in range(2):
                    ulist.append((sg * (SEG // 128) + tl, slice(tl * 128, (tl + 1) * 128), e_))
            prev_ch = []
            for gi_ in range(0, len(ulist) + NU, NU):
                grp = ulist[gi_:gi_ + NU]
                p1 = []
                for ui_, (tg_, cs2_, e2_) in enumerate(grp):
                    gen = unit(tg_, cs2_, e2_, ((gi_ // NU) % 2) * NU + ui_)
                    p1.append((gen, e2_))
                parked = []
                chains = prev_ch
                while p1 or any(chains):
                    for item in list(p1):
                        if next(item[0]) == "P2":
                            p1.remove(item)
                            parked.append(item)
                    for ch in chains:
                        if ch:
                            try:
                                next(ch[0][0])
                            except StopIteration:
                                ch.pop(0)
                prev_ch = [[it for it in parked if it[1] == 0], [it for it in parked if it[1] == 1]]
            for e_ in range(2):
                P.dma("sp", lambda e, e_=e_, t_lo=t_lo: e.dma_start(out=out_d[e_, :, t_lo:t_lo + SEG], in_=sego[e_][0][:]), reads=[sego[e_][1]], sbuf=sego[e_][1])
        P.barrier()
        P.emit()


DILS = (1, 4, 16)


def t5_bucket_np(dist):
    max_exact = 16
    d_f = np.maximum(dist, 1).astype(np.float32)
    large = max_exact + (np.log(d_f / np.float32(max_exact)) / np.float32(np.log(2048 / max_exact))
                         * np.float32(32 - max_exact)).astype(np.int32)
    large = np.minimum(large, 31)
    return np.where(dist < max_exact, dist, large)


def dsa_bias_tiles(rel_bias, heads):
    k = np.arange(128)[:, None]
    j = np.arange(256)[None, :]
    rel = np.where(j < 128, j + 128 - k, j - 128 - k)
    valid = np.where(j < 128, j <= k, (j - 128) >= k)
    out = np.zeros((3, len(heads), 128, 256), np.float32)
    for g, d in enumerate(DILS):
        bucket = t5_bucket_np(np.clip(rel, 0, 128) * d)
        for e_, h in enumerate(heads):
            vals = rel_bias[:, g * 16 + h][bucket]
            out[g, e_] = np.where(valid, vals, np.float32(NEG))
    return out


def stage_dsa(nc, q_d, k_d, v_d, bias_d, ident_d, out_d, SL):
    NSEG = SL // SEG
    scale = float(128 ** -0.5)
    with contextlib.ExitStack() as st:
        P = Prog(nc, st)
        idb, idbb = P.sb("idb", [128, 128], BF16)
        P.dma("sp", lambda e: e.dma_start(out=idb[:], in_=ident_d[:, :]), writes=[idbb], sbuf=idbb)
        ones, onesb = P.sb("ones", [128, 128], BF16)
        P.op("pool", lambda e: e.memset(ones[:], 1.0), writes=[onesb])
        bias, biasb = P.sb("bias", [128, 6, 256], F32)
        P.dma("sp", lambda e: e.dma_start(out=bias[:], in_=bias_d.rearrange("g e k j -> k (g e) j")), writes=[biasb], sbuf=biasb)
        qs = [P.sb(f"qs{i}", [128, SEG], BF16) for i in range(2)]
        kw = [P.sb(f"kw{i}", [128, 2 * SEG], BF16) for i in range(2)]
        vw = [P.sb(f"vw{i}", [128, 2 * SEG], BF16) for i in range(2)]
        vt = [P.sb(f"vt{i}", [128, 32, 128], BF16) for i in range(2)]
        acc, accb = P.sb("acc", [128, 2, SEG], F32)
        rec, recb = P.sb("rec", [128, SEG], F32)
        ot = [P.sb(f"ot{i}", [128, SEG], BF16) for i in range(2)]
        NSLOT = 5
        tmp = [P.sb(f"tmp{i}", [128, 256], F32) for i in range(NSLOT)]
        pt = [P.sb(f"pt{i}", [128, 256], BF16) for i in range(NSLOT)]
        pf = [P.ps(f"pf{i}", [128, 512]) for i in range(6)]
        pb = [P.ps(f"pb{i}", [128, 1024], BF16) for i in range(2)]
        it = 0
        nblk = 0
        npf = 0
        npb = 0
        for sg in range(NSEG):
            t_lo = sg * SEG
            for e_ in range(2):
                for g, d in enumerate(DILS):
                    q, qb = qs[it % 2]
                    kk, kb = kw[it % 2]
                    vv, vb = vw[it % 2]
                    vtt, vtb = vt[it % 2]
                    it += 1
                    P.dma("sp", lambda e, q=q, g=g, e_=e_, t_lo=t_lo: e.dma_start(out=q[:], in_=q_d[g, e_, :, t_lo:t_lo + SEG]), writes=[qb], sbuf=qb)
                    w0 = 0 if sg > 0 else SEG
                    P.dma("sp", lambda e, kk=kk, g=g, e_=e_, t_lo=t_lo, w0=w0: e.dma_start(out=kk[:, w0:2 * SEG], in_=k_d[g, e_, :, t_lo - SEG + w0:t_lo + SEG]), writes=[kb], sbuf=kb)
                    P.dma("sp", lambda e, vv=vv, g=g, e_=e_, t_lo=t_lo, w0=w0: e.dma_start(out=vv[:, w0:2 * SEG], in_=v_d[g, e_, :, t_lo - SEG + w0:t_lo + SEG]), writes=[vb], sbuf=vb)
                    nper = SEG // (128 * d)
                    def kcols(r, n_):
                        start = SEG + 128 * n_ * d + r
                        return slice(start, start + 127 * d + 1, d)
                    slots = []
                    for r in range(d):
                        for n_ in range(-1, nper):
                            if n_ == -1 and sg == 0:
                                continue
                            slots.append((r, n_))
                    for s0 in range(0, len(slots), 8):
                        grp = slots[s0:s0 + 8]
                        pT, pTb = pb[npb % 2]
                        npb += 1
                        for i_, (r, n_) in enumerate(grp):
                            P.op("pe", lambda e, pT=pT, i_=i_, vv=vv, c=kcols(r, n_): e.transpose(pT[:, i_ * 128:(i_ + 1) * 128], vv[:, c], idb[:]),
                                 reads=[vb, idbb], writes=[pTb])
                        n_g = len(grp)
                        P.op("act", lambda e, pT=pT, vtt=vtt, s0=s0, n_g=n_g: e.activation(
                            vtt[:, s0:s0 + n_g, :], pT[:, 0:n_g * 128].rearrange("p (s c) -> p s c", c=128), AF.Copy),
                            reads=[pTb], writes=[vtb])
                    sidx = {s: i for i, s in enumerate(slots)}

                    def blk(slot, r, n_, kk=kk, kb=kb, q=q, qb=qb, vtt=vtt, vtb=vtb, g=g, e_=e_, d=d, sidx=sidx):
                        has_prev = (r, n_ - 1) in sidx
                        qc = slice(128 * n_ * d + r, 128 * n_ * d + r + 127 * d + 1, d)
                        ps, psb = pf[slot]
                        lo = 0 if has_prev else 128
                        def kc(n2):
                            start = SEG + 128 * n2 * d + r
                            return slice(start, start + 127 * d + 1, d)
                        if has_prev:
                            P.op("pe", lambda e, c=kc(n_ - 1): e.matmul(ps[:, 0:128], kk[:, c], q[:, qc], start=True, stop=True),
                                 reads=[kb, qb], writes=[psb])
                        P.op("pe", lambda e, c=kc(n_): e.matmul(ps[:, 128:256], kk[:, c], q[:, qc], start=True, stop=True),
                             reads=[kb, qb], writes=[psb])
                        yield None
                        tm, tmb = tmp[slot]
                        pp, ppb = pt[slot]
                        P.op("dve", lambda e: e.scalar_tensor_tensor(
                            tm[:, lo:256], ps[:, lo:256], scale, bias[:, g * 2 + e_, lo:256], ALU.mult, ALU.add),
                            reads=[psb, biasb], writes=[tmb])
                        yield None
                        P.op("act", lambda e: e.activation(pp[:, lo:256], tm[:, lo:256], AF.Exp), reads=[tmb], writes=[ppb])
                        yield None
                        halves = ([(n_ - 1, 0)] if has_prev else []) + [(n_, 128)]
                        for hi, (nn, c0) in enumerate(halves):
                            si = sidx[(r, nn)]
                            P.op("pe", lambda e, si=si, c0=c0, hi=hi, nh=len(halves): e.matmul(
                                ps[:, 256:384], vtt[:, si, :], pp[:, c0:c0 + 128], start=(hi == 0), stop=(hi == nh - 1)),
                                reads=[vtb, ppb], writes=[psb])
                        for hi, (nn, c0) in enumerate(halves):
                            P.op("pe", lambda e, c0=c0, hi=hi, nh=len(halves): e.matmul(
                                ps[:, 384:512], ones[:], pp[:, c0:c0 + 128], start=(hi == 0), stop=(hi == nh - 1)),
                                reads=[onesb, ppb], writes=[psb])
                        src = lambda: ps[:, 256:512].rearrange("p (a c) -> p a c", a=2)
                        if g == 0:
                            P.op("act", lambda e: e.activation(acc[:, :, qc], src(), AF.Copy), reads=[psb], writes=[accb])
                        else:
                            P.op("dve", lambda e: e.tensor_tensor(acc[:, :, qc], acc[:, :, qc], src(), ALU.add), reads=[psb, accb], writes=[accb])
                        yield None

                    todo = [(r, n_) for r in range(d) for n_ in range(nper)]
                    active = []
                    free = list(range(NSLOT))
                    ti = 0
                    while ti < len(todo) or active:
                        if ti < len(todo) and free:
                            sl_ = free.pop(0)
                            active.append((blk(sl_, *todo[ti]), sl_))
                            ti += 1
                        for item in list(active):
                            try:
                                next(item[0])
                            except StopIteration:
                                active.remove(item)
                                free.append(item[1])
                o, ob = ot[(sg * 2 + e_) % 2]
                P.op("dve", lambda e: e.reciprocal(rec[:], acc[:, 1, :]), reads=[accb], writes=[recb])
                P.op("dve", lambda e, o=o: e.tensor_tensor(o[:], acc[:, 0, :], rec[:], ALU.mult), reads=[accb, recb], writes=[ob])
                P.dma("sp", lambda e, o=o, e_=e_, t_lo=t_lo: e.dma_start(out=out_d[e_, :, t_lo:t_lo + SEG], in_=o[:]), reads=[ob], sbuf=ob)
        P.barrier()
        P.emit()


T_ = TPC
NG = 4 * 4 * 16 + 16


def ba_extra(W_d, ba_d):
    def mk(P):
        wba, wbab = P.sb("wba", [128, 16, 32], BF16)
        bat, batb = P.sb("bat", [128, T_ // 128, 32], F32)
        psb_t, psb_b = P.ps("psba", [128, 512])

        def fn(act, actb, half):
            P.dma("pool", lambda e: e.dma_start(out=wba[:], in_=W_d[:, 8192:8224].rearrange("(k p) n -> p k n", p=128)),
                  writes=[wbab], sbuf=wbab)
            for tt in range(T_ // 128):
                for k in range(16):
                    P.op("pe", lambda e, tt=tt, k=k: e.matmul(psb_t[:, 0:32], act[:, k, tt * 128:(tt + 1) * 128], wba[:, k, :],
                                                              start=(k == 0), stop=(k == 15)), reads=[actb, wbab], writes=[psb_b])
                P.op("dve", lambda e, tt=tt: e.tensor_copy(bat[:, tt, :], psb_t[:, 0:32]), reads=[psb_b], writes=[batb])
            P.dma("sp", lambda e: e.dma_start(out=ba_d.rearrange("(n p) x -> p n x", p=128), in_=bat[:]), reads=[batb], sbuf=batb)
        return fn
    return mk


def gu_evac(hid_d):
    def factory(P):
        sgs = [P.sb(f"sg{i}", [128, TB], F32) for i in range(2)]
        hts = [P.sb(f"hid{i}", [128, T_], BF16) for i in range(3)]
        state = {"n": 0}

        def evac(u, tb, pss):
            (psg, psgb), (psu, psub) = pss
            sg, sgb = sgs[state["n"] % 2]
            state["n"] += 1
            ht, hb = hts[u % 3]
            sl = slice(tb * TB, (tb + 1) * TB)
            P.op("act", lambda e: e.activation(sg[:], psg[:], AF.Silu), reads=[psgb], writes=[sgb])
            P.op("dve", lambda e: e.tensor_tensor(ht[:, sl], sg[:], psu[:], ALU.mult), reads=[sgb, psub], writes=[hb])
            if tb == T_ // TB - 1:
                P.dma("sp", lambda e: e.dma_start(out=hid_d[u, :, :], in_=ht[:]), reads=[hb], sbuf=hb)
        return evac
    return factory


def plain_groups(ncols, gw=256):
    return [([(c0, gw)], [[o] for o in range(0, gw, 128)]) for c0 in range(0, ncols, gw)]


def build_ts(layer, do_block, nxt):
    nc = bass.Bass("TRN2", target_bir_lowering=False)
    dt = lambda name, shape, dty, kind: nc.dram_tensor(name, list(shape), dty, kind=kind).ap()
    x_in = dt("x_in", [16, 128, T_], F32, "ExternalInput")
    gains = dt("gains", [128, NG], F32, "ExternalInput")
    h_d = dt("h_scr", [16, 128, T_], BF16, "Internal")
    gcol = lambda l, w: (l * 4 + w) * 16
    x_cur = x_in
    if do_block:
        o_in = dt("o_in", [16, 128, T_], BF16, "ExternalInput")
        w_out = dt("w_out", [D, D], F32, "ExternalInput")
        w_gu = dt("w_gu", [D, 2 * DFF], F32, "ExternalInput")
        w_dn = dt("w_dn", [DFF, D], F32, "ExternalInput")
        f_d = dt("f_scr", [16, 128, T_], F32, "Internal")
        x_mid = dt("x_mid", [16, 128, T_], F32, "Internal")
        x_out = dt("x_out", [16, 128, T_], F32, "ExternalOutput")
        hid_d = dt("hid_scr", [DFF // 128, 128, T_], BF16, "Internal")
        piece_gemm(nc, o_in, 16, T_, w_out, plain_groups(D), store_evac(lambda u, half: f_d[u, :, :], T_, F32))
        piece_norm(nc, x_in, h_d, gains, gcol(layer, 2), T_, f_d=f_d, gcol_post=gcol(layer, 1), x_out_d=x_mid)
        gu_groups = [([(j * 128, 128), (DFF + j * 128, 128)], [[0, 128]]) for j in range(DFF // 128)]
        piece_gemm(nc, h_d, 16, T_, w_gu, gu_groups, gu_evac(hid_d))
        piece_gemm(nc, hid_d, DFF // 128, T_, w_dn, plain_groups(D, 128),
                   store_evac(lambda u, half: f_d[u, :, half * 1024:(half + 1) * 1024], 1024, F32), TR=1024)
        nxt_gcol = gcol(layer + 1, 0) if nxt is not None else None
        piece_norm(nc, x_mid, h_d if nxt is not None else None, gains, nxt_gcol, T_, f_d=f_d, gcol_post=gcol(layer, 3), x_out_d=x_out)
        x_cur = x_out
    else:
        piece_norm(nc, x_in, h_d, gains, gcol(layer, 0), T_)
    if nxt == "gdn":
        w_in = dt("w_in", [D, 8224], F32, "ExternalInput")
        proj = dt("proj", [64, 128, T_], BF16, "ExternalOutput")
        ba = dt("ba", [T_, 32], F32, "ExternalOutput")
        piece_gemm(nc, h_d, 16, T_, w_in, plain_groups(8192), store_evac(lambda u, half: proj[u, :, :], T_, BF16),
                   extra=ba_extra(w_in, ba))
    elif nxt in ("dsa", "dsa_kv"):
        if nxt == "dsa_kv":
            kv_w = dt("kv_w", [D, 12288], F32, "ExternalInput")
            kv = dt("kv", [96, 128, T_], BF16, "ExternalOutput")
            hk_d = dt("hk_scr", [16, 128, T_], BF16, "Internal")
            piece_norm(nc, x_cur, hk_d, gains, 256, T_)
            piece_gemm(nc, hk_d, 16, T_, kv_w, plain_groups(12288), store_evac(lambda u, half: kv[u, :, :], T_, BF16))
        w_q = dt("w_q", [D, 6144], F32, "ExternalInput")
        qo = dt("q_out", [48, 128, T_], BF16, "ExternalOutput")
        piece_gemm(nc, h_d, 16, T_, w_q, plain_groups(6144), store_evac(lambda u, half: qo[u, :, :], T_, BF16))
    return nc


def build_gdn():
    nc = bass.Bass("TRN2", target_bir_lowering=False)
    dt = lambda name, shape, dty, kind: nc.dram_tensor(name, list(shape), dty, kind=kind).ap()
    pj = dt("pj", [4, 2, 128, S], BF16, "ExternalInput")
    ba = dt("ba", [S, 4], F32, "ExternalInput")
    cw = dt("cw", [128, 6, 4], F32, "ExternalInput")
    hp = dt("hp", [128, 4], F32, "ExternalInput")
    on = dt("on", [128, 1], F32, "ExternalInput")
    gc = dt("gc", [128, GC_W], F32, "ExternalInput")
    gs = dt("gs", [2, 128], F32, "ExternalInput")
    out = dt("out", [2, 128, S], BF16, "ExternalOutput")
    stage_gdn_seq(nc, pj, ba, cw, hp, on, gc, gs, out, S)
    return nc


def build_dsa():
    nc = bass.Bass("TRN2", target_bir_lowering=False)
    dt = lambda name, shape, dty, kind: nc.dram_tensor(name, list(shape), dty, kind=kind).ap()
    qd = dt("q", [3, 2, 128, S], BF16, "ExternalInput")
    kd = dt("k", [3, 2, 128, S], BF16, "ExternalInput")
    vd = dt("v", [3, 2, 128, S], BF16, "ExternalInput")
    bd = dt("bias", [3, 2, 128, 256], F32, "ExternalInput")
    idd = dt("ident", [128, 128], BF16, "ExternalInput")
    out = dt("out", [2, 128, S], BF16, "ExternalOutput")
    stage_dsa(nc, qd, kd, vd, bd, idd, out, S)
    return nc


def _run(nc, in_maps):
    res = run_bass_kernel_spmd(nc, in_maps, core_ids=list(range(NCORES)))
    return res.results


def _to_heads(chunks_per_core, sel):
    outs = []
    for j in range(NCORES):
        ids = sel(j)
        outs.append(np.concatenate([chunks_per_core[c][ids] for c in range(NCORES)], axis=2))
    return outs


def _to_tokens(outs_per_core):
    full = np.concatenate(outs_per_core, axis=0)
    return [np.ascontiguousarray(full[:, :, c * T_:(c + 1) * T_]) for c in range(NCORES)]


def kernel(x, norm_gains, ffn_w_gate_up, ffn_w_down, gdn_w_in, gdn_conv_w, gdn_a_log, gdn_dt_bias,
           gdn_out_norm, gdn_w_out, kv_norm, kv_w, dsa_w_q, dsa_w_out, rel_bias):
    f32 = lambda a: np.ascontiguousarray(np.asarray(a, dtype=np.float32))
    x = f32(x)
    xs = [np.ascontiguousarray(x[0, c * T_:(c + 1) * T_, :].T.reshape(16, 128, T_)) for c in range(NCORES)]
    gall = np.concatenate([f32(norm_gains).reshape(16, D), f32(kv_norm)[None]], 0)
    gains = np.ascontiguousarray(gall.reshape(17, 16, 128).transpose(2, 0, 1).reshape(128, NG))
    gcs = gdn_consts()
    identb = np.eye(128, dtype=np.float32).astype(ml_dtypes.bfloat16)
    w_gu = f32(ffn_w_gate_up)
    w_dn = f32(ffn_w_down)
    w_in = f32(gdn_w_in)
    conv_w = f32(gdn_conv_w)
    a_log = f32(gdn_a_log)
    dtb = f32(gdn_dt_bias)
    onorm = f32(gdn_out_norm)
    g_w_out = f32(gdn_w_out)
    kvw = f32(kv_w)
    wq = f32(dsa_w_q)
    d_w_out = f32(dsa_w_out)
    rb = f32(rel_bias)

    def gdn_inputs(l, projs, bas):
        ba_full = np.concatenate(bas, axis=0)
        pjs = _to_heads(projs, lambda j: [kind * 16 + 2 * j + e for kind in range(4) for e in range(2)])
        maps = []
        for j in range(NCORES):
            hs = [2 * j, 2 * j + 1]
            cwn = np.zeros((128, 6, 4), np.float32)
            for kind in range(3):
                for e_, hd in enumerate(hs):
                    cwn[:, kind * 2 + e_, :] = conv_w[l][:, kind * 2048 + hd * 128: kind * 2048 + (hd + 1) * 128].T
            hpn = np.tile(np.array([a_log[l][hs[0]], a_log[l][hs[1]], dtb[l][hs[0]], dtb[l][hs[1]]], np.float32)[None], (128, 1))
            maps.append({"pj": np.ascontiguousarray(pjs[j].reshape(4, 2, 128, S)),
                         "ba": np.ascontiguousarray(ba_full[:, [hs[0], hs[1], 16 + hs[0], 16 + hs[1]]]),
                         "cw": cwn, "hp": hpn, "on": np.ascontiguousarray(onorm[l][:, None]),
                         "gc": gcs["gc"], "gs": gcs["gsel"]})
        return maps

    def dsa_inputs(qs_, kvs):
        qh = _to_heads(qs_, lambda j: [g * 16 + 2 * j + e for g in range(3) for e in range(2)])
        kh = _to_heads(kvs, lambda j: [g * 16 + 2 * j + e for g in range(3) for e in range(2)])
        vh = _to_heads(kvs, lambda j: [48 + g * 16 + 2 * j + e for g in range(3) for e in range(2)])
        return [{"q": np.ascontiguousarray(qh[j].reshape(3, 2, 128, S)), "k": np.ascontiguousarray(kh[j].reshape(3, 2, 128, S)),
                 "v": np.ascontiguousarray(vh[j].reshape(3, 2, 128, S)),
                 "bias": dsa_bias_tiles(rb, [2 * j, 2 * j + 1]), "ident": identb} for j in range(NCORES)]

    r = _run(build_ts(0, False, "gdn"), [{"x_in": xs[c], "gains": gains, "w_in": w_in[0]} for c in range(NCORES)])
    projs, bas = [r[c]["proj"] for c in range(NCORES)], [r[c]["ba"] for c in range(NCORES)]
    nc_gdn = build_gdn()
    nc_dsa = None
    kvs = None
    for l in range(4):
        if l < 2:
            ro = _run(nc_gdn if l == 0 else build_gdn(), gdn_inputs(l, projs, bas))
        else:
            ro = _run(build_dsa(), dsa_inputs(qs_, kvs))
        o_tok = _to_tokens([ro[j]["out"] for j in range(NCORES)])
        nxt = ["gdn", "dsa_kv", "dsa", None][l]
        w_o = g_w_out[l] if l < 2 else d_w_out[l - 2]
        maps = []
        for c in range(NCORES):
            m = {"x_in": xs[c], "gains": gains, "o_in": o_tok[c], "w_out": w_o, "w_gu": w_gu[l], "w_dn": w_dn[l]}
            if nxt == "gdn":
                m["w_in"] = w_in[l + 1]
            if nxt == "dsa_kv":
                m["kv_w"] = kvw
            if nxt in ("dsa", "dsa_kv"):
                m["w_q"] = wq[l - 1]
            maps.append(m)
        r = _run(build_ts(l, True, nxt), maps)
        xs = [r[c]["x_out"] for c in range(NCORES)]
        if nxt == "gdn":
            projs, bas = [r[c]["proj"] for c in range(NCORES)], [r[c]["ba"] for c in range(NCORES)]
        if nxt == "dsa_kv":
            kvs = [r[c]["kv"] for c in range(NCORES)]
        if nxt in ("dsa", "dsa_kv"):
            qs_ = [r[c]["q_out"] for c in range(NCORES)]
    out = np.empty((1, S, D), np.float32)
    for c in range(NCORES):
        out[0, c * T_:(c + 1) * T_, :] = xs[c].reshape(D, T_).T
    return out
```

```python
import contextlib
import os
import numpy as np
import ml_dtypes
import concourse.bass as bass
import concourse.mybir as mybir
from concourse.bass_utils import run_bass_kernel_spmd

F32 = mybir.dt.float32
BF16 = mybir.dt.bfloat16
ALU = mybir.AluOpType
AF = mybir.ActivationFunctionType
AX = mybir.AxisListType

NCORES = 8
D = 2048
S = 16384
TPC = S // NCORES
DFF = 5632
EPS = 1e-6
NEG = -30000.0


class Buf:
    __slots__ = ("name", "last_writer", "readers", "sem", "dma_cnt", "last_dma", "excl")

    def __init__(self, name):
        self.name = name
        self.excl = False
        self.last_writer = None
        self.readers = []
        self.sem = None
        self.dma_cnt = 0
        self.last_dma = None


class Op:
    __slots__ = ("eng", "fn", "deps", "signal", "token", "is_dma", "dbuf", "inc")

    def __init__(self, eng, fn, is_dma=False, dbuf=None):
        self.eng = eng
        self.fn = fn
        self.deps = []
        self.signal = False
        self.token = None
        self.is_dma = is_dma
        self.dbuf = dbuf


class Prog:
    ENGS = ("pe", "act", "dve", "pool", "sp")

    NPROG = [0]

    def __init__(self, nc, stack):
        self.nc = nc
        self.stack = stack
        Prog.NPROG[0] += 1
        self.tag = f"_g{Prog.NPROG[0]}"
        self.ops = []
        self.bufs = []
        self.nbuf = 0

    def buf(self, name=None):
        b = Buf(name or f"b{self.nbuf}")
        self.nbuf += 1
        self.bufs.append(b)
        return b

    def sb(self, name, shape, dt):
        t = self.stack.enter_context(self.nc.sbuf_tensor(name + self.tag, list(shape), dt))
        return t, self.buf(name)

    def ps(self, name, shape, dt=F32):
        t = self.stack.enter_context(self.nc.psum_tensor(name + self.tag, list(shape), dt))
        b = self.buf(name)
        b.excl = True
        return t, b

    def _deps(self, op, reads, writes):
        deps = []
        for r in reads:
            if r.last_writer is not None:
                deps.append(r.last_writer)
        for w in writes:
            if w.last_writer is not None:
                deps.append(w.last_writer)
            deps.extend(w.readers)
        seen = set()
        for d in deps:
            if d is op or id(d) in seen:
                continue
            seen.add(id(d))
            if d.eng == "pe" and op.eng == "pe" and not d.is_dma and not op.is_dma:
                continue
            op.deps.append(d)
            d.signal = True
        for w in writes:
            w.last_writer = op
            w.readers = []
        for r in reads:
            if r not in writes:
                if op.is_dma:
                    r.readers.append(op)
                else:
                    r.readers = [x for x in r.readers if x.is_dma or x.eng != op.eng]
                    r.readers.append(op)

    def op(self, eng, fn, reads=(), writes=()):
        o = Op(eng, fn)
        reads = list(reads)
        writes = list(writes)
        for r in list(reads):
            if r.excl:
                reads.remove(r)
                if r not in writes:
                    writes.append(r)
        self._deps(o, reads, writes)
        self.ops.append(o)
        return o

    def dma(self, queue, fn, reads=(), writes=(), sbuf=None, inc=16):
        o = Op(queue, fn, is_dma=True, dbuf=sbuf)
        o.inc = inc
        self._deps(o, list(reads), list(writes))
        if sbuf.last_dma is not None and sbuf.last_dma not in o.deps:
            o.deps.append(sbuf.last_dma)
        sbuf.last_dma = o
        sbuf.dma_cnt += inc
        o.token = (sbuf, sbuf.dma_cnt)
        o.signal = True
        self.ops.append(o)
        return o

    def dma_parts(self, queue, fns, reads=(), writes=(), sbuf=None):
        first = self.dma(queue, fns[0], reads=reads, writes=writes, sbuf=sbuf)
        last = first
        for fn in fns[1:]:
            o = Op(queue, fn, is_dma=True, dbuf=sbuf)
            o.inc = 16
            o.deps = list(first.deps)
            sbuf.dma_cnt += 16
            o.token = (sbuf, sbuf.dma_cnt)
            o.signal = True
            self.ops.append(o)
            last = o
        for w in writes:
            w.last_writer = last
        sbuf.last_dma = last
        return last

    def barrier(self):
        last = {}
        for o in self.ops:
            if o.is_dma:
                last[("dma", id(o.dbuf))] = o
            else:
                last[o.eng] = o
        for e in self.ENGS:
            o = Op(e, None)
            for k, d in last.items():
                if not d.is_dma and d.eng == e and e == "pe":
                    continue
                o.deps.append(d)
                d.signal = True
            self.ops.append(o)
        for b in self.bufs:
            b.last_writer = None
            b.readers = []

    def emit(self):
        nc = self.nc
        st = self.stack
        esem = {e: st.enter_context(nc.semaphore(f"sem_{e}{self.tag}")) for e in self.ENGS}
        cnt = {e: 0 for e in self.ENGS}
        for o in self.ops:
            if o.is_dma:
                if o.dbuf.sem is None:
                    o.dbuf.sem = st.enter_context(nc.semaphore(f"sd_{o.dbuf.name}{self.tag}"))
                o.token = (o.dbuf.sem, o.token[1])
            elif o.signal and o.fn is not None:
                cnt[o.eng] += 1
                o.token = (esem[o.eng], cnt[o.eng])
        block = st.enter_context(nc.Block())
        per = {e: [o for o in self.ops if o.eng == e] for e in self.ENGS}

        def run(e, eng):
            known = {}
            for o in per[e]:
                need = {}
                for d in o.deps:
                    if d.token is None:
                        continue
                    s, c = d.token
                    if known.get(id(s), 0) >= c:
                        continue
                    if need.get(id(s), (None, 0))[1] < c:
                        need[id(s)] = (s, c)
                for s, c in need.values():
                    eng.wait_ge(s, c)
                    known[id(s)] = c
                if o.fn is None:
                    continue
                ins = o.fn(eng)
                if o.is_dma:
                    ins.then_inc(o.token[0], o.inc)
                elif o.signal:
                    ins.then_inc(o.token[0], 1)

        @block.tensor
        def _(eng):
            run("pe", eng)

        @block.scalar
        def _(eng):
            run("act", eng)

        @block.vector
        def _(eng):
            run("dve", eng)

        @block.gpsimd
        def _(eng):
            run("pool", eng)

        @block.sync
        def _(eng):
            run("sp", eng)


TB = 512


def _rstd_from_sq(P, sqt, sqb, KC, ones, onesb, pss, pssb, rs, rsb, inv_n):
    for c in range(KC):
        P.op("pe", lambda e, c=c: e.matmul(pss[:], ones[:], sqt[:, c, :], start=(c == 0), stop=(c == KC - 1)),
             reads=[onesb, sqb], writes=[pssb])
    P.op("act", lambda e: e.activation(rs[:], pss[:], AF.Sqrt, bias=EPS, scale=inv_n), reads=[pssb], writes=[rsb])
    P.op("dve", lambda e: e.reciprocal(rs[:], rs[:]), reads=[rsb], writes=[rsb])


def piece_norm(nc, x_d, h_d, gains_d, gcol, T, f_d=None, gcol_post=None, x_out_d=None):
    KC = D // 128
    with contextlib.ExitStack() as st:
        P = Prog(nc, st)
        ones, onesb = P.sb("ones", [128, 128], BF16)
        gt, gtb = P.sb("gains", [128, gains_d.shape[1]], F32)
        P.op("pool", lambda e: e.memset(ones[:], 1.0), writes=[onesb])
        P.dma("sp", lambda e: e.dma_start(out=gt[:], in_=gains_d[:, :]), writes=[gtb], sbuf=gtb)
        xts = [P.sb(f"xt{i}", [128, KC, TB], F32) for i in range(2)]
        fts = [P.sb(f"ft{i}", [128, KC, TB], F32) for i in range(2)] if f_d is not None else None
        sqs = [P.sb(f"sq{i}", [128, KC, TB], BF16) for i in range(1)]
        hts = [P.sb(f"ht{i}", [128, KC, TB], BF16) for i in range(2)] if h_d is not None else None
        rss = [P.sb(f"rs{i}", [128, TB], F32) for i in range(2)]
        tmpp, tmppb = P.sb("tmpp", [128, TB], F32)
        psl = [P.ps(f"pss{i}", [128, TB]) for i in range(2)]
        nb = T // TB
        for tb in range(nb):
            sl = slice(tb * TB, (tb + 1) * TB)
            xt, xb = xts[tb % 2]
            sq, sqb = sqs[0]
            P.dma("sp", lambda e, xt=xt, sl=sl: e.dma_start(out=xt[:], in_=x_d[:, :, sl].rearrange("c p t -> p c t")),
                  writes=[xb], sbuf=xb)
            if f_d is not None:
                ft, fb = fts[tb % 2]
                P.dma("sp", lambda e, ft=ft, sl=sl: e.dma_start(out=ft[:], in_=f_d[:, :, sl].rearrange("c p t -> p c t")),
                      writes=[fb], sbuf=fb)
                P.op("act", lambda e, ft=ft, sq=sq: e.activation(sq[:], ft[:], AF.Square), reads=[fb], writes=[sqb])
                rs, rsb = rss[0]
                pss, pssb = psl[0]
                _rstd_from_sq(P, sq, sqb, KC, ones, onesb, pss, pssb, rs, rsb, 1.0 / D)
                for c in range(KC):
                    if False:
                        P.op("pool", lambda e, c=c, ft=ft: e.tensor_scalar_mul(ft[:, c, :], ft[:, c, :], gt[:, gcol_post + c:gcol_post + c + 1]),
                             reads=[fb, gtb], writes=[fb])
                        P.op("pool", lambda e, c=c, ft=ft, rs=rs: e.tensor_tensor(ft[:, c, :], ft[:, c, :], rs[:], ALU.mult),
                             reads=[fb, rsb], writes=[fb])
                        continue
                    P.op("dve", lambda e, c=c, ft=ft, rs=rs: e.scalar_tensor_tensor(
                        ft[:, c, :], ft[:, c, :], gt[:, gcol_post + c:gcol_post + c + 1], rs[:], ALU.mult, ALU.mult),
                        reads=[fb, rsb, gtb], writes=[fb])
                P.op("pool", lambda e, xt=xt, ft=ft: e.tensor_tensor(xt[:], xt[:], ft[:], ALU.add),
                     reads=[xb, fb], writes=[xb])
                P.dma("sp", lambda e, xt=xt, sl=sl: e.dma_start(out=x_out_d[:, :, sl].rearrange("c p t -> p c t"), in_=xt[:]),
                      reads=[xb], sbuf=xb)
            if h_d is not None:
                ht, hb = hts[tb % 2]
                P.op("act", lambda e, xt=xt, sq=sq: e.activation(sq[:], xt[:], AF.Square), reads=[xb], writes=[sqb])
                rs, rsb = rss[1]
                pss, pssb = psl[1]
                _rstd_from_sq(P, sq, sqb, KC, ones, onesb, pss, pssb, rs, rsb, 1.0 / D)
                for c in range(KC):
                    if False:
                        P.op("pool", lambda e, c=c, xt=xt: e.tensor_scalar_mul(tmpp[:], xt[:, c, :], gt[:, gcol + c:gcol + c + 1]),
                             reads=[xb, gtb], writes=[tmppb])
                        P.op("pool", lambda e, c=c, ht=ht, rs=rs: e.tensor_tensor(ht[:, c, :], tmpp[:], rs[:], ALU.mult),
                             reads=[tmppb, rsb], writes=[hb])
                        continue
                    P.op("dve", lambda e, c=c, xt=xt, ht=ht, rs=rs: e.scalar_tensor_tensor(
                        ht[:, c, :], xt[:, c, :], gt[:, gcol + c:gcol + c + 1], rs[:], ALU.mult, ALU.mult),
                        reads=[xb, rsb, gtb], writes=[hb])
                P.dma("sp", lambda e, ht=ht, sl=sl: e.dma_start(out=h_d[:, :, sl].rearrange("c p t -> p c t"), in_=ht[:]),
                      reads=[hb], sbuf=hb)
        P.barrier()
        P.emit()


def piece_gemm(nc, act_d, KC, T, W_d, groups, evac_factory, TR=None, extra=None):
    TR = TR or T
    maxw = max(sum(n for _, n in g[0]) for g in groups)
    with contextlib.ExitStack() as st:
        P = Prog(nc, st)
        act, actb = P.sb("act", [128, KC, TR], BF16)
        wsl = [P.sb(f"w{i}", [128, KC, maxw], BF16) for i in range(3)]
        psl = [P.ps(f"pg{i}", [128, TB]) for i in range(6)]
        evac = evac_factory(P)
        if extra is not None:
            extra_fn = extra(P)
        pi = 0
        gi = 0
        for half in range(T // TR):
            P.dma("sp", lambda e, half=half: e.dma_start(
                out=act[:], in_=act_d[:, :, half * TR:(half + 1) * TR].rearrange("c p t -> p c t")),
                writes=[actb], sbuf=actb)
            if extra is not None:
                extra_fn(act, actb, half)
            ug = 0
            for cols, units in groups:
                w, wb = wsl[gi % 3]
                gi += 1
                off = 0
                fns = []
                for (c0, n) in cols:
                    for k0 in range(0, KC, 4):
                        k1 = min(KC, k0 + 4)
                        fns.append(lambda e, w=w, off=off, c0=c0, n=n, k0=k0, k1=k1: e.dma_start(
                            out=w[:, k0:k1, off:off + n],
                            in_=W_d[k0 * 128:k1 * 128, c0:c0 + n].rearrange("(k p) n -> p k n", p=128)))
                    off += n
                P.dma_parts("pool", fns, writes=[wb], sbuf=wb)
                for unit in units:
                    for tb in range(TR // TB):
                        pss = []
                        for uo in unit:
                            ps, psb = psl[pi % 6]
                            pi += 1
                            for k in range(KC):
                                P.op("pe", lambda e, ps=ps, w=w, uo=uo, k=k, tb=tb: e.matmul(
                                    ps[:], w[:, k, uo:uo + 128], act[:, k, tb * TB:(tb + 1) * TB],
                                    start=(k == 0), stop=(k == KC - 1)), reads=[wb, actb], writes=[psb])
                            pss.append((ps, psb))
                        evac(ug, half * (TR // TB) + tb, pss)
                    ug += 1
        P.barrier()
        P.emit()


def store_evac(out_ap_fn, TR, dt, nslots=3):
    def factory(P):
        tiles = [P.sb(f"ot{i}", [128, TR], dt) for i in range(nslots)]
        state = {"n": 0, "t": -1, "key": None}
        nb = TR // TB

        def evac(u, tb, pss):
            key = (u, tb // nb)
            if key != state["key"]:
                state["key"] = key
                state["t"] += 1
            ot, ob = tiles[state["t"] % nslots]
            ps, psb = pss[0]
            i = state["n"]
            state["n"] += 1
            sl = slice((tb % nb) * TB, (tb % nb + 1) * TB)
            if i % 2 == 0:
                P.op("act", lambda e: e.activation(ot[:, sl], ps[:], AF.Copy), reads=[psb], writes=[ob])
            else:
                P.op("dve", lambda e: e.tensor_copy(ot[:, sl], ps[:]), reads=[psb], writes=[ob])
            if tb % nb == nb - 1:
                dst = out_ap_fn(u, tb // nb)
                P.dma("sp", lambda e: e.dma_start(out=dst, in_=ot[:]), reads=[ob], sbuf=ob)
        return evac
    return factory


SEG = 2048
CH = 64


def gdn_consts():
    idx = np.arange(128)
    same = (idx[:, None] // CH) == (idx[None, :] // CH)
    U = (same & (idx[:, None] <= idx[None, :])).astype(np.float32)
    BM = same.astype(np.float32)
    negm = np.where(same & (idx[None, :] >= idx[:, None]), 0.0, NEG).astype(np.float32)
    strict = (same & (idx[None, :] > idx[:, None])).astype(np.float32)
    ident = np.eye(128, dtype=np.float32)
    cs = (idx % CH == 0).astype(np.float32)[:, None]
    half = np.stack([(idx < CH), (idx >= CH)], 1).astype(np.float32)
    sel = np.zeros((2, 128), np.float32)
    sel[0] = 1.0
    c = np.concatenate([U, BM, negm, strict, ident, cs, half, np.ones((128, 1), np.float32)], 1)
    return {"gc": np.ascontiguousarray(c), "gsel": sel}


GC_U, GC_BM, GC_NEG, GC_STR, GC_ID, GC_CS, GC_HALF, GC_ONE = 0, 128, 256, 384, 512, 640, 641, 643
GC_W = 644


class _Stop(Exception):
    pass


def stage_gdn_seq(nc, pj_d, ba_d, convw_d, hp_d, onorm_d, gc_d, gsel_d, out_d, SL):
    try:
        _stage_gdn_seq(nc, pj_d, ba_d, convw_d, hp_d, onorm_d, gc_d, gsel_d, out_d, SL)
    except _Stop:
        pass


def _stage_gdn_seq(nc, pj_d, ba_d, convw_d, hp_d, onorm_d, gc_d, gsel_d, out_d, SL):
    LVL = float(os.environ.get("GDN_DBG", "99"))

    def chk(P, k):
        if LVL <= k:
            P.barrier()
            P.emit()
            raise _Stop()
    NT = SL // 128
    NSEG = SL // SEG
    with contextlib.ExitStack() as st:
        P = Prog(nc, st)
        gc, gcb = P.sb("gc", [128, GC_W], F32)
        P.dma("sp", lambda e: e.dma_start(out=gc[:], in_=gc_d[:, :]), writes=[gcb], sbuf=gcb)
        gsel, gselb = P.sb("gsel", [2, 128], F32)
        P.dma("sp", lambda e: e.dma_start(out=gsel[:], in_=gsel_d[:, :]), writes=[gselb], sbuf=gselb)
        cw, cwb = P.sb("cw", [128, 6, 4], F32)
        P.dma("sp", lambda e: e.dma_start(out=cw[:], in_=convw_d[:, :, :]), writes=[cwb], sbuf=cwb)
        hp, hpb = P.sb("hp", [128, 4], F32)
        P.dma("sp", lambda e: e.dma_start(out=hp[:], in_=hp_d[:, :]), writes=[hpb], sbuf=hpb)
        onm, onmb = P.sb("onm", [128, 1], F32)
        P.dma("sp", lambda e: e.dma_start(out=onm[:], in_=onorm_d[:, :]), writes=[onmb], sbuf=onmb)
        cbf, cbfb = P.sb("cbf", [128, 3, 128], BF16)
        P.op("dve", lambda e: e.tensor_copy(cbf[:, 0, :], gc[:, GC_ID:GC_ID + 128]), reads=[gcb], writes=[cbfb])
        P.op("dve", lambda e: e.tensor_copy(cbf[:, 1, :], gc[:, GC_STR:GC_STR + 128]), reads=[gcb], writes=[cbfb])
        P.op("dve", lambda e: e.memset(cbf[:, 2, :], 1.0), writes=[cbfb])
        identb = cbf[:, 0, :]
        onesb16 = cbf[:, 2, :]
        dg, dgb = P.sb("dg", [128, 24, 128], BF16)
        for ci_ in range(6):
            for s_ in range(4):
                P.op("dve", lambda e, ci_=ci_, s_=s_: e.tensor_scalar_mul(dg[:, ci_ * 4 + s_, :], gc[:, GC_ID:GC_ID + 128], cw[:, ci_, s_:s_ + 1]),
                     reads=[gcb, cwb], writes=[dgb])
        U = gc[:, GC_U:GC_U + 128]
        pf = [P.ps(f"pf{i}", [128, 512]) for i in range(6)]
        pb = [P.ps(f"pb{i}", [128, 1024], BF16) for i in range(2)]
        ba, bab = P.sb("ba", [128, NT, 4], F32)
        P.dma("sp", lambda e: e.dma_start(out=ba[:], in_=ba_d.rearrange("(n p) x -> p n x", p=128)), writes=[bab], sbuf=bab)
        gt, gtb = P.sb("gt", [128, NT, 2], F32)
        beta, betab = P.sb("beta", [128, NT, 2], F32)
        gcum, gcumb = P.sb("gcum", [128, NT, 2], F32)
        eg, egb = P.sb("eg", [128, NT, 2], F32)
        edec, edecb = P.sb("edec", [128, NT, 2], F32)
        gh, ghb = P.sb("gh", [128, NT, 2, 2], F32)
        glx, glxb = P.sb("glx", [128, NT, 2, 2], F32)
        gm, gmb = P.sb("gm", [128, NT, 2, 4], F32)
        tmp, tmpb = P.sb("gtmp", [128, NT, 2], F32)
        ea, eab = P.sb("ea", [128, 2], F32)
        P.op("act", lambda e: e.activation(beta[:], ba[:, :, 0:2], AF.Exp, scale=-1.0), reads=[bab], writes=[betab])
        P.op("dve", lambda e: e.tensor_scalar(beta[:], beta[:], 1.0, 1.0, ALU.mult, ALU.add), reads=[betab], writes=[betab])
        P.op("dve", lambda e: e.reciprocal(beta[:], beta[:]), reads=[betab], writes=[betab])
        for e_ in range(2):
            P.op("act", lambda e, e_=e_: e.activation(tmp[:, :, e_], ba[:, :, 2 + e_], AF.Exp, bias=hp[:, 2 + e_:3 + e_]),
                 reads=[bab, hpb], writes=[tmpb])
        P.op("act", lambda e: e.activation(tmp[:], tmp[:], AF.Ln, bias=1.0), reads=[tmpb], writes=[tmpb])
        P.op("act", lambda e: e.activation(ea[:], hp[:, 0:2], AF.Exp), reads=[hpb], writes=[eab])
        for e_ in range(2):
            P.op("dve", lambda e, e_=e_: e.tensor_scalar(gt[:, :, e_], tmp[:, :, e_], ea[:, e_:e_ + 1], -1.0, ALU.mult, ALU.mult),
                 reads=[tmpb, eab], writes=[gtb])
        G2 = NT * 2
        def gflat(t):
            return t[:].rearrange("p n e -> p (n e)")
        for c0 in range(0, G2, 512):
            c1 = min(G2, c0 + 512)
            ps, psb = pf[0]
            P.op("pe", lambda e, c0=c0, c1=c1, ps=ps: e.matmul(ps[:, 0:c1 - c0], U, gflat(gt)[:, c0:c1], start=True, stop=True),
                 reads=[gcb, gtb], writes=[psb])
            P.op("dve", lambda e, c0=c0, c1=c1, ps=ps: e.tensor_copy(gflat(gcum)[:, c0:c1], ps[:, 0:c1 - c0]), reads=[psb], writes=[gcumb])
            ps2, ps2b = pf[1]
            P.op("pe", lambda e, c0=c0, c1=c1, ps2=ps2: e.matmul(ps2[:, 0:c1 - c0], gc[:, GC_BM:GC_BM + 128], gflat(gt)[:, c0:c1], start=True, stop=True),
                 reads=[gcb, gtb], writes=[ps2b])
            P.op("dve", lambda e, c0=c0, c1=c1, ps2=ps2: e.tensor_tensor(gflat(edec)[:, c0:c1], ps2[:, 0:c1 - c0], gflat(gcum)[:, c0:c1], ALU.subtract),
                 reads=[ps2b, gcumb], writes=[edecb])
        P.op("act", lambda e: e.activation(edec[:], edec[:], AF.Exp), reads=[edecb], writes=[edecb])
        P.op("act", lambda e: e.activation(eg[:], gcum[:], AF.Exp), reads=[gcumb], writes=[egb])
        for h_ in range(2):
            P.op("dve", lambda e, h_=h_: e.tensor_scalar_mul(gh[:, :, :, h_], gt[:], gc[:, GC_HALF + h_:GC_HALF + h_ + 1]),
                 reads=[gtb, gcb], writes=[ghb])
        G4 = NT * 4
        onesf, onesfb = P.sb("onesf", [128, 128], F32)
        P.op("pool", lambda e: e.memset(onesf[:], 1.0), writes=[onesfb])
        for c0 in range(0, G4, 512):
            c1 = min(G4, c0 + 512)
            ps, psb = pf[2]
            P.op("pe", lambda e, c0=c0, c1=c1, ps=ps: e.matmul(ps[:, 0:c1 - c0], onesf[:], gh[:].rearrange("p n e h -> p (n e h)")[:, c0:c1], start=True, stop=True),
                 reads=[onesfb, ghb], writes=[psb])
            P.op("act", lambda e, c0=c0, c1=c1, ps=ps: e.activation(glx[:].rearrange("p n e h -> p (n e h)")[:, c0:c1], ps[:, 0:c1 - c0], AF.Exp),
                 reads=[psb], writes=[glxb])
        P.op("dve", lambda e: e.tensor_copy(gm[:, :, :, 0], gt[:]), reads=[gtb], writes=[gmb])
        P.op("dve", lambda e: e.tensor_scalar_mul(gm[:, :, :, 3], gt[:], -1.0), reads=[gtb], writes=[gmb])
        for k_ in (1, 2):
            P.op("act", lambda e, k_=k_: e.activation(gm[:, :, :, k_], gt[:], AF.Identity, bias=gc[:, GC_CS:GC_CS + 1], scale=0.0),
                 reads=[gtb, gcb], writes=[gmb])
        Rr, Rrb = P.sb("Rr", [2, 2, SEG], F32)
        Lr, Lrb = P.sb("Lr", [2, 2, SEG], F32)
        chk(P, 1)
        Sf = [P.sb(f"Sf{e_}", [128, 128], F32) for e_ in range(2)]
        Sb = [P.sb(f"Sb{e_}", [128, 128], BF16) for e_ in range(2)]
        for e_ in range(2):
            P.op("pool", lambda e, e_=e_: e.memset(Sf[e_][0][:], 0.0), writes=[Sf[e_][1]])
            P.op("pool", lambda e, e_=e_: e.memset(Sb[e_][0][:], 0.0), writes=[Sb[e_][1]])
        xin = [P.sb(f"xin{i}", [128, 4 + SEG], BF16) for i in range(2)]
        acc, accb = P.sb("acc", [128, SEG], F32)
        ysil, ysilb = P.sb("ysil", [128, SEG], F32)
        sq, sqb = P.sb("sqs", [128, SEG], BF16)
        rn, rnb = P.sb("rn", [128, TB], F32)
        egr, egrb = P.sb("egr", [128, TB], F32)
        segq = [P.sb(f"segq{e_}", [128, SEG], BF16) for e_ in range(2)]
        segqd = [P.sb(f"segqd{e_}", [128, SEG], BF16) for e_ in range(2)]
        segk = [P.sb(f"segk{e_}", [128, SEG], BF16) for e_ in range(2)]
        segv = [P.sb(f"segv{e_}", [128, SEG], BF16) for e_ in range(2)]
        segz = [P.sb(f"segz{e_}", [128, SEG], BF16) for e_ in range(2)]
        sego = [P.sb(f"sego{e_}", [128, SEG], BF16) for e_ in range(2)]
        def mk(name, dt, w=128, n=8, p=128):
            return [P.sb(f"{name}{i}", [p, w], dt) for i in range(n)]
        dT = mk("dT", F32); dTs = mk("dTs", F32); Wt = mk("Wt", BF16); inT = mk("inT", BF16)
        Yp = [mk(f"Yp{k_}_", BF16) for k_ in range(1)]
        YW = [mk(f"YW{k_}_", BF16, w=256) for k_ in range(5)]
        Pm = [mk(f"Pm{k_}_", BF16) for k_ in range(2)]
        kg = mk("kg", BF16); kdec = mk("kdec", BF16); vtok = mk("vtok", BF16); nwT = mk("nwT", BF16)
        vnew = mk("vnew", BF16); otok = mk("otok", F32); onb = mk("onb", BF16)
        ssq = mk("ssq", F32, w=1); junk = mk("junk", F32)
        xi = 0
        pfi = [0]
        pbi = [0]

        def PF():
            i = pfi[0]
            pfi[0] += 1
            t, b = pf[i % 6]
            return t, b

        def PB():
            i = pbi[0]
            pbi[0] += 1
            t, b = pb[i % 2]
            return t, b

        kk = 0
        for sg in range(NSEG):
            t_lo = sg * SEG
            for e_ in range(2):
                for t0 in range(0, SEG // 128, 4):
                    for (dstt, dstb, cofs) in ((Rr, Rrb, 0), (Lr, Lrb, 2)):
                        ps, psb = pf[3 + (kk % 2)]
                        kk += 1
                        for q_ in range(4):
                            P.op("pe", lambda e, ps=ps, e_=e_, t=sg * (SEG // 128) + t0 + q_, q_=q_, cofs=cofs: e.matmul(
                                ps[0:2, q_ * 128:(q_ + 1) * 128], gm[:, t, e_, cofs:cofs + 2], U, start=True, stop=True),
                                reads=[gmb, gcb], writes=[psb])
                        P.op("act", lambda e, ps=ps, dstt=dstt, e_=e_, t0=t0: e.activation(dstt[:, e_, t0 * 128:(t0 + 4) * 128], ps[0:2, :], AF.Copy),
                             reads=[psb], writes=[dstb])
            chk(P, 2)
            for e_ in range(2):
                for kind in range(4):
                    xt, xb = xin[xi % 2]
                    xi += 1
                    P.dma("sp", lambda e, xt=xt, kind=kind, e_=e_, t_lo=t_lo: e.dma_start(out=xt[:, 4:4 + SEG], in_=pj_d[kind, e_, :, t_lo:t_lo + SEG]),
                          writes=[xb], sbuf=xb)
                    if kind == 3:
                        P.op("act", lambda e, xt=xt, e_=e_: e.activation(segz[e_][0][:], xt[:, 4:4 + SEG], AF.Silu), reads=[xb], writes=[segz[e_][1]])
                        continue
                    if sg == 0:
                        P.op("pool", lambda e, xt=xt: e.memset(xt[:, 0:4], 0.0), writes=[xb])
                    else:
                        P.dma("sp", lambda e, xt=xt, kind=kind, e_=e_, t_lo=t_lo: e.dma_start(out=xt[:, 1:4], in_=pj_d[kind, e_, :, t_lo - 3:t_lo]),
                              writes=[xb], sbuf=xb)
                    ci = kind * 2 + e_
                    dst, dstb = (segv[e_] if kind == 2 else (ysil, ysilb))
                    for tb in range(SEG // TB):
                        ps, psb = PF()
                        for s_ in range(4):
                            P.op("pe", lambda e, ps=ps, xt=xt, ci=ci, s_=s_, tb=tb: e.matmul(
                                ps[:], dg[:, ci * 4 + s_, :], xt[:, 1 + s_ + tb * TB:1 + s_ + (tb + 1) * TB],
                                start=(s_ == 0), stop=(s_ == 3)), reads=[dgb, xb], writes=[psb])
                        P.op("act", lambda e, ps=ps, dst=dst, tb=tb: e.activation(dst[:, tb * TB:(tb + 1) * TB], ps[:], AF.Silu),
                             reads=[psb], writes=[dstb])
                    if kind == 2:
                        continue
                    P.op("act", lambda e: e.activation(sq[:], ysil[:], AF.Square), reads=[ysilb], writes=[sqb])
                    for tb in range(SEG // TB):
                        sl = slice(tb * TB, (tb + 1) * TB)
                        ps, psb = PF()
                        P.op("pe", lambda e, ps=ps, sl=sl: e.matmul(ps[:], onesb16, sq[:, sl], start=True, stop=True), reads=[cbfb, sqb], writes=[psb])
                        P.op("act", lambda e, ps=ps: e.activation(rn[:], ps[:], AF.Sqrt, bias=EPS, scale=1.0), reads=[psb], writes=[rnb])
                        P.op("dve", lambda e: e.reciprocal(rn[:], rn[:]), reads=[rnb], writes=[rnb])
                        if kind == 1:
                            P.op("dve", lambda e, sl=sl, e_=e_: e.tensor_tensor(segk[e_][0][:, sl], ysil[:, sl], rn[:], ALU.mult),
                                 reads=[ysilb, rnb], writes=[segk[e_][1]])
                        else:
                            P.op("dve", lambda e, sl=sl, e_=e_: e.scalar_tensor_tensor(segq[e_][0][:, sl], ysil[:, sl], float(128 ** -0.5), rn[:], ALU.mult, ALU.mult),
                                 reads=[ysilb, rnb], writes=[segq[e_][1]])
                            ps2, ps2b = PF()
                            P.op("pe", lambda e, ps2=ps2, sl=sl, e_=e_, t_lo=t_lo: e.matmul(ps2[:], gsel[:], Rr[:, e_, sl], start=True, stop=True),
                                 reads=[gselb, Rrb], writes=[ps2b])
                            P.op("act", lambda e, ps2=ps2: e.activation(egr[:], ps2[:], AF.Exp), reads=[ps2b], writes=[egrb])
                            P.op("dve", lambda e, sl=sl, e_=e_: e.tensor_tensor(segqd[e_][0][:, sl], segq[e_][0][:, sl], egr[:], ALU.mult),
                                 reads=[segq[e_][1], egrb], writes=[segqd[e_][1]])
            chk(P, 3)
            def unit(tg, cs_, e_, w):
                gs_ = cs_
                ui = w % 4
                PF = lambda: pf[ui]
                PB = lambda: pb[ui % 2]
                pTo = (ui // 2) * 384
                qT = segq[e_][0][:, cs_]; qdT = segqd[e_][0][:, cs_]; kT = segk[e_][0][:, cs_]; vT = segv[e_][0][:, cs_]
                rb = [segq[e_][1], segqd[e_][1], segk[e_][1], segv[e_][1]]
                psA, psAb = PF()
                P.op("pe", lambda e, psA=psA, kT=kT: e.matmul(psA[:, 0:128], kT, kT, start=True, stop=True), reads=[rb[2]], writes=[psAb])
                P.op("pe", lambda e, psA=psA, kT=kT, qT=qT: e.matmul(psA[:, 128:256], kT, qT, start=True, stop=True), reads=[rb[2], rb[0]], writes=[psAb])
                P.op("pe", lambda e, psA=psA, e_=e_, gs_=gs_: e.matmul(psA[:, 256:384], Lr[:, e_, gs_], Rr[:, e_, gs_], start=True, stop=False),
                     reads=[Lrb, Rrb], writes=[psAb])
                P.op("pe", lambda e, psA=psA: e.matmul(psA[:, 256:384], gc[:, GC_ID:GC_ID + 128], gc[:, GC_NEG:GC_NEG + 128], start=False, stop=True),
                     reads=[gcb], writes=[psAb])
                P.op("act", lambda e, psA=psA, w=w: e.activation(dT[w][0][:], psA[:, 256:384], AF.Exp), reads=[psAb], writes=[dT[w][1]])
                yield None
                P.op("pool", lambda e, w=w: e.tensor_tensor(dTs[w][0][:], dT[w][0][:], gc[:, GC_STR:GC_STR + 128], ALU.mult),
                     reads=[dT[w][1], gcb], writes=[dTs[w][1]])
                yield None
                P.op("dve", lambda e, psA=psA, w=w, tg=tg, e_=e_: e.scalar_tensor_tensor(Wt[w][0][:], psA[:, 0:128], beta[:, tg, e_:e_ + 1], dTs[w][0][:], ALU.mult, ALU.mult),
                     reads=[psAb, betab, dTs[w][1]], writes=[Wt[w][1]])
                yield None
                P.op("dve", lambda e, psA=psA, w=w: e.tensor_tensor(inT[w][0][:], psA[:, 128:256], dT[w][0][:], ALU.mult),
                     reads=[psAb, dT[w][1]], writes=[inT[w][1]])
                yield None
                pT, pTb = PB()
                P.op("pe", lambda e, pT=pT, w=w: e.transpose(pT[:, pTo:pTo + 128], Wt[w][0][:], identb), reads=[Wt[w][1], cbfb], writes=[pTb])
                P.op("act", lambda e, pT=pT, w=w: e.activation(Yp[0][w][0][:], pT[:, pTo:pTo + 128], AF.Copy), reads=[pTb], writes=[Yp[0][w][1]])
                yield None
                P.op("pe", lambda e, pT=pT, kT=kT: e.transpose(pT[:, pTo + 128:pTo + 256], kT, identb), reads=[rb[2], cbfb], writes=[pTb])
                P.op("pe", lambda e, pT=pT, vT=vT: e.transpose(pT[:, pTo + 256:pTo + 384], vT, identb), reads=[rb[3], cbfb], writes=[pTb])
                P.op("act", lambda e, pT=pT, w=w, tg=tg, e_=e_: e.activation(kg[w][0][:], pT[:, pTo + 128:pTo + 256], AF.Copy, scale=eg[:, tg, e_:e_ + 1]),
                     reads=[pTb, egb], writes=[kg[w][1]])
                yield None
                P.op("act", lambda e, pT=pT, w=w, tg=tg, e_=e_: e.activation(kdec[w][0][:], pT[:, pTo + 128:pTo + 256], AF.Copy, scale=edec[:, tg, e_:e_ + 1]),
                     reads=[pTb, edecb], writes=[kdec[w][1]])
                yield None
                P.op("dve", lambda e, pT=pT, w=w: e.tensor_copy(vtok[w][0][:], pT[:, pTo + 256:pTo + 384]), reads=[pTb], writes=[vtok[w][1]])
                yield None
                Yc_ap, Yc_b = Yp[0][w][0][:], Yp[0][w][1]
                Wc_ap, Wc_b = Wt[w][0][:], Wt[w][1]
                for k_ in range(5):
                    ps, psb = PF()
                    P.op("pe", lambda e, ps=ps, Wc=Wc_ap, Yc=Yc_ap: e.matmul(ps[:, 0:128], Wc, Yc, start=True, stop=True),
                         reads=[Wc_b, Yc_b], writes=[psb])
                    if k_ < 4:
                        P.op("pe", lambda e, ps=ps, Wc=Wc_ap, Yc=Yc_ap: e.matmul(ps[:, 128:256], Yc, Wc, start=True, stop=True),
                             reads=[Wc_b, Yc_b], writes=[psb])
                    YWn, YWnb = YW[k_][w]
                    ncol = 256 if k_ < 4 else 128
                    if k_ % 2 == 0:
                        P.op("act", lambda e, ps=ps, YWn=YWn, ncol=ncol: e.activation(YWn[:, 0:ncol], ps[:, 0:ncol], AF.Copy), reads=[psb], writes=[YWnb])
                    else:
                        P.op("dve", lambda e, ps=ps, YWn=YWn, ncol=ncol: e.tensor_copy(YWn[:, 0:ncol], ps[:, 0:ncol]), reads=[psb], writes=[YWnb])
                    yield None
                    Yc_ap, Yc_b = YWn[:, 0:128], YWnb
                    Wc_ap, Wc_b = YWn[:, 128:256], YWnb
                Pc = Pm[0][w]
                P.op("pool", lambda e, Pc=Pc, w=w: e.tensor_tensor(Pc[0][:], identb, Wt[w][0][:], ALU.subtract), reads=[cbfb, Wt[w][1]], writes=[Pc[1]])
                yield None
                for k_ in range(1, 6):
                    ps, psb = PF()
                    Yk = YW[k_ - 1][w]
                    P.op("pe", lambda e, ps=ps, Yk=Yk, Pc=Pc: e.matmul(ps[:, 0:128], Yk[0][:, 0:128], Pc[0][:], start=True, stop=True),
                         reads=[Yk[1], Pc[1]], writes=[psb])
                    Pn = Pm[k_ % 2][w]
                    P.op("dve", lambda e, ps=ps, Pn=Pn, Pc=Pc: e.tensor_tensor(Pn[0][:], ps[:, 0:128], Pc[0][:], ALU.add), reads=[psb, Pc[1]], writes=[Pn[1]])
                    yield None
                    Pc = Pn
                TT = Pc
                ps, psb = PF()
                P.op("pe", lambda e, ps=ps, w=w, TT=TT: e.matmul(ps[:, 0:128], kg[w][0][:], TT[0][:], start=True, stop=True),
                     reads=[kg[w][1], TT[1]], writes=[psb])
                P.op("act", lambda e, ps=ps, w=w: e.activation(nwT[w][0][:], ps[:, 0:128], AF.Copy, scale=-1.0), reads=[psb], writes=[nwT[w][1]])
                yield None
                yield "P2"
                PF = lambda: pf[4 + e_]
                S_f, S_fb = Sf[e_]
                S_b, S_bb = Sb[e_]
                for h_ in range(2):
                    r = slice(h_ * CH, (h_ + 1) * CH)
                    ps, psb = PF()
                    P.op("pe", lambda e, ps=ps, TT=TT, w=w, r=r: e.matmul(ps[r, 0:128], TT[0][:, r], vtok[w][0][:], start=True, stop=False),
                         reads=[TT[1], vtok[w][1]], writes=[psb])
                    P.op("pe", lambda e, ps=ps, w=w, r=r, S_b=S_b: e.matmul(ps[r, 0:128], nwT[w][0][:, r], S_b[:], start=False, stop=True),
                         reads=[nwT[w][1], S_bb], writes=[psb])
                    P.op("act", lambda e, ps=ps, w=w, r=r, tg=tg, e_=e_: e.activation(vnew[w][0][r, :], ps[r, 0:128], AF.Copy, scale=beta[r, tg, e_:e_ + 1]),
                         reads=[psb, betab], writes=[vnew[w][1]])
                    yield None
                    P.op("pe", lambda e, ps=ps, r=r, qdT=qdT, S_b=S_b: e.matmul(ps[r, 128:256], qdT[:, r], S_b[:], start=True, stop=False),
                         reads=[rb[1], S_bb], writes=[psb])
                    P.op("pe", lambda e, ps=ps, r=r, w=w: e.matmul(ps[r, 128:256], inT[w][0][r, r], vnew[w][0][r, :], start=False, stop=True),
                         reads=[inT[w][1], vnew[w][1]], writes=[psb])
                    P.op("pe", lambda e, ps=ps, r=r, w=w: e.matmul(ps[:, 256:384], kdec[w][0][r, :], vnew[w][0][r, :], start=True, stop=True),
                         reads=[kdec[w][1], vnew[w][1]], writes=[psb])
                    P.op("act", lambda e, ps=ps, r=r, w=w: e.activation(otok[w][0][r, :], ps[r, 128:256], AF.Copy), reads=[psb], writes=[otok[w][1]])
                    yield None
                    gcol = glx[:, tg, e_, h_:h_ + 1]
                    P.op("dve", lambda e, ps=ps, S_b=S_b, S_f=S_f, gcol=gcol: e.scalar_tensor_tensor(S_b[:], S_f[:], gcol, ps[:, 256:384], ALU.mult, ALU.add),
                         reads=[psb, S_fb, glxb], writes=[S_bb])
                    yield None
                    P.op("dve", lambda e, ps=ps, S_f=S_f, gcol=gcol: e.scalar_tensor_tensor(S_f[:], S_f[:], gcol, ps[:, 256:384], ALU.mult, ALU.add),
                         reads=[psb, S_fb, glxb], writes=[S_fb])
                    yield None
                P.op("act", lambda e, w=w: e.activation(junk[w][0][:], otok[w][0][:], AF.Square, accum_out=ssq[w][0][:]), reads=[otok[w][1]], writes=[junk[w][1], ssq[w][1]])
                yield None
                P.op("act", lambda e, w=w: e.activation(ssq[w][0][:], ssq[w][0][:], AF.Sqrt, bias=EPS, scale=1.0 / 128), reads=[ssq[w][1]], writes=[ssq[w][1]])
                yield None
                P.op("dve", lambda e, w=w: e.reciprocal(ssq[w][0][:], ssq[w][0][:]), reads=[ssq[w][1]], writes=[ssq[w][1]])
                yield None
                P.op("dve", lambda e, w=w: e.tensor_scalar_mul(onb[w][0][:], otok[w][0][:], ssq[w][0][:, 0:1]), reads=[otok[w][1], ssq[w][1]], writes=[onb[w][1]])
                yield None
                pT2, pT2b = pb[e_]
                P.op("pe", lambda e, pT2=pT2, w=w: e.transpose(pT2[:, 768:896], onb[w][0][:], identb), reads=[onb[w][1], cbfb], writes=[pT2b])
                P.op("dve", lambda e, pT2=pT2, e_=e_, cs_=cs_: e.scalar_tensor_tensor(sego[e_][0][:, cs_], pT2[:, 768:896], onm[:, 0:1], segz[e_][0][:, cs_], ALU.mult, ALU.mult),
                     reads=[pT2b, onmb, segz[e_][1]], writes=[sego[e_][1]])
                yield None

            NU = 4
            ulist = []
            for tl in range(SEG // 128):
                for e_ in range(2):
                    ulist.append((sg * (SEG // 128) + tl, slice(tl * 128, (tl + 1) * 128), e_))
            prev_ch = []
            for gi_ in range(0, len(ulist) + NU, NU):
                grp = ulist[gi_:gi_ + NU]
                p1 = []
                for ui_, (tg_, cs2_, e2_) in enumerate(grp):
                    gen = unit(tg_, cs2_, e2_, ((gi_ // NU) % 2) * NU + ui_)
                    p1.append((gen, e2_))
                parked = []
                chains = prev_ch
                while p1 or any(chains):
                    for item in list(p1):
                        if next(item[0]) == "P2":
                            p1.remove(item)
                            parked.append(item)
                    for ch in chains:
                        if ch:
                            try:
                                next(ch[0][0])
                            except StopIteration:
                                ch.pop(0)
                prev_ch = [[it for it in parked if it[1] == 0], [it for it in parked if it[1] == 1]]
            for e_ in range(2):
                P.dma("sp", lambda e, e_=e_, t_lo=t_lo: e.dma_start(out=out_d[e_, :, t_lo:t_lo + SEG], in_=sego[e_][0][:]), reads=[sego[e_][1]], sbuf=sego[e_][1])
        P.barrier()
        P.emit()


DILS = (1, 4, 16)


def t5_bucket_np(dist):
    max_exact = 16
    d_f = np.maximum(dist, 1).astype(np.float32)
    large = max_exact + (np.log(d_f / np.float32(max_exact)) / np.float32(np.log(2048 / max_exact))
                         * np.float32(32 - max_exact)).astype(np.int32)
    large = np.minimum(large, 31)
    return np.where(dist < max_exact, dist, large)


def dsa_bias_tiles(rel_bias, heads):
    k = np.arange(128)[:, None]
    j = np.arange(256)[None, :]
    rel = np.where(j < 128, j + 128 - k, j - 128 - k)
    valid = np.where(j < 128, j <= k, (j - 128) >= k)
    out = np.zeros((3, len(heads), 128, 256), np.float32)
    for g, d in enumerate(DILS):
        bucket = t5_bucket_np(np.clip(rel, 0, 128) * d)
        for e_, h in enumerate(heads):
            vals = rel_bias[:, g * 16 + h][bucket]
            out[g, e_] = np.where(valid, vals, np.float32(NEG))
    return out


def stage_dsa(nc, q_d, k_d, v_d, bias_d, ident_d, out_d, SL):
    NSEG = SL // SEG
    scale = float(128 ** -0.5)
    with contextlib.ExitStack() as st:
        P = Prog(nc, st)
        idb, idbb = P.sb("idb", [128, 128], BF16)
        P.dma("sp", lambda e: e.dma_start(out=idb[:], in_=ident_d[:, :]), writes=[idbb], sbuf=idbb)
        ones, onesb = P.sb("ones", [128, 128], BF16)
        P.op("pool", lambda e: e.memset(ones[:], 1.0), writes=[onesb])
        bias, biasb = P.sb("bias", [128, 6, 256], F32)
        P.dma("sp", lambda e: e.dma_start(out=bias[:], in_=bias_d.rearrange("g e k j -> k (g e) j")), writes=[biasb], sbuf=biasb)
        bias16, bias16b = P.sb("bias16", [128, 6, 256], BF16)
        P.op("dve", lambda e: e.tensor_scalar_mul(bias16[:], bias[:], 1.0 / scale), reads=[biasb], writes=[bias16b])
        qs = [P.sb(f"qs{i}", [128, SEG], BF16) for i in range(2)]
        kw = [P.sb(f"kw{i}", [128, 2 * SEG], BF16) for i in range(2)]
        vw = [P.sb(f"vw{i}", [128, 2 * SEG], BF16) for i in range(2)]
        vt = [P.sb(f"vt{i}", [128, 32, 128], BF16) for i in range(2)]
        acc, accb = P.sb("acc", [128, 2, SEG], F32)
        rec, recb = P.sb("rec", [128, SEG], F32)
        ot = [P.sb(f"ot{i}", [128, SEG], BF16) for i in range(2)]
        NSLOT = 5
        tmp = [P.sb(f"tmp{i}", [128, 256], F32) for i in range(NSLOT)]
        pt = [P.sb(f"pt{i}", [128, 256], BF16) for i in range(NSLOT)]
        pf = [P.ps(f"pf{i}", [128, 512]) for i in range(6)]
        pb = [P.ps(f"pb{i}", [128, 1024], BF16) for i in range(2)]
        it = 0
        nblk = 0
        npf = 0
        npb = 0
        for sg in range(NSEG):
            t_lo = sg * SEG
            for e_ in range(2):
                for g, d in enumerate(DILS):
                    q, qb = qs[it % 2]
                    kk, kb = kw[it % 2]
                    vv, vb = vw[it % 2]
                    vtt, vtb = vt[it % 2]
                    it += 1
                    P.dma("sp", lambda e, q=q, g=g, e_=e_, t_lo=t_lo: e.dma_start(out=q[:], in_=q_d[g, e_, :, t_lo:t_lo + SEG]), writes=[qb], sbuf=qb)
                    w0 = 0 if sg > 0 else SEG
                    P.dma("sp", lambda e, kk=kk, g=g, e_=e_, t_lo=t_lo, w0=w0: e.dma_start(out=kk[:, w0:2 * SEG], in_=k_d[g, e_, :, t_lo - SEG + w0:t_lo + SEG]), writes=[kb], sbuf=kb)
                    P.dma("sp", lambda e, vv=vv, g=g, e_=e_, t_lo=t_lo, w0=w0: e.dma_start(out=vv[:, w0:2 * SEG], in_=v_d[g, e_, :, t_lo - SEG + w0:t_lo + SEG]), writes=[vb], sbuf=vb)
                    nper = SEG // (128 * d)
                    def kcols(r, n_):
                        start = SEG + 128 * n_ * d + r
                        return slice(start, start + 127 * d + 1, d)
                    slots = []
                    for r in range(d):
                        for n_ in range(-1, nper):
                            if n_ == -1 and sg == 0:
                                continue
                            slots.append((r, n_))
                    for s0 in range(0, len(slots), 8):
                        grp = slots[s0:s0 + 8]
                        pT, pTb = pb[npb % 2]
                        npb += 1
                        for i_, (r, n_) in enumerate(grp):
                            P.op("pe", lambda e, pT=pT, i_=i_, vv=vv, c=kcols(r, n_): e.transpose(pT[:, i_ * 128:(i_ + 1) * 128], vv[:, c], idb[:]),
                                 reads=[vb, idbb], writes=[pTb])
                        n_g = len(grp)
                        P.op("act", lambda e, pT=pT, vtt=vtt, s0=s0, n_g=n_g: e.activation(
                            vtt[:, s0:s0 + n_g, :], pT[:, 0:n_g * 128].rearrange("p (s c) -> p s c", c=128), AF.Copy),
                            reads=[pTb], writes=[vtb])
                    sidx = {s: i for i, s in enumerate(slots)}

                    def blk(slot, r, n_, kk=kk, kb=kb, q=q, qb=qb, vtt=vtt, vtb=vtb, g=g, e_=e_, d=d, sidx=sidx):
                        has_prev = (r, n_ - 1) in sidx
                        qc = slice(128 * n_ * d + r, 128 * n_ * d + r + 127 * d + 1, d)
                        ps, psb = pf[slot]
                        lo = 0 if has_prev else 128
                        def kc(n2):
                            start = SEG + 128 * n2 * d + r
                            return slice(start, start + 127 * d + 1, d)
                        if has_prev:
                            P.op("pe", lambda e, c=kc(n_ - 1): e.matmul(ps[:, 0:128], kk[:, c], q[:, qc], start=True, stop=False),
                                 reads=[kb, qb], writes=[psb])
                        P.op("pe", lambda e, c=kc(n_): e.matmul(ps[:, 128:256], kk[:, c], q[:, qc], start=(not has_prev), stop=False),
                             reads=[kb, qb], writes=[psb])
                        P.op("pe", lambda e: e.matmul(ps[:, lo:256], idb[:], bias16[:, g * 2 + e_, lo:256], start=False, stop=True),
                             reads=[idbb, bias16b], writes=[psb])
                        yield None
                        pp, ppb = pt[slot]
                        P.op("act", lambda e: e.activation(pp[:, lo:256], ps[:, lo:256], AF.Exp, scale=scale), reads=[psb], writes=[ppb])
                        yield None
                        halves = ([(n_ - 1, 0)] if has_prev else []) + [(n_, 128)]
                        for hi, (nn, c0) in enumerate(halves):
                            si = sidx[(r, nn)]
                            P.op("pe", lambda e, si=si, c0=c0, hi=hi, nh=len(halves): e.matmul(
                                ps[:, 256:384], vtt[:, si, :], pp[:, c0:c0 + 128], start=(hi == 0), stop=(hi == nh - 1)),
                                reads=[vtb, ppb], writes=[psb])
                        for hi, (nn, c0) in enumerate(halves):
                            P.op("pe", lambda e, c0=c0, hi=hi, nh=len(halves): e.matmul(
                                ps[:, 384:512], ones[:], pp[:, c0:c0 + 128], start=(hi == 0), stop=(hi == nh - 1)),
                                reads=[onesb, ppb], writes=[psb])
                        src = lambda: ps[:, 256:512].rearrange("p (a c) -> p a c", a=2)
                        if g == 0:
                            P.op("act", lambda e: e.activation(acc[:, :, qc], src(), AF.Copy), reads=[psb], writes=[accb])
                        else:
                            P.op("dve", lambda e: e.tensor_tensor(acc[:, :, qc], acc[:, :, qc], src(), ALU.add), reads=[psb, accb], writes=[accb])
                        yield None

                    todo = [(r, n_) for r in range(d) for n_ in range(nper)]
                    active = []
                    free = list(range(NSLOT))
                    ti = 0
                    while ti < len(todo) or active:
                        if ti < len(todo) and free:
                            sl_ = free.pop(0)
                            active.append((blk(sl_, *todo[ti]), sl_))
                            ti += 1
                        for item in list(active):
                            try:
                                next(item[0])
                            except StopIteration:
                                active.remove(item)
                                free.append(item[1])
                o, ob = ot[(sg * 2 + e_) % 2]
                P.op("dve", lambda e: e.reciprocal(rec[:], acc[:, 1, :]), reads=[accb], writes=[recb])
                P.op("dve", lambda e, o=o: e.tensor_tensor(o[:], acc[:, 0, :], rec[:], ALU.mult), reads=[accb, recb], writes=[ob])
                P.dma("sp", lambda e, o=o, e_=e_, t_lo=t_lo: e.dma_start(out=out_d[e_, :, t_lo:t_lo + SEG], in_=o[:]), reads=[ob], sbuf=ob)
        P.barrier()
        P.emit()


T_ = TPC
NG = 4 * 4 * 16 + 16


def ba_extra(W_d, ba_d):
    def mk(P):
        wba, wbab = P.sb("wba", [128, 16, 32], BF16)
        bat, batb = P.sb("bat", [128, T_ // 128, 32], F32)
        psb_t, psb_b = P.ps("psba", [128, 512])

        def fn(act, actb, half):
            P.dma("pool", lambda e: e.dma_start(out=wba[:], in_=W_d[:, 8192:8224].rearrange("(k p) n -> p k n", p=128)),
                  writes=[wbab], sbuf=wbab)
            for tt in range(T_ // 128):
                for k in range(16):
                    P.op("pe", lambda e, tt=tt, k=k: e.matmul(psb_t[:, 0:32], act[:, k, tt * 128:(tt + 1) * 128], wba[:, k, :],
                                                              start=(k == 0), stop=(k == 15)), reads=[actb, wbab], writes=[psb_b])
                P.op("dve", lambda e, tt=tt: e.tensor_copy(bat[:, tt, :], psb_t[:, 0:32]), reads=[psb_b], writes=[batb])
            P.dma("sp", lambda e: e.dma_start(out=ba_d.rearrange("(n p) x -> p n x", p=128), in_=bat[:]), reads=[batb], sbuf=batb)
        return fn
    return mk


def gu_evac(hid_d):
    def factory(P):
        sgs = [P.sb(f"sg{i}", [128, TB], F32) for i in range(2)]
        hts = [P.sb(f"hid{i}", [128, T_], BF16) for i in range(3)]
        state = {"n": 0}

        def evac(u, tb, pss):
            (psg, psgb), (psu, psub) = pss
            sg, sgb = sgs[state["n"] % 2]
            state["n"] += 1
            ht, hb = hts[u % 3]
            sl = slice(tb * TB, (tb + 1) * TB)
            P.op("act", lambda e: e.activation(sg[:], psg[:], AF.Silu), reads=[psgb], writes=[sgb])
            P.op("dve", lambda e: e.tensor_tensor(ht[:, sl], sg[:], psu[:], ALU.mult), reads=[sgb, psub], writes=[hb])
            if tb == T_ // TB - 1:
                P.dma("sp", lambda e: e.dma_start(out=hid_d[u, :, :], in_=ht[:]), reads=[hb], sbuf=hb)
        return evac
    return factory


def plain_groups(ncols, gw=256):
    return [([(c0, gw)], [[o] for o in range(0, gw, 128)]) for c0 in range(0, ncols, gw)]


def build_ts(layer, do_block, nxt):
    nc = bass.Bass("TRN2", target_bir_lowering=False)
    dt = lambda name, shape, dty, kind: nc.dram_tensor(name, list(shape), dty, kind=kind).ap()
    x_in = dt("x_in", [16, 128, T_], F32, "ExternalInput")
    gains = dt("gains", [128, NG], F32, "ExternalInput")
    h_d = dt("h_scr", [16, 128, T_], BF16, "Internal")
    gcol = lambda l, w: (l * 4 + w) * 16
    x_cur = x_in
    if do_block:
        o_in = dt("o_in", [16, 128, T_], BF16, "ExternalInput")
        w_out = dt("w_out", [D, D], F32, "ExternalInput")
        w_gu = dt("w_gu", [D, 2 * DFF], F32, "ExternalInput")
        w_dn = dt("w_dn", [DFF, D], F32, "ExternalInput")
        f_d = dt("f_scr", [16, 128, T_], F32, "Internal")
        x_mid = dt("x_mid", [16, 128, T_], F32, "Internal")
        x_out = dt("x_out", [16, 128, T_], F32, "ExternalOutput")
        hid_d = dt("hid_scr", [DFF // 128, 128, T_], BF16, "Internal")
        piece_gemm(nc, o_in, 16, T_, w_out, plain_groups(D), store_evac(lambda u, half: f_d[u, :, :], T_, F32))
        piece_norm(nc, x_in, h_d, gains, gcol(layer, 2), T_, f_d=f_d, gcol_post=gcol(layer, 1), x_out_d=x_mid)
        gu_groups = [([(j * 128, 128), (DFF + j * 128, 128)], [[0, 128]]) for j in range(DFF // 128)]
        piece_gemm(nc, h_d, 16, T_, w_gu, gu_groups, gu_evac(hid_d))
        piece_gemm(nc, hid_d, DFF // 128, T_, w_dn, plain_groups(D, 128),
                   store_evac(lambda u, half: f_d[u, :, half * 1024:(half + 1) * 1024], 1024, F32), TR=1024)
        nxt_gcol = gcol(layer + 1, 0) if nxt is not None else None
        piece_norm(nc, x_mid, h_d if nxt is not None else None, gains, nxt_gcol, T_, f_d=f_d, gcol_post=gcol(layer, 3), x_out_d=x_out)
        x_cur = x_out
    else:
        piece_norm(nc, x_in, h_d, gains, gcol(layer, 0), T_)
    if nxt == "gdn":
        w_in = dt("w_in", [D, 8224], F32, "ExternalInput")
        proj = dt("proj", [64, 128, T_], BF16, "ExternalOutput")
        ba = dt("ba", [T_, 32], F32, "ExternalOutput")
        piece_gemm(nc, h_d, 16, T_, w_in, plain_groups(8192), store_evac(lambda u, half: proj[u, :, :], T_, BF16),
                   extra=ba_extra(w_in, ba))
    elif nxt in ("dsa", "dsa_kv"):
        if nxt == "dsa_kv":
            kv_w = dt("kv_w", [D, 12288], F32, "ExternalInput")
            kv = dt("kv", [96, 128, T_], BF16, "ExternalOutput")
            hk_d = dt("hk_scr", [16, 128, T_], BF16, "Internal")
            piece_norm(nc, x_cur, hk_d, gains, 256, T_)
            piece_gemm(nc, hk_d, 16, T_, kv_w, plain_groups(12288), store_evac(lambda u, half: kv[u, :, :], T_, BF16))
        w_q = dt("w_q", [D, 6144], F32, "ExternalInput")
        qo = dt("q_out", [48, 128, T_], BF16, "ExternalOutput")
        piece_gemm(nc, h_d, 16, T_, w_q, plain_groups(6144), store_evac(lambda u, half: qo[u, :, :], T_, BF16))
    return nc


def build_gdn():
    nc = bass.Bass("TRN2", target_bir_lowering=False)
    dt = lambda name, shape, dty, kind: nc.dram_tensor(name, list(shape), dty, kind=kind).ap()
    pj = dt("pj", [4, 2, 128, S], BF16, "ExternalInput")
    ba = dt("ba", [S, 4], F32, "ExternalInput")
    cw = dt("cw", [128, 6, 4], F32, "ExternalInput")
    hp = dt("hp", [128, 4], F32, "ExternalInput")
    on = dt("on", [128, 1], F32, "ExternalInput")
    gc = dt("gc", [128, GC_W], F32, "ExternalInput")
    gs = dt("gs", [2, 128], F32, "ExternalInput")
    out = dt("out", [2, 128, S], BF16, "ExternalOutput")
    stage_gdn_seq(nc, pj, ba, cw, hp, on, gc, gs, out, S)
    return nc


def build_dsa():
    nc = bass.Bass("TRN2", target_bir_lowering=False)
    dt = lambda name, shape, dty, kind: nc.dram_tensor(name, list(shape), dty, kind=kind).ap()
    qd = dt("q", [3, 2, 128, S], BF16, "ExternalInput")
    kd = dt("k", [3, 2, 128, S], BF16, "ExternalInput")
    vd = dt("v", [3, 2, 128, S], BF16, "ExternalInput")
    bd = dt("bias", [3, 2, 128, 256], F32, "ExternalInput")
    idd = dt("ident", [128, 128], BF16, "ExternalInput")
    out = dt("out", [2, 128, S], BF16, "ExternalOutput")
    stage_dsa(nc, qd, kd, vd, bd, idd, out, S)
    return nc


def _run(nc, in_maps):
    res = run_bass_kernel_spmd(nc, in_maps, core_ids=list(range(NCORES)))
    return res.results


def _to_heads(chunks_per_core, sel):
    outs = []
    for j in range(NCORES):
        ids = sel(j)
        outs.append(np.concatenate([chunks_per_core[c][ids] for c in range(NCORES)], axis=2))
    return outs


def _to_tokens(outs_per_core):
    full = np.concatenate(outs_per_core, axis=0)
    return [np.ascontiguousarray(full[:, :, c * T_:(c + 1) * T_]) for c in range(NCORES)]


def kernel(x, norm_gains, ffn_w_gate_up, ffn_w_down, gdn_w_in, gdn_conv_w, gdn_a_log, gdn_dt_bias,
           gdn_out_norm, gdn_w_out, kv_norm, kv_w, dsa_w_q, dsa_w_out, rel_bias):
    f32 = lambda a: np.ascontiguousarray(np.asarray(a, dtype=np.float32))
    x = f32(x)
    xs = [np.ascontiguousarray(x[0, c * T_:(c + 1) * T_, :].T.reshape(16, 128, T_)) for c in range(NCORES)]
    gall = np.concatenate([f32(norm_gains).reshape(16, D), f32(kv_norm)[None]], 0)
    gains = np.ascontiguousarray(gall.reshape(17, 16, 128).transpose(2, 0, 1).reshape(128, NG))
    gcs = gdn_consts()
    identb = np.eye(128, dtype=np.float32).astype(ml_dtypes.bfloat16)
    w_gu = f32(ffn_w_gate_up)
    w_dn = f32(ffn_w_down)
    w_in = f32(gdn_w_in)
    conv_w = f32(gdn_conv_w)
    a_log = f32(gdn_a_log)
    dtb = f32(gdn_dt_bias)
    onorm = f32(gdn_out_norm)
    g_w_out = f32(gdn_w_out)
    kvw = f32(kv_w)
    wq = f32(dsa_w_q)
    d_w_out = f32(dsa_w_out)
    rb = f32(rel_bias)

    def gdn_inputs(l, projs, bas):
        ba_full = np.concatenate(bas, axis=0)
        pjs = _to_heads(projs, lambda j: [kind * 16 + 2 * j + e for kind in range(4) for e in range(2)])
        maps = []
        for j in range(NCORES):
            hs = [2 * j, 2 * j + 1]
            cwn = np.zeros((128, 6, 4), np.float32)
            for kind in range(3):
                for e_, hd in enumerate(hs):
                    cwn[:, kind * 2 + e_, :] = conv_w[l][:, kind * 2048 + hd * 128: kind * 2048 + (hd + 1) * 128].T
            hpn = np.tile(np.array([a_log[l][hs[0]], a_log[l][hs[1]], dtb[l][hs[0]], dtb[l][hs[1]]], np.float32)[None], (128, 1))
            maps.append({"pj": np.ascontiguousarray(pjs[j].reshape(4, 2, 128, S)),
                         "ba": np.ascontiguousarray(ba_full[:, [hs[0], hs[1], 16 + hs[0], 16 + hs[1]]]),
                         "cw": cwn, "hp": hpn, "on": np.ascontiguousarray(onorm[l][:, None]),
                         "gc": gcs["gc"], "gs": gcs["gsel"]})
        return maps

    def dsa_inputs(qs_, kvs):
        qh = _to_heads(qs_, lambda j: [g * 16 + 2 * j + e for g in range(3) for e in range(2)])
        kh = _to_heads(kvs, lambda j: [g * 16 + 2 * j + e for g in range(3) for e in range(2)])
        vh = _to_heads(kvs, lambda j: [48 + g * 16 + 2 * j + e for g in range(3) for e in range(2)])
        return [{"q": np.ascontiguousarray(qh[j].reshape(3, 2, 128, S)), "k": np.ascontiguousarray(kh[j].reshape(3, 2, 128, S)),
                 "v": np.ascontiguousarray(vh[j].reshape(3, 2, 128, S)),
                 "bias": dsa_bias_tiles(rb, [2 * j, 2 * j + 1]), "ident": identb} for j in range(NCORES)]

    r = _run(build_ts(0, False, "gdn"), [{"x_in": xs[c], "gains": gains, "w_in": w_in[0]} for c in range(NCORES)])
    projs, bas = [r[c]["proj"] for c in range(NCORES)], [r[c]["ba"] for c in range(NCORES)]
    nc_gdn = build_gdn()
    nc_dsa = None
    kvs = None
    for l in range(4):
        if l < 2:
            ro = _run(nc_gdn if l == 0 else build_gdn(), gdn_inputs(l, projs, bas))
        else:
            ro = _run(build_dsa(), dsa_inputs(qs_, kvs))
        o_tok = _to_tokens([ro[j]["out"] for j in range(NCORES)])
        nxt = ["gdn", "dsa_kv", "dsa", None][l]
        w_o = g_w_out[l] if l < 2 else d_w_out[l - 2]
        maps = []
        for c in range(NCORES):
            m = {"x_in": xs[c], "gains": gains, "o_in": o_tok[c], "w_out": w_o, "w_gu": w_gu[l], "w_dn": w_dn[l]}
            if nxt == "gdn":
                m["w_in"] = w_in[l + 1]
            if nxt == "dsa_kv":
                m["kv_w"] = kvw
            if nxt in ("dsa", "dsa_kv"):
                m["w_q"] = wq[l - 1]
            maps.append(m)
        r = _run(build_ts(l, True, nxt), maps)
        xs = [r[c]["x_out"] for c in range(NCORES)]
        if nxt == "gdn":
            projs, bas = [r[c]["proj"] for c in range(NCORES)], [r[c]["ba"] for c in range(NCORES)]
        if nxt == "dsa_kv":
            kvs = [r[c]["kv"] for c in range(NCORES)]
        if nxt in ("dsa", "dsa_kv"):
            qs_ = [r[c]["q_out"] for c in range(NCORES)]
    out = np.empty((1, S, D), np.float32)
    for c in range(NCORES):
        out[0, c * T_:(c + 1) * T_, :] = xs[c].reshape(D, T_).T
    return out
```

```python
import contextlib
import os
import numpy as np
import ml_dtypes
import concourse.bass as bass
import concourse.mybir as mybir
from concourse.bass_utils import run_bass_kernel_spmd

F32 = mybir.dt.float32
BF16 = mybir.dt.bfloat16
ALU = mybir.AluOpType
AF = mybir.ActivationFunctionType
AX = mybir.AxisListType

NCORES = 8
D = 2048
S = 16384
TPC = S // NCORES
DFF = 5632
EPS = 1e-6
NEG = -30000.0


class Buf:
    __slots__ = ("name", "last_writer", "readers", "sem", "dma_cnt", "last_dma", "excl")

    def __init__(self, name):
        self.name = name
        self.excl = False
        self.last_writer = None
        self.readers = []
        self.sem = None
        self.dma_cnt = 0
        self.last_dma = None


class Op:
    __slots__ = ("eng", "fn", "deps", "signal", "token", "is_dma", "dbuf", "inc")

    def __init__(self, eng, fn, is_dma=False, dbuf=None):
        self.eng = eng
        self.fn = fn
        self.deps = []
        self.signal = False
        self.token = None
        self.is_dma = is_dma
        self.dbuf = dbuf


class Prog:
    ENGS = ("pe", "act", "dve", "pool", "sp")

    NPROG = [0]

    def __init__(self, nc, stack):
        self.nc = nc
        self.stack = stack
        Prog.NPROG[0] += 1
        self.tag = f"_g{Prog.NPROG[0]}"
        self.ops = []
        self.bufs = []
        self.nbuf = 0

    def buf(self, name=None):
        b = Buf(name or f"b{self.nbuf}")
        self.nbuf += 1
        self.bufs.append(b)
        return b

    def sb(self, name, shape, dt):
        t = self.stack.enter_context(self.nc.sbuf_tensor(name + self.tag, list(shape), dt))
        return t, self.buf(name)

    def ps(self, name, shape, dt=F32):
        t = self.stack.enter_context(self.nc.psum_tensor(name + self.tag, list(shape), dt))
        b = self.buf(name)
        b.excl = True
        return t, b

    def _deps(self, op, reads, writes):
        deps = []
        for r in reads:
            if r.last_writer is not None:
                deps.append(r.last_writer)
        for w in writes:
            if w.last_writer is not None:
                deps.append(w.last_writer)
            deps.extend(w.readers)
        seen = set()
        for d in deps:
            if d is op or id(d) in seen:
                continue
            seen.add(id(d))
            if d.eng == "pe" and op.eng == "pe" and not d.is_dma and not op.is_dma:
                continue
            op.deps.append(d)
            d.signal = True
        for w in writes:
            w.last_writer = op
            w.readers = []
        for r in reads:
            if r not in writes:
                if op.is_dma:
                    r.readers.append(op)
                else:
                    r.readers = [x for x in r.readers if x.is_dma or x.eng != op.eng]
                    r.readers.append(op)

    def op(self, eng, fn, reads=(), writes=()):
        o = Op(eng, fn)
        reads = list(reads)
        writes = list(writes)
        for r in list(reads):
            if r.excl:
                reads.remove(r)
                if r not in writes:
                    writes.append(r)
        self._deps(o, reads, writes)
        self.ops.append(o)
        return o

    def dma(self, queue, fn, reads=(), writes=(), sbuf=None, inc=16):
        o = Op(queue, fn, is_dma=True, dbuf=sbuf)
        o.inc = inc
        self._deps(o, list(reads), list(writes))
        if sbuf.last_dma is not None and sbuf.last_dma not in o.deps:
            o.deps.append(sbuf.last_dma)
        sbuf.last_dma = o
        sbuf.dma_cnt += inc
        o.token = (sbuf, sbuf.dma_cnt)
        o.signal = True
        self.ops.append(o)
        return o

    def dma_parts(self, queue, fns, reads=(), writes=(), sbuf=None):
        first = self.dma(queue, fns[0], reads=reads, writes=writes, sbuf=sbuf)
        last = first
        for fn in fns[1:]:
            o = Op(queue, fn, is_dma=True, dbuf=sbuf)
            o.inc = 16
            o.deps = list(first.deps)
            sbuf.dma_cnt += 16
            o.token = (sbuf, sbuf.dma_cnt)
            o.signal = True
            self.ops.append(o)
            last = o
        for w in writes:
            w.last_writer = last
        sbuf.last_dma = last
        return last

    def barrier(self):
        last = {}
        for o in self.ops:
            if o.is_dma:
                last[("dma", id(o.dbuf))] = o
            else:
                last[o.eng] = o
        for e in self.ENGS:
            o = Op(e, None)
            for k, d in last.items():
                if not d.is_dma and d.eng == e and e == "pe":
                    continue
                o.deps.append(d)
                d.signal = True
            self.ops.append(o)
        for b in self.bufs:
            b.last_writer = None
            b.readers = []

    def emit(self):
        nc = self.nc
        st = self.stack
        esem = {e: st.enter_context(nc.semaphore(f"sem_{e}{self.tag}")) for e in self.ENGS}
        cnt = {e: 0 for e in self.ENGS}
        for o in self.ops:
            if o.is_dma:
                if o.dbuf.sem is None:
                    o.dbuf.sem = st.enter_context(nc.semaphore(f"sd_{o.dbuf.name}{self.tag}"))
                o.token = (o.dbuf.sem, o.token[1])
            elif o.signal and o.fn is not None:
                cnt[o.eng] += 1
                o.token = (esem[o.eng], cnt[o.eng])
        block = st.enter_context(nc.Block())
        per = {e: [o for o in self.ops if o.eng == e] for e in self.ENGS}

        def run(e, eng):
            known = {}
            for o in per[e]:
                need = {}
                for d in o.deps:
                    if d.token is None:
                        continue
                    s, c = d.token
                    if known.get(id(s), 0) >= c:
                        continue
                    if need.get(id(s), (None, 0))[1] < c:
                        need[id(s)] = (s, c)
                for s, c in need.values():
                    eng.wait_ge(s, c)
                    known[id(s)] = c
                if o.fn is None:
                    continue
                ins = o.fn(eng)
                if o.is_dma:
                    ins.then_inc(o.token[0], o.inc)
                elif o.signal:
                    ins.then_inc(o.token[0], 1)

        @block.tensor
        def _(eng):
            run("pe", eng)

        @block.scalar
        def _(eng):
            run("act", eng)

        @block.vector
        def _(eng):
            run("dve", eng)

        @block.gpsimd
        def _(eng):
            run("pool", eng)

        @block.sync
        def _(eng):
            run("sp", eng)


TB = 512


def _rstd_from_sq(P, sqt, sqb, KC, ones, onesb, pss, pssb, rs, rsb, inv_n):
    for c in range(KC):
        P.op("pe", lambda e, c=c: e.matmul(pss[:], ones[:], sqt[:, c, :], start=(c == 0), stop=(c == KC - 1)),
             reads=[onesb, sqb], writes=[pssb])
    P.op("act", lambda e: e.activation(rs[:], pss[:], AF.Sqrt, bias=EPS, scale=inv_n), reads=[pssb], writes=[rsb])
    P.op("dve", lambda e: e.reciprocal(rs[:], rs[:]), reads=[rsb], writes=[rsb])


def piece_norm(nc, x_d, h_d, gains_d, gcol, T, f_d=None, gcol_post=None, x_out_d=None):
    KC = D // 128
    with contextlib.ExitStack() as st:
        P = Prog(nc, st)
        ones, onesb = P.sb("ones", [128, 128], BF16)
        gt, gtb = P.sb("gains", [128, gains_d.shape[1]], F32)
        P.op("pool", lambda e: e.memset(ones[:], 1.0), writes=[onesb])
        P.dma("sp", lambda e: e.dma_start(out=gt[:], in_=gains_d[:, :]), writes=[gtb], sbuf=gtb)
        xts = [P.sb(f"xt{i}", [128, KC, TB], F32) for i in range(2)]
        fts = [P.sb(f"ft{i}", [128, KC, TB], F32) for i in range(2)] if f_d is not None else None
        sqs = [P.sb(f"sq{i}", [128, KC, TB], BF16) for i in range(1)]
        hts = [P.sb(f"ht{i}", [128, KC, TB], BF16) for i in range(2)] if h_d is not None else None
        rss = [P.sb(f"rs{i}", [128, TB], F32) for i in range(2)]
        tmpp, tmppb = P.sb("tmpp", [128, TB], F32)
        psl = [P.ps(f"pss{i}", [128, TB]) for i in range(2)]
        nb = T // TB

        def issue_loads(tb):
            sl = slice(tb * TB, (tb + 1) * TB)
            xt, xb = xts[tb % 2]
            P.dma("sp", lambda e, xt=xt, sl=sl: e.dma_start(out=xt[:], in_=x_d[:, :, sl].rearrange("c p t -> p c t")),
                  writes=[xb], sbuf=xb)
            if f_d is not None:
                ft, fb = fts[tb % 2]
                P.dma("sp", lambda e, ft=ft, sl=sl: e.dma_start(out=ft[:], in_=f_d[:, :, sl].rearrange("c p t -> p c t")),
                      writes=[fb], sbuf=fb)

        issue_loads(0)
        for tb in range(nb):
            sl = slice(tb * TB, (tb + 1) * TB)
            xt, xb = xts[tb % 2]
            sq, sqb = sqs[0]
            if tb + 1 < nb:
                issue_loads(tb + 1)
            if f_d is not None:
                ft, fb = fts[tb % 2]
                P.op("act", lambda e, ft=ft, sq=sq: e.activation(sq[:], ft[:], AF.Square), reads=[fb], writes=[sqb])
                rs, rsb = rss[0]
                pss, pssb = psl[0]
                _rstd_from_sq(P, sq, sqb, KC, ones, onesb, pss, pssb, rs, rsb, 1.0 / D)
                for c in range(KC):
                    if False:
                        P.op("pool", lambda e, c=c, ft=ft: e.tensor_scalar_mul(ft[:, c, :], ft[:, c, :], gt[:, gcol_post + c:gcol_post + c + 1]),
                             reads=[fb, gtb], writes=[fb])
                        P.op("pool", lambda e, c=c, ft=ft, rs=rs: e.tensor_tensor(ft[:, c, :], ft[:, c, :], rs[:], ALU.mult),
                             reads=[fb, rsb], writes=[fb])
                        continue
                    P.op("dve", lambda e, c=c, ft=ft, rs=rs: e.scalar_tensor_tensor(
                        ft[:, c, :], ft[:, c, :], gt[:, gcol_post + c:gcol_post + c + 1], rs[:], ALU.mult, ALU.mult),
                        reads=[fb, rsb, gtb], writes=[fb])
                P.op("pool", lambda e, xt=xt, ft=ft: e.tensor_tensor(xt[:], xt[:], ft[:], ALU.add),
                     reads=[xb, fb], writes=[xb])
                P.dma("sp", lambda e, xt=xt, sl=sl: e.dma_start(out=x_out_d[:, :, sl].rearrange("c p t -> p c t"), in_=xt[:]),
                      reads=[xb], sbuf=xb)
            if h_d is not None:
                ht, hb = hts[tb % 2]
                P.op("act", lambda e, xt=xt, sq=sq: e.activation(sq[:], xt[:], AF.Square), reads=[xb], writes=[sqb])
                rs, rsb = rss[1]
                pss, pssb = psl[1]
                _rstd_from_sq(P, sq, sqb, KC, ones, onesb, pss, pssb, rs, rsb, 1.0 / D)
                for c in range(KC):
                    if False:
                        P.op("pool", lambda e, c=c, xt=xt: e.tensor_scalar_mul(tmpp[:], xt[:, c, :], gt[:, gcol + c:gcol + c + 1]),
                             reads=[xb, gtb], writes=[tmppb])
                        P.op("pool", lambda e, c=c, ht=ht, rs=rs: e.tensor_tensor(ht[:, c, :], tmpp[:], rs[:], ALU.mult),
                             reads=[tmppb, rsb], writes=[hb])
                        continue
                    P.op("dve", lambda e, c=c, xt=xt, ht=ht, rs=rs: e.scalar_tensor_tensor(
                        ht[:, c, :], xt[:, c, :], gt[:, gcol + c:gcol + c + 1], rs[:], ALU.mult, ALU.mult),
                        reads=[xb, rsb, gtb], writes=[hb])
                P.dma("sp", lambda e, ht=ht, sl=sl: e.dma_start(out=h_d[:, :, sl].rearrange("c p t -> p c t"), in_=ht[:]),
                      reads=[hb], sbuf=hb)
        P.barrier()
        P.emit()


def piece_gemm(nc, act_d, KC, T, W_d, groups, evac_factory, TR=None, extra=None):
    TR = TR or T
    maxw = max(sum(n for _, n in g[0]) for g in groups)
    with contextlib.ExitStack() as st:
        P = Prog(nc, st)
        act, actb = P.sb("act", [128, KC, TR], BF16)
        wsl = [P.sb(f"w{i}", [128, KC, maxw], BF16) for i in range(3)]
        psl = [P.ps(f"pg{i}", [128, TB]) for i in range(6)]
        evac = evac_factory(P)
        if extra is not None:
            extra_fn = extra(P)
        pi = 0
        gi = 0
        for half in range(T // TR):
            P.dma("sp", lambda e, half=half: e.dma_start(
                out=act[:], in_=act_d[:, :, half * TR:(half + 1) * TR].rearrange("c p t -> p c t")),
                writes=[actb], sbuf=actb)
            if extra is not None:
                extra_fn(act, actb, half)
            ug = 0
            for cols, units in groups:
                w, wb = wsl[gi % 3]
                gi += 1
                off = 0
                fns = []
                for (c0, n) in cols:
                    for k0 in range(0, KC, 4):
                        k1 = min(KC, k0 + 4)
                        fns.append(lambda e, w=w, off=off, c0=c0, n=n, k0=k0, k1=k1: e.dma_start(
                            out=w[:, k0:k1, off:off + n],
                            in_=W_d[k0 * 128:k1 * 128, c0:c0 + n].rearrange("(k p) n -> p k n", p=128)))
                    off += n
                P.dma_parts("pool", fns, writes=[wb], sbuf=wb)
                for unit in units:
                    for tb in range(TR // TB):
                        pss = []
                        for uo in unit:
                            ps, psb = psl[pi % 6]
                            pi += 1
                            for k in range(KC):
                                P.op("pe", lambda e, ps=ps, w=w, uo=uo, k=k, tb=tb: e.matmul(
                                    ps[:], w[:, k, uo:uo + 128], act[:, k, tb * TB:(tb + 1) * TB],
                                    start=(k == 0), stop=(k == KC - 1)), reads=[wb, actb], writes=[psb])
                            pss.append((ps, psb))
                        evac(ug, half * (TR // TB) + tb, pss)
                    ug += 1
        P.barrier()
        P.emit()


def store_evac(out_ap_fn, TR, dt, nslots=3):
    def factory(P):
        tiles = [P.sb(f"ot{i}", [128, TR], dt) for i in range(nslots)]
        state = {"n": 0, "t": -1, "key": None}
        nb = TR // TB

        def evac(u, tb, pss):
            key = (u, tb // nb)
            if key != state["key"]:
                state["key"] = key
                state["t"] += 1
            ot, ob = tiles[state["t"] % nslots]
            ps, psb = pss[0]
            i = state["n"]
            state["n"] += 1
            sl = slice((tb % nb) * TB, (tb % nb + 1) * TB)
            if i % 2 == 0:
                P.op("act", lambda e: e.activation(ot[:, sl], ps[:], AF.Copy), reads=[psb], writes=[ob])
            else:
                P.op("dve", lambda e: e.tensor_copy(ot[:, sl], ps[:]), reads=[psb], writes=[ob])
            if tb % nb == nb - 1:
                dst = out_ap_fn(u, tb // nb)
                P.dma("sp", lambda e: e.dma_start(out=dst, in_=ot[:]), reads=[ob], sbuf=ob)
        return evac
    return factory


SEG = 2048
CH = 64


def gdn_consts():
    idx = np.arange(128)
    same = (idx[:, None] // CH) == (idx[None, :] // CH)
    U = (same & (idx[:, None] <= idx[None, :])).astype(np.float32)
    BM = same.astype(np.float32)
    negm = np.where(same & (idx[None, :] >= idx[:, None]), 0.0, NEG).astype(np.float32)
    strict = (same & (idx[None, :] > idx[:, None])).astype(np.float32)
    ident = np.eye(128, dtype=np.float32)
    cs = (idx % CH == 0).astype(np.float32)[:, None]
    half = np.stack([(idx < CH), (idx >= CH)], 1).astype(np.float32)
    sel = np.zeros((2, 128), np.float32)
    sel[0] = 1.0
    c = np.concatenate([U, BM, negm, strict, ident, cs, half, np.ones((128, 1), np.float32)], 1)
    return {"gc": np.ascontiguousarray(c), "gsel": sel}


GC_U, GC_BM, GC_NEG, GC_STR, GC_ID, GC_CS, GC_HALF, GC_ONE = 0, 128, 256, 384, 512, 640, 641, 643
GC_W = 644


class _Stop(Exception):
    pass


def stage_gdn_seq(nc, pj_d, ba_d, convw_d, hp_d, onorm_d, gc_d, gsel_d, out_d, SL):
    try:
        _stage_gdn_seq(nc, pj_d, ba_d, convw_d, hp_d, onorm_d, gc_d, gsel_d, out_d, SL)
    except _Stop:
        pass


def _stage_gdn_seq(nc, pj_d, ba_d, convw_d, hp_d, onorm_d, gc_d, gsel_d, out_d, SL):
    LVL = float(os.environ.get("GDN_DBG", "99"))

    def chk(P, k):
        if LVL <= k:
            P.barrier()
            P.emit()
            raise _Stop()
    NT = SL // 128
    NSEG = SL // SEG
    with contextlib.ExitStack() as st:
        P = Prog(nc, st)
        gc, gcb = P.sb("gc", [128, GC_W], F32)
        P.dma("sp", lambda e: e.dma_start(out=gc[:], in_=gc_d[:, :]), writes=[gcb], sbuf=gcb)
        gsel, gselb = P.sb("gsel", [2, 128], F32)
        P.dma("sp", lambda e: e.dma_start(out=gsel[:], in_=gsel_d[:, :]), writes=[gselb], sbuf=gselb)
        cw, cwb = P.sb("cw", [128, 6, 4], F32)
        P.dma("sp", lambda e: e.dma_start(out=cw[:], in_=convw_d[:, :, :]), writes=[cwb], sbuf=cwb)
        hp, hpb = P.sb("hp", [128, 4], F32)
        P.dma("sp", lambda e: e.dma_start(out=hp[:], in_=hp_d[:, :]), writes=[hpb], sbuf=hpb)
        onm, onmb = P.sb("onm", [128, 1], F32)
        P.dma("sp", lambda e: e.dma_start(out=onm[:], in_=onorm_d[:, :]), writes=[onmb], sbuf=onmb)
        cbf, cbfb = P.sb("cbf", [128, 3, 128], BF16)
        P.op("dve", lambda e: e.tensor_copy(cbf[:, 0, :], gc[:, GC_ID:GC_ID + 128]), reads=[gcb], writes=[cbfb])
        P.op("dve", lambda e: e.tensor_copy(cbf[:, 1, :], gc[:, GC_STR:GC_STR + 128]), reads=[gcb], writes=[cbfb])
        P.op("dve", lambda e: e.memset(cbf[:, 2, :], 1.0), writes=[cbfb])
        identb = cbf[:, 0, :]
        onesb16 = cbf[:, 2, :]
        dg, dgb = P.sb("dg", [128, 24, 128], BF16)
        for ci_ in range(6):
            for s_ in range(4):
                P.op("dve", lambda e, ci_=ci_, s_=s_: e.tensor_scalar_mul(dg[:, ci_ * 4 + s_, :], gc[:, GC_ID:GC_ID + 128], cw[:, ci_, s_:s_ + 1]),
                     reads=[gcb, cwb], writes=[dgb])
        U = gc[:, GC_U:GC_U + 128]
        pf = [P.ps(f"pf{i}", [128, 512]) for i in range(6)]
        pb = [P.ps(f"pb{i}", [128, 1024], BF16) for i in range(2)]
        ba, bab = P.sb("ba", [128, NT, 4], F32)
        P.dma("sp", lambda e: e.dma_start(out=ba[:], in_=ba_d.rearrange("(n p) x -> p n x", p=128)), writes=[bab], sbuf=bab)
        gt, gtb = P.sb("gt", [128, NT, 2], F32)
        beta, betab = P.sb("beta", [128, NT, 2], F32)
        gcum, gcumb = P.sb("gcum", [128, NT, 2], F32)
        eg, egb = P.sb("eg", [128, NT, 2], F32)
        edec, edecb = P.sb("edec", [128, NT, 2], F32)
        gh, ghb = P.sb("gh", [128, NT, 2, 2], F32)
        glx, glxb = P.sb("glx", [128, NT, 2, 2], F32)
        gm, gmb = P.sb("gm", [128, NT, 2, 4], F32)
        tmp, tmpb = P.sb("gtmp", [128, NT, 2], F32)
        ea, eab = P.sb("ea", [128, 2], F32)
        P.op("act", lambda e: e.activation(beta[:], ba[:, :, 0:2], AF.Exp, scale=-1.0), reads=[bab], writes=[betab])
        P.op("dve", lambda e: e.tensor_scalar(beta[:], beta[:], 1.0, 1.0, ALU.mult, ALU.add), reads=[betab], writes=[betab])
        P.op("dve", lambda e: e.reciprocal(beta[:], beta[:]), reads=[betab], writes=[betab])
        for e_ in range(2):
            P.op("act", lambda e, e_=e_: e.activation(tmp[:, :, e_], ba[:, :, 2 + e_], AF.Exp, bias=hp[:, 2 + e_:3 + e_]),
                 reads=[bab, hpb], writes=[tmpb])
        P.op("act", lambda e: e.activation(tmp[:], tmp[:], AF.Ln, bias=1.0), reads=[tmpb], writes=[tmpb])
        P.op("act", lambda e: e.activation(ea[:], hp[:, 0:2], AF.Exp), reads=[hpb], writes=[eab])
        for e_ in range(2):
            P.op("dve", lambda e, e_=e_: e.tensor_scalar(gt[:, :, e_], tmp[:, :, e_], ea[:, e_:e_ + 1], -1.0, ALU.mult, ALU.mult),
                 reads=[tmpb, eab], writes=[gtb])
        G2 = NT * 2
        def gflat(t):
            return t[:].rearrange("p n e -> p (n e)")
        for c0 in range(0, G2, 512):
            c1 = min(G2, c0 + 512)
            ps, psb = pf[0]
            P.op("pe", lambda e, c0=c0, c1=c1, ps=ps: e.matmul(ps[:, 0:c1 - c0], U, gflat(gt)[:, c0:c1], start=True, stop=True),
                 reads=[gcb, gtb], writes=[psb])
            P.op("dve", lambda e, c0=c0, c1=c1, ps=ps: e.tensor_copy(gflat(gcum)[:, c0:c1], ps[:, 0:c1 - c0]), reads=[psb], writes=[gcumb])
            ps2, ps2b = pf[1]
            P.op("pe", lambda e, c0=c0, c1=c1, ps2=ps2: e.matmul(ps2[:, 0:c1 - c0], gc[:, GC_BM:GC_BM + 128], gflat(gt)[:, c0:c1], start=True, stop=True),
                 reads=[gcb, gtb], writes=[ps2b])
            P.op("dve", lambda e, c0=c0, c1=c1, ps2=ps2: e.tensor_tensor(gflat(edec)[:, c0:c1], ps2[:, 0:c1 - c0], gflat(gcum)[:, c0:c1], ALU.subtract),
                 reads=[ps2b, gcumb], writes=[edecb])
        P.op("act", lambda e: e.activation(edec[:], edec[:], AF.Exp), reads=[edecb], writes=[edecb])
        P.op("act", lambda e: e.activation(eg[:], gcum[:], AF.Exp), reads=[gcumb], writes=[egb])
        for h_ in range(2):
            P.op("dve", lambda e, h_=h_: e.tensor_scalar_mul(gh[:, :, :, h_], gt[:], gc[:, GC_HALF + h_:GC_HALF + h_ + 1]),
                 reads=[gtb, gcb], writes=[ghb])
        G4 = NT * 4
        onesf, onesfb = P.sb("onesf", [128, 128], F32)
        P.op("pool", lambda e: e.memset(onesf[:], 1.0), writes=[onesfb])
        for c0 in range(0, G4, 512):
            c1 = min(G4, c0 + 512)
            ps, psb = pf[2]
            P.op("pe", lambda e, c0=c0, c1=c1, ps=ps: e.matmul(ps[:, 0:c1 - c0], onesf[:], gh[:].rearrange("p n e h -> p (n e h)")[:, c0:c1], start=True, stop=True),
                 reads=[onesfb, ghb], writes=[psb])
            P.op("act", lambda e, c0=c0, c1=c1, ps=ps: e.activation(glx[:].rearrange("p n e h -> p (n e h)")[:, c0:c1], ps[:, 0:c1 - c0], AF.Exp),
                 reads=[psb], writes=[glxb])
        P.op("dve", lambda e: e.tensor_copy(gm[:, :, :, 0], gt[:]), reads=[gtb], writes=[gmb])
        P.op("dve", lambda e: e.tensor_scalar_mul(gm[:, :, :, 3], gt[:], -1.0), reads=[gtb], writes=[gmb])
        for k_ in (1, 2):
            P.op("act", lambda e, k_=k_: e.activation(gm[:, :, :, k_], gt[:], AF.Identity, bias=gc[:, GC_CS:GC_CS + 1], scale=0.0),
                 reads=[gtb, gcb], writes=[gmb])
        Rr, Rrb = P.sb("Rr", [2, 2, SEG], F32)
        Lr, Lrb = P.sb("Lr", [2, 2, SEG], F32)
        chk(P, 1)
        Sf = [P.sb(f"Sf{e_}", [128, 128], F32) for e_ in range(2)]
        Sb = [P.sb(f"Sb{e_}", [128, 128], BF16) for e_ in range(2)]
        for e_ in range(2):
            P.op("pool", lambda e, e_=e_: e.memset(Sf[e_][0][:], 0.0), writes=[Sf[e_][1]])
            P.op("pool", lambda e, e_=e_: e.memset(Sb[e_][0][:], 0.0), writes=[Sb[e_][1]])
        xin = [P.sb(f"xin{i}", [128, 4 + SEG], BF16) for i in range(2)]
        acc, accb = P.sb("acc", [128, SEG], F32)
        ysil, ysilb = P.sb("ysil", [128, SEG], F32)
        sq, sqb = P.sb("sqs", [128, SEG], BF16)
        rn, rnb = P.sb("rn", [128, TB], F32)
        egr, egrb = P.sb("egr", [128, TB], F32)
        segq = [P.sb(f"segq{e_}", [128, SEG], BF16) for e_ in range(2)]
        segqd = [P.sb(f"segqd{e_}", [128, SEG], BF16) for e_ in range(2)]
        segk = [P.sb(f"segk{e_}", [128, SEG], BF16) for e_ in range(2)]
        segv = [P.sb(f"segv{e_}", [128, SEG], BF16) for e_ in range(2)]
        segz = [P.sb(f"segz{e_}", [128, SEG], BF16) for e_ in range(2)]
        sego = [P.sb(f"sego{e_}", [128, SEG], BF16) for e_ in range(2)]
        def mk(name, dt, w=128, n=8, p=128):
            return [P.sb(f"{name}{i}", [p, w], dt) for i in range(n)]
        dT = mk("dT", F32); dTs = mk("dTs", F32); Wt = mk("Wt", BF16); inT = mk("inT", BF16)
        Yp = [mk(f"Yp{k_}_", BF16) for k_ in range(1)]
        YW = [mk(f"YW{k_}_", BF16, w=256) for k_ in range(5)]
        Pm = [mk(f"Pm{k_}_", BF16) for k_ in range(2)]
        kg = mk("kg", BF16); kdec = mk("kdec", BF16); vtok = mk("vtok", BF16); nwT = mk("nwT", BF16)
        vnew = mk("vnew", BF16); otok = mk("otok", F32); onb = mk("onb", BF16)
        ssq = mk("ssq", F32, w=1); junk = mk("junk", F32)
        xi = 0
        pfi = [0]
        pbi = [0]

        def PF():
            i = pfi[0]
            pfi[0] += 1
            t, b = pf[i % 6]
            return t, b

        def PB():
            i = pbi[0]
            pbi[0] += 1
            t, b = pb[i % 2]
            return t, b

        kk = 0
        for sg in range(NSEG):
            t_lo = sg * SEG
            for e_ in range(2):
                for t0 in range(0, SEG // 128, 4):
                    for (dstt, dstb, cofs) in ((Rr, Rrb, 0), (Lr, Lrb, 2)):
                        ps, psb = pf[3 + (kk % 2)]
                        kk += 1
                        for q_ in range(4):
                            P.op("pe", lambda e, ps=ps, e_=e_, t=sg * (SEG // 128) + t0 + q_, q_=q_, cofs=cofs: e.matmul(
                                ps[0:2, q_ * 128:(q_ + 1) * 128], gm[:, t, e_, cofs:cofs + 2], U, start=True, stop=True),
                                reads=[gmb, gcb], writes=[psb])
                        P.op("act", lambda e, ps=ps, dstt=dstt, e_=e_, t0=t0: e.activation(dstt[:, e_, t0 * 128:(t0 + 4) * 128], ps[0:2, :], AF.Copy),
                             reads=[psb], writes=[dstb])
            chk(P, 2)
            for e_ in range(2):
                for kind in range(4):
                    xt, xb = xin[xi % 2]
                    xi += 1
                    P.dma("sp", lambda e, xt=xt, kind=kind, e_=e_, t_lo=t_lo: e.dma_start(out=xt[:, 4:4 + SEG], in_=pj_d[kind, e_, :, t_lo:t_lo + SEG]),
                          writes=[xb], sbuf=xb)
                    if kind == 3:
                        P.op("act", lambda e, xt=xt, e_=e_: e.activation(segz[e_][0][:], xt[:, 4:4 + SEG], AF.Silu), reads=[xb], writes=[segz[e_][1]])
                        continue
                    if sg == 0:
                        P.op("pool", lambda e, xt=xt: e.memset(xt[:, 0:4], 0.0), writes=[xb])
                    else:
                        P.dma("sp", lambda e, xt=xt, kind=kind, e_=e_, t_lo=t_lo: e.dma_start(out=xt[:, 1:4], in_=pj_d[kind, e_, :, t_lo - 3:t_lo]),
                              writes=[xb], sbuf=xb)
                    ci = kind * 2 + e_
                    dst, dstb = (segv[e_] if kind == 2 else (ysil, ysilb))
                    for tb in range(SEG // TB):
                        ps, psb = PF()
                        for s_ in range(4):
                            P.op("pe", lambda e, ps=ps, xt=xt, ci=ci, s_=s_, tb=tb: e.matmul(
                                ps[:], dg[:, ci * 4 + s_, :], xt[:, 1 + s_ + tb * TB:1 + s_ + (tb + 1) * TB],
                                start=(s_ == 0), stop=(s_ == 3)), reads=[dgb, xb], writes=[psb])
                        P.op("act", lambda e, ps=ps, dst=dst, tb=tb: e.activation(dst[:, tb * TB:(tb + 1) * TB], ps[:], AF.Silu),
                             reads=[psb], writes=[dstb])
                    if kind == 2:
                        continue
                    P.op("act", lambda e: e.activation(sq[:], ysil[:], AF.Square), reads=[ysilb], writes=[sqb])
                    for tb in range(SEG // TB):
                        sl = slice(tb * TB, (tb + 1) * TB)
                        ps, psb = PF()
                        P.op("pe", lambda e, ps=ps, sl=sl: e.matmul(ps[:], onesb16, sq[:, sl], start=True, stop=True), reads=[cbfb, sqb], writes=[psb])
                        P.op("act", lambda e, ps=ps: e.activation(rn[:], ps[:], AF.Sqrt, bias=EPS, scale=1.0), reads=[psb], writes=[rnb])
                        P.op("dve", lambda e: e.reciprocal(rn[:], rn[:]), reads=[rnb], writes=[rnb])
                        if kind == 1:
                            P.op("dve", lambda e, sl=sl, e_=e_: e.tensor_tensor(segk[e_][0][:, sl], ysil[:, sl], rn[:], ALU.mult),
                                 reads=[ysilb, rnb], writes=[segk[e_][1]])
                        else:
                            P.op("dve", lambda e, sl=sl, e_=e_: e.scalar_tensor_tensor(segq[e_][0][:, sl], ysil[:, sl], float(128 ** -0.5), rn[:], ALU.mult, ALU.mult),
                                 reads=[ysilb, rnb], writes=[segq[e_][1]])
                            ps2, ps2b = PF()
                            P.op("pe", lambda e, ps2=ps2, sl=sl, e_=e_, t_lo=t_lo: e.matmul(ps2[:], gsel[:], Rr[:, e_, sl], start=True, stop=True),
                                 reads=[gselb, Rrb], writes=[ps2b])
                            P.op("act", lambda e, ps2=ps2: e.activation(egr[:], ps2[:], AF.Exp), reads=[ps2b], writes=[egrb])
                            P.op("dve", lambda e, sl=sl, e_=e_: e.tensor_tensor(segqd[e_][0][:, sl], segq[e_][0][:, sl], egr[:], ALU.mult),
                                 reads=[segq[e_][1], egrb], writes=[segqd[e_][1]])
            chk(P, 3)
            def unit(tg, cs_, e_, w):
                gs_ = cs_
                ui = w % 4
                PF = lambda: pf[ui]
                PB = lambda: pb[ui % 2]
                pTo = (ui // 2) * 384
                qT = segq[e_][0][:, cs_]; qdT = segqd[e_][0][:, cs_]; kT = segk[e_][0][:, cs_]; vT = segv[e_][0][:, cs_]
                rb = [segq[e_][1], segqd[e_][1], segk[e_][1], segv[e_][1]]
                psA, psAb = PF()
                P.op("pe", lambda e, psA=psA, kT=kT: e.matmul(psA[:, 0:128], kT, kT, start=True, stop=True), reads=[rb[2]], writes=[psAb])
                P.op("pe", lambda e, psA=psA, kT=kT, qT=qT: e.matmul(psA[:, 128:256], kT, qT, start=True, stop=True), reads=[rb[2], rb[0]], writes=[psAb])
                P.op("pe", lambda e, psA=psA, e_=e_, gs_=gs_: e.matmul(psA[:, 256:384], Lr[:, e_, gs_], Rr[:, e_, gs_], start=True, stop=False),
                     reads=[Lrb, Rrb], writes=[psAb])
                P.op("pe", lambda e, psA=psA: e.matmul(psA[:, 256:384], gc[:, GC_ID:GC_ID + 128], gc[:, GC_NEG:GC_NEG + 128], start=False, stop=True),
                     reads=[gcb], writes=[psAb])
                P.op("act", lambda e, psA=psA, w=w: e.activation(dT[w][0][:], psA[:, 256:384], AF.Exp), reads=[psAb], writes=[dT[w][1]])
                yield None
                P.op("pool", lambda e, w=w: e.tensor_tensor(dTs[w][0][:], dT[w][0][:], gc[:, GC_STR:GC_STR + 128], ALU.mult),
                     reads=[dT[w][1], gcb], writes=[dTs[w][1]])
                yield None
                P.op("dve", lambda e, psA=psA, w=w, tg=tg, e_=e_: e.scalar_tensor_tensor(Wt[w][0][:], psA[:, 0:128], beta[:, tg, e_:e_ + 1], dTs[w][0][:], ALU.mult, ALU.mult),
                     reads=[psAb, betab, dTs[w][1]], writes=[Wt[w][1]])
                yield None
                P.op("dve", lambda e, psA=psA, w=w: e.tensor_tensor(inT[w][0][:], psA[:, 128:256], dT[w][0][:], ALU.mult),
                     reads=[psAb, dT[w][1]], writes=[inT[w][1]])
                yield None
                pT, pTb = PB()
                P.op("pe", lambda e, pT=pT, w=w: e.transpose(pT[:, pTo:pTo + 128], Wt[w][0][:], identb), reads=[Wt[w][1], cbfb], writes=[pTb])
                P.op("act", lambda e, pT=pT, w=w: e.activation(Yp[0][w][0][:], pT[:, pTo:pTo + 128], AF.Copy), reads=[pTb], writes=[Yp[0][w][1]])
                yield None
                P.op("pe", lambda e, pT=pT, kT=kT: e.transpose(pT[:, pTo + 128:pTo + 256], kT, identb), reads=[rb[2], cbfb], writes=[pTb])
                P.op("pe", lambda e, pT=pT, vT=vT: e.transpose(pT[:, pTo + 256:pTo + 384], vT, identb), reads=[rb[3], cbfb], writes=[pTb])
                P.op("act", lambda e, pT=pT, w=w, tg=tg, e_=e_: e.activation(kg[w][0][:], pT[:, pTo + 128:pTo + 256], AF.Copy, scale=eg[:, tg, e_:e_ + 1]),
                     reads=[pTb, egb], writes=[kg[w][1]])
                yield None
                P.op("act", lambda e, pT=pT, w=w, tg=tg, e_=e_: e.activation(kdec[w][0][:], pT[:, pTo + 128:pTo + 256], AF.Copy, scale=edec[:, tg, e_:e_ + 1]),
                     reads=[pTb, edecb], writes=[kdec[w][1]])
                yield None
                P.op("dve", lambda e, pT=pT, w=w: e.tensor_copy(vtok[w][0][:], pT[:, pTo + 256:pTo + 384]), reads=[pTb], writes=[vtok[w][1]])
                yield None
                Yc_ap, Yc_b = Yp[0][w][0][:], Yp[0][w][1]
                Wc_ap, Wc_b = Wt[w][0][:], Wt[w][1]
                for k_ in range(5):
                    ps, psb = PF()
                    P.op("pe", lambda e, ps=ps, Wc=Wc_ap, Yc=Yc_ap: e.matmul(ps[:, 0:128], Wc, Yc, start=True, stop=True),
                         reads=[Wc_b, Yc_b], writes=[psb])
                    if k_ < 4:
                        P.op("pe", lambda e, ps=ps, Wc=Wc_ap, Yc=Yc_ap: e.matmul(ps[:, 128:256], Yc, Wc, start=True, stop=True),
                             reads=[Wc_b, Yc_b], writes=[psb])
                    YWn, YWnb = YW[k_][w]
                    ncol = 256 if k_ < 4 else 128
                    if k_ % 2 == 0:
                        P.op("act", lambda e, ps=ps, YWn=YWn, ncol=ncol: e.activation(YWn[:, 0:ncol], ps[:, 0:ncol], AF.Copy), reads=[psb], writes=[YWnb])
                    else:
                        P.op("dve", lambda e, ps=ps, YWn=YWn, ncol=ncol: e.tensor_copy(YWn[:, 0:ncol], ps[:, 0:ncol]), reads=[psb], writes=[YWnb])
                    yield None
                    Yc_ap, Yc_b = YWn[:, 0:128], YWnb
                    Wc_ap, Wc_b = YWn[:, 128:256], YWnb
                Pc = Pm[0][w]
                P.op("pool", lambda e, Pc=Pc, w=w: e.tensor_tensor(Pc[0][:], identb, Wt[w][0][:], ALU.subtract), reads=[cbfb, Wt[w][1]], writes=[Pc[1]])
                yield None
                for k_ in range(1, 6):
                    ps, psb = PF()
                    Yk = YW[k_ - 1][w]
                    P.op("pe", lambda e, ps=ps, Yk=Yk, Pc=Pc: e.matmul(ps[:, 0:128], Yk[0][:, 0:128], Pc[0][:], start=True, stop=True),
                         reads=[Yk[1], Pc[1]], writes=[psb])
                    Pn = Pm[k_ % 2][w]
                    P.op("dve", lambda e, ps=ps, Pn=Pn, Pc=Pc: e.tensor_tensor(Pn[0][:], ps[:, 0:128], Pc[0][:], ALU.add), reads=[psb, Pc[1]], writes=[Pn[1]])
                    yield None
                    Pc = Pn
                TT = Pc
                ps, psb = PF()
                P.op("pe", lambda e, ps=ps, w=w, TT=TT: e.matmul(ps[:, 0:128], kg[w][0][:], TT[0][:], start=True, stop=True),
                     reads=[kg[w][1], TT[1]], writes=[psb])
                P.op("act", lambda e, ps=ps, w=w: e.activation(nwT[w][0][:], ps[:, 0:128], AF.Copy, scale=-1.0), reads=[psb], writes=[nwT[w][1]])
                yield None
                yield "P2"
                PF = lambda: pf[4 + e_]
                S_f, S_fb = Sf[e_]
                S_b, S_bb = Sb[e_]
                for h_ in range(2):
                    r = slice(h_ * CH, (h_ + 1) * CH)
                    ps, psb = PF()
                    P.op("pe", lambda e, ps=ps, TT=TT, w=w, r=r: e.matmul(ps[r, 0:128], TT[0][:, r], vtok[w][0][:], start=True, stop=False),
                         reads=[TT[1], vtok[w][1]], writes=[psb])
                    P.op("pe", lambda e, ps=ps, w=w, r=r, S_b=S_b: e.matmul(ps[r, 0:128], nwT[w][0][:, r], S_b[:], start=False, stop=True),
                         reads=[nwT[w][1], S_bb], writes=[psb])
                    P.op("act", lambda e, ps=ps, w=w, r=r, tg=tg, e_=e_: e.activation(vnew[w][0][r, :], ps[r, 0:128], AF.Copy, scale=beta[r, tg, e_:e_ + 1]),
                         reads=[psb, betab], writes=[vnew[w][1]])
                    yield None
                    P.op("pe", lambda e, ps=ps, r=r, qdT=qdT, S_b=S_b: e.matmul(ps[r, 128:256], qdT[:, r], S_b[:], start=True, stop=False),
                         reads=[rb[1], S_bb], writes=[psb])
                    P.op("pe", lambda e, ps=ps, r=r, w=w: e.matmul(ps[r, 128:256], inT[w][0][r, r], vnew[w][0][r, :], start=False, stop=True),
                         reads=[inT[w][1], vnew[w][1]], writes=[psb])
                    P.op("pe", lambda e, ps=ps, r=r, w=w: e.matmul(ps[:, 256:384], kdec[w][0][r, :], vnew[w][0][r, :], start=True, stop=True),
                         reads=[kdec[w][1], vnew[w][1]], writes=[psb])
                    P.op("act", lambda e, ps=ps, r=r, w=w: e.activation(otok[w][0][r, :], ps[r, 128:256], AF.Copy), reads=[psb], writes=[otok[w][1]])
                    yield None
                    gcol = glx[:, tg, e_, h_:h_ + 1]
                    P.op("dve", lambda e, ps=ps, S_b=S_b, S_f=S_f, gcol=gcol: e.scalar_tensor_tensor(S_b[:], S_f[:], gcol, ps[:, 256:384], ALU.mult, ALU.add),
                         reads=[psb, S_fb, glxb], writes=[S_bb])
                    yield None
                    P.op("dve", lambda e, ps=ps, S_f=S_f, gcol=gcol: e.scalar_tensor_tensor(S_f[:], S_f[:], gcol, ps[:, 256:384], ALU.mult, ALU.add),
                         reads=[psb, S_fb, glxb], writes=[S_fb])
                    yield None
                P.op("act", lambda e, w=w: e.activation(junk[w][0][:], otok[w][0][:], AF.Square, accum_out=ssq[w][0][:]), reads=[otok[w][1]], writes=[junk[w][1], ssq[w][1]])
                yield None
                P.op("act", lambda e, w=w: e.activation(ssq[w][0][:], ssq[w][0][:], AF.Sqrt, bias=EPS, scale=1.0 / 128), reads=[ssq[w][1]], writes=[ssq[w][1]])
                yield None
                P.op("dve", lambda e, w=w: e.reciprocal(ssq[w][0][:], ssq[w][0][:]), reads=[ssq[w][1]], writes=[ssq[w][1]])
                yield None
                P.op("dve", lambda e, w=w: e.tensor_scalar_mul(onb[w][0][:], otok[w][0][:], ssq[w][0][:, 0:1]), reads=[otok[w][1], ssq[w][1]], writes=[onb[w][1]])
                yield None
                pT2, pT2b = pb[e_]
                P.op("pe", lambda e, pT2=pT2, w=w: e.transpose(pT2[:, 768:896], onb[w][0][:], identb), reads=[onb[w][1], cbfb], writes=[pT2b])
                P.op("dve", lambda e, pT2=pT2, e_=e_, cs_=cs_: e.scalar_tensor_tensor(sego[e_][0][:, cs_], pT2[:, 768:896], onm[:, 0:1], segz[e_][0][:, cs_], ALU.mult, ALU.mult),
                     reads=[pT2b, onmb, segz[e_][1]], writes=[sego[e_][1]])
                yield None

            NU = 4
            ulist = []
            for tl in range(SEG // 128):
                for e_ in range(2):
                    ulist.append((sg * (SEG // 128) + tl, slice(tl * 128, (tl + 1) * 128), e_))
            prev_ch = []
            for gi_ in range(0, len(ulist) + NU, NU):
                grp = ulist[gi_:gi_ + NU]
                p1 = []
                for ui_, (tg_, cs2_, e2_) in enumerate(grp):
                    gen = unit(tg_, cs2_, e2_, ((gi_ // NU) % 2) * NU + ui_)
                    p1.append((gen, e2_))
                parked = []
                chains = prev_ch
                while p1 or any(chains):
                    for item in list(p1):
                        if next(item[0]) == "P2":
                            p1.remove(item)
                            parked.append(item)
                    for ch in chains:
                        if ch:
                            try:
                                next(ch[0][0])
                            except StopIteration:
                                ch.pop(0)
                prev_ch = [[it for it in parked if it[1] == 0], [it for it in parked if it[1] == 1]]
            for e_ in range(2):
                P.dma("pool", lambda e, e_=e_, t_lo=t_lo: e.dma_start(out=out_d[e_, :, t_lo:t_lo + SEG], in_=sego[e_][0][:]), reads=[sego[e_][1]], sbuf=sego[e_][1])
        P.barrier()
        P.emit()


DILS = (1, 4, 16)


def t5_bucket_np(dist):
    max_exact = 16
    d_f = np.maximum(dist, 1).astype(np.float32)
    large = max_exact + (np.log(d_f / np.float32(max_exact)) / np.float32(np.log(2048 / max_exact))
                         * np.float32(32 - max_exact)).astype(np.int32)
    large = np.minimum(large, 31)
    return np.where(dist < max_exact, dist, large)


def dsa_bias_tiles(rel_bias, heads):
    k = np.arange(128)[:, None]
    j = np.arange(256)[None, :]
    rel = np.where(j < 128, j + 128 - k, j - 128 - k)
    valid = np.where(j < 128, j <= k, (j - 128) >= k)
    out = np.zeros((3, len(heads), 128, 256), np.float32)
    for g, d in enumerate(DILS):
        bucket = t5_bucket_np(np.clip(rel, 0, 128) * d)
        for e_, h in enumerate(heads):
            vals = rel_bias[:, g * 16 + h][bucket]
            out[g, e_] = np.where(valid, vals, np.float32(NEG))
    return out


def stage_dsa(nc, q_d, k_d, v_d, bias_d, ident_d, out_d, SL):
    NSEG = SL // SEG
    scale = float(128 ** -0.5)
    with contextlib.ExitStack() as st:
        P = Prog(nc, st)
        idb, idbb = P.sb("idb", [128, 128], BF16)
        P.dma("sp", lambda e: e.dma_start(out=idb[:], in_=ident_d[:, :]), writes=[idbb], sbuf=idbb)
        ones, onesb = P.sb("ones", [128, 128], BF16)
        P.op("pool", lambda e: e.memset(ones[:], 1.0), writes=[onesb])
        bias, biasb = P.sb("bias", [128, 6, 256], F32)
        P.dma("sp", lambda e: e.dma_start(out=bias[:], in_=bias_d.rearrange("g e k j -> k (g e) j")), writes=[biasb], sbuf=biasb)
        bias16, bias16b = P.sb("bias16", [128, 6, 256], BF16)
        P.op("dve", lambda e: e.tensor_scalar_mul(bias16[:], bias[:], 1.0 / scale), reads=[biasb], writes=[bias16b])
        qs = [P.sb(f"qs{i}", [128, SEG], BF16) for i in range(2)]
        kw = [P.sb(f"kw{i}", [128, 2 * SEG], BF16) for i in range(2)]
        vw = [P.sb(f"vw{i}", [128, 2 * SEG], BF16) for i in range(2)]
        vt = [P.sb(f"vt{i}", [128, 32, 128], BF16) for i in range(2)]
        acc, accb = P.sb("acc", [128, 2, SEG], F32)
        rec, recb = P.sb("rec", [128, SEG], F32)
        ot = [P.sb(f"ot{i}", [128, SEG], BF16) for i in range(2)]
        NSLOT = 5
        tmp = [P.sb(f"tmp{i}", [128, 256], F32) for i in range(NSLOT)]
        pt = [P.sb(f"pt{i}", [128, 256], BF16) for i in range(NSLOT)]
        pf = [P.ps(f"pf{i}", [128, 512]) for i in range(6)]
        pb = [P.ps(f"pb{i}", [128, 1024], BF16) for i in range(2)]
        it = 0
        nblk = 0
        npf = 0
        npb = 0
        for sg in range(NSEG):
            t_lo = sg * SEG
            for e_ in range(2):
                for g, d in enumerate(DILS):
                    q, qb = qs[it % 2]
                    kk, kb = kw[it % 2]
                    vv, vb = vw[it % 2]
                    vtt, vtb = vt[it % 2]
                    it += 1
                    P.dma("sp", lambda e, q=q, g=g, e_=e_, t_lo=t_lo: e.dma_start(out=q[:], in_=q_d[g, e_, :, t_lo:t_lo + SEG]), writes=[qb], sbuf=qb)
                    w0 = 0 if sg > 0 else SEG
                    P.dma("sp", lambda e, kk=kk, g=g, e_=e_, t_lo=t_lo, w0=w0: e.dma_start(out=kk[:, w0:2 * SEG], in_=k_d[g, e_, :, t_lo - SEG + w0:t_lo + SEG]), writes=[kb], sbuf=kb)
                    P.dma("sp", lambda e, vv=vv, g=g, e_=e_, t_lo=t_lo, w0=w0: e.dma_start(out=vv[:, w0:2 * SEG], in_=v_d[g, e_, :, t_lo - SEG + w0:t_lo + SEG]), writes=[vb], sbuf=vb)
                    nper = SEG // (128 * d)
                    def kcols(r, n_):
                        start = SEG + 128 * n_ * d + r
                        return slice(start, start + 127 * d + 1, d)
                    slots = []
                    for r in range(d):
                        for n_ in range(-1, nper):
                            if n_ == -1 and sg == 0:
                                continue
                            slots.append((r, n_))
                    for s0 in range(0, len(slots), 8):
                        grp = slots[s0:s0 + 8]
                        pT, pTb = pb[npb % 2]
                        npb += 1
                        for i_, (r, n_) in enumerate(grp):
                            P.op("pe", lambda e, pT=pT, i_=i_, vv=vv, c=kcols(r, n_): e.transpose(pT[:, i_ * 128:(i_ + 1) * 128], vv[:, c], idb[:]),
                                 reads=[vb, idbb], writes=[pTb])
                        n_g = len(grp)
                        P.op("act", lambda e, pT=pT, vtt=vtt, s0=s0, n_g=n_g: e.activation(
                            vtt[:, s0:s0 + n_g, :], pT[:, 0:n_g * 128].rearrange("p (s c) -> p s c", c=128), AF.Copy),
                            reads=[pTb], writes=[vtb])
                    sidx = {s: i for i, s in enumerate(slots)}

                    def blk(slot, r, n_, kk=kk, kb=kb, q=q, qb=qb, vtt=vtt, vtb=vtb, g=g, e_=e_, d=d, sidx=sidx):
                        has_prev = (r, n_ - 1) in sidx
                        qc = slice(128 * n_ * d + r, 128 * n_ * d + r + 127 * d + 1, d)
                        ps, psb = pf[slot]
                        lo = 0 if has_prev else 128
                        def kc(n2):
                            start = SEG + 128 * n2 * d + r
                            return slice(start, start + 127 * d + 1, d)
                        if has_prev:
                            P.op("pe", lambda e, c=kc(n_ - 1): e.matmul(ps[:, 0:128], kk[:, c], q[:, qc], start=True, stop=False),
                                 reads=[kb, qb], writes=[psb])
                        P.op("pe", lambda e, c=kc(n_): e.matmul(ps[:, 128:256], kk[:, c], q[:, qc], start=(not has_prev), stop=False),
                             reads=[kb, qb], writes=[psb])
                        P.op("pe", lambda e: e.matmul(ps[:, lo:256], idb[:], bias16[:, g * 2 + e_, lo:256], start=False, stop=True),
                             reads=[idbb, bias16b], writes=[psb])
                        yield None
                        pp, ppb = pt[slot]
                        P.op("act", lambda e: e.activation(pp[:, lo:256], ps[:, lo:256], AF.Exp, scale=scale), reads=[psb], writes=[ppb])
                        yield None
                        halves = ([(n_ - 1, 0)] if has_prev else []) + [(n_, 128)]
                        for hi, (nn, c0) in enumerate(halves):
                            si = sidx[(r, nn)]
                            P.op("pe", lambda e, si=si, c0=c0, hi=hi, nh=len(halves): e.matmul(
                                ps[:, 256:384], vtt[:, si, :], pp[:, c0:c0 + 128], start=(hi == 0), stop=(hi == nh - 1)),
                                reads=[vtb, ppb], writes=[psb])
                        for hi, (nn, c0) in enumerate(halves):
                            P.op("pe", lambda e, c0=c0, hi=hi, nh=len(halves): e.matmul(
                                ps[:, 384:512], ones[:], pp[:, c0:c0 + 128], start=(hi == 0), stop=(hi == nh - 1)),
                                reads=[onesb, ppb], writes=[psb])
                        src = lambda: ps[:, 256:512].rearrange("p (a c) -> p a c", a=2)
                        if g == 0:
                            P.op("act", lambda e: e.activation(acc[:, :, qc], src(), AF.Copy), reads=[psb], writes=[accb])
                        else:
                            P.op("dve", lambda e: e.tensor_tensor(acc[:, :, qc], acc[:, :, qc], src(), ALU.add), reads=[psb, accb], writes=[accb])
                        yield None

                    todo = [(r, n_) for r in range(d) for n_ in range(nper)]
                    active = []
                    free = list(range(NSLOT))
                    ti = 0
                    while ti < len(todo) or active:
                        if ti < len(todo) and free:
                            sl_ = free.pop(0)
                            active.append((blk(sl_, *todo[ti]), sl_))
                            ti += 1
                        for item in list(active):
                            try:
                                next(item[0])
                            except StopIteration:
                                active.remove(item)
                                free.append(item[1])
                o, ob = ot[(sg * 2 + e_) % 2]
                P.op("dve", lambda e: e.reciprocal(rec[:], acc[:, 1, :]), reads=[accb], writes=[recb])
                P.op("dve", lambda e, o=o: e.tensor_tensor(o[:], acc[:, 0, :], rec[:], ALU.mult), reads=[accb, recb], writes=[ob])
                P.dma("pool", lambda e, o=o, e_=e_, t_lo=t_lo: e.dma_start(out=out_d[e_, :, t_lo:t_lo + SEG], in_=o[:]), reads=[ob], sbuf=ob)
        P.barrier()
        P.emit()


T_ = TPC
NG = 4 * 4 * 16 + 16


def ba_extra(W_d, ba_d):
    def mk(P):
        wba, wbab = P.sb("wba", [128, 16, 32], BF16)
        bat, batb = P.sb("bat", [128, T_ // 128, 32], F32)
        psb_t, psb_b = P.ps("psba", [128, 512])

        def fn(act, actb, half):
            P.dma("pool", lambda e: e.dma_start(out=wba[:], in_=W_d[:, 8192:8224].rearrange("(k p) n -> p k n", p=128)),
                  writes=[wbab], sbuf=wbab)
            for tt in range(T_ // 128):
                for k in range(16):
                    P.op("pe", lambda e, tt=tt, k=k: e.matmul(psb_t[:, 0:32], act[:, k, tt * 128:(tt + 1) * 128], wba[:, k, :],
                                                              start=(k == 0), stop=(k == 15)), reads=[actb, wbab], writes=[psb_b])
                P.op("dve", lambda e, tt=tt: e.tensor_copy(bat[:, tt, :], psb_t[:, 0:32]), reads=[psb_b], writes=[batb])
            P.dma("sp", lambda e: e.dma_start(out=ba_d.rearrange("(n p) x -> p n x", p=128), in_=bat[:]), reads=[batb], sbuf=batb)
        return fn
    return mk


def gu_evac(hid_d):
    def factory(P):
        sgs = [P.sb(f"sg{i}", [128, TB], F32) for i in range(2)]
        hts = [P.sb(f"hid{i}", [128, T_], BF16) for i in range(3)]
        state = {"n": 0}

        def evac(u, tb, pss):
            (psg, psgb), (psu, psub) = pss
            sg, sgb = sgs[state["n"] % 2]
            state["n"] += 1
            ht, hb = hts[u % 3]
            sl = slice(tb * TB, (tb + 1) * TB)
            P.op("act", lambda e: e.activation(sg[:], psg[:], AF.Silu), reads=[psgb], writes=[sgb])
            P.op("dve", lambda e: e.tensor_tensor(ht[:, sl], sg[:], psu[:], ALU.mult), reads=[sgb, psub], writes=[hb])
            if tb == T_ // TB - 1:
                P.dma("sp", lambda e: e.dma_start(out=hid_d[u, :, :], in_=ht[:]), reads=[hb], sbuf=hb)
        return evac
    return factory


def plain_groups(ncols, gw=256):
    return [([(c0, gw)], [[o] for o in range(0, gw, 128)]) for c0 in range(0, ncols, gw)]


def build_ts(layer, do_block, nxt):
    nc = bass.Bass("TRN2", target_bir_lowering=False)
    dt = lambda name, shape, dty, kind: nc.dram_tensor(name, list(shape), dty, kind=kind).ap()
    x_in = dt("x_in", [16, 128, T_], F32, "ExternalInput")
    gains = dt("gains", [128, NG], F32, "ExternalInput")
    h_d = dt("h_scr", [16, 128, T_], BF16, "Internal")
    gcol = lambda l, w: (l * 4 + w) * 16
    x_cur = x_in
    if do_block:
        o_in = dt("o_in", [16, 128, T_], BF16, "ExternalInput")
        w_out = dt("w_out", [D, D], F32, "ExternalInput")
        w_gu = dt("w_gu", [D, 2 * DFF], F32, "ExternalInput")
        w_dn = dt("w_dn", [DFF, D], F32, "ExternalInput")
        f_d = dt("f_scr", [16, 128, T_], F32, "Internal")
        x_mid = dt("x_mid", [16, 128, T_], F32, "Internal")
        x_out = dt("x_out", [16, 128, T_], F32, "ExternalOutput")
        hid_d = dt("hid_scr", [DFF // 128, 128, T_], BF16, "Internal")
        piece_gemm(nc, o_in, 16, T_, w_out, plain_groups(D), store_evac(lambda u, half: f_d[u, :, :], T_, F32))
        piece_norm(nc, x_in, h_d, gains, gcol(layer, 2), T_, f_d=f_d, gcol_post=gcol(layer, 1), x_out_d=x_mid)
        gu_groups = [([(j * 128, 128), (DFF + j * 128, 128)], [[0, 128]]) for j in range(DFF // 128)]
        piece_gemm(nc, h_d, 16, T_, w_gu, gu_groups, gu_evac(hid_d))
        piece_gemm(nc, hid_d, DFF // 128, T_, w_dn, plain_groups(D, 128),
                   store_evac(lambda u, half: f_d[u, :, half * 1024:(half + 1) * 1024], 1024, F32), TR=1024)
        nxt_gcol = gcol(layer + 1, 0) if nxt is not None else None
        piece_norm(nc, x_mid, h_d if nxt is not None else None, gains, nxt_gcol, T_, f_d=f_d, gcol_post=gcol(layer, 3), x_out_d=x_out)
        x_cur = x_out
    else:
        piece_norm(nc, x_in, h_d, gains, gcol(layer, 0), T_)
    if nxt == "gdn":
        w_in = dt("w_in", [D, 8224], F32, "ExternalInput")
        proj = dt("proj", [64, 128, T_], BF16, "ExternalOutput")
        ba = dt("ba", [T_, 32], F32, "ExternalOutput")
        piece_gemm(nc, h_d, 16, T_, w_in, plain_groups(8192), store_evac(lambda u, half: proj[u, :, :], T_, BF16),
                   extra=ba_extra(w_in, ba))
    elif nxt in ("dsa", "dsa_kv"):
        if nxt == "dsa_kv":
            kv_w = dt("kv_w", [D, 12288], F32, "ExternalInput")
            kv = dt("kv", [96, 128, T_], BF16, "ExternalOutput")
            hk_d = dt("hk_scr", [16, 128, T_], BF16, "Internal")
            piece_norm(nc, x_cur, hk_d, gains, 256, T_)
            piece_gemm(nc, hk_d, 16, T_, kv_w, plain_groups(12288), store_evac(lambda u, half: kv[u, :, :], T_, BF16))
        w_q = dt("w_q", [D, 6144], F32, "ExternalInput")
        qo = dt("q_out", [48, 128, T_], BF16, "ExternalOutput")
        piece_gemm(nc, h_d, 16, T_, w_q, plain_groups(6144), store_evac(lambda u, half: qo[u, :, :], T_, BF16))
    return nc


def build_gdn():
    nc = bass.Bass("TRN2", target_bir_lowering=False)
    dt = lambda name, shape, dty, kind: nc.dram_tensor(name, list(shape), dty, kind=kind).ap()
    pj = dt("pj", [4, 2, 128, S], BF16, "ExternalInput")
    ba = dt("ba", [S, 4], F32, "ExternalInput")
    cw = dt("cw", [128, 6, 4], F32, "ExternalInput")
    hp = dt("hp", [128, 4], F32, "ExternalInput")
    on = dt("on", [128, 1], F32, "ExternalInput")
    gc = dt("gc", [128, GC_W], F32, "ExternalInput")
    gs = dt("gs", [2, 128], F32, "ExternalInput")
    out = dt("out", [2, 128, S], BF16, "ExternalOutput")
    stage_gdn_seq(nc, pj, ba, cw, hp, on, gc, gs, out, S)
    return nc


def build_dsa():
    nc = bass.Bass("TRN2", target_bir_lowering=False)
    dt = lambda name, shape, dty, kind: nc.dram_tensor(name, list(shape), dty, kind=kind).ap()
    qd = dt("q", [3, 2, 128, S], BF16, "ExternalInput")
    kd = dt("k", [3, 2, 128, S], BF16, "ExternalInput")
    vd = dt("v", [3, 2, 128, S], BF16, "ExternalInput")
    bd = dt("bias", [3, 2, 128, 256], F32, "ExternalInput")
    idd = dt("ident", [128, 128], BF16, "ExternalInput")
    out = dt("out", [2, 128, S], BF16, "ExternalOutput")
    stage_dsa(nc, qd, kd, vd, bd, idd, out, S)
    return nc


def _run(nc, in_maps):
    res = run_bass_kernel_spmd(nc, in_maps, core_ids=list(range(NCORES)))
    return res.results


def _to_heads(chunks_per_core, sel):
    outs = []
    for j in range(NCORES):
        ids = sel(j)
        outs.append(np.concatenate([chunks_per_core[c][ids] for c in range(NCORES)], axis=2))
    return outs


def _to_tokens(outs_per_core):
    full = np.concatenate(outs_per_core, axis=0)
    return [np.ascontiguousarray(full[:, :, c * T_:(c + 1) * T_]) for c in range(NCORES)]


def kernel(x, norm_gains, ffn_w_gate_up, ffn_w_down, gdn_w_in, gdn_conv_w, gdn_a_log, gdn_dt_bias,
           gdn_out_norm, gdn_w_out, kv_norm, kv_w, dsa_w_q, dsa_w_out, rel_bias):
    f32 = lambda a: np.ascontiguousarray(np.asarray(a, dtype=np.float32))
    x = f32(x)
    xs = [np.ascontiguousarray(x[0, c * T_:(c + 1) * T_, :].T.reshape(16, 128, T_)) for c in range(NCORES)]
    gall = np.concatenate([f32(norm_gains).reshape(16, D), f32(kv_norm)[None]], 0)
    gains = np.ascontiguousarray(gall.reshape(17, 16, 128).transpose(2, 0, 1).reshape(128, NG))
    gcs = gdn_consts()
    identb = np.eye(128, dtype=np.float32).astype(ml_dtypes.bfloat16)
    w_gu = f32(ffn_w_gate_up)
    w_dn = f32(ffn_w_down)
    w_in = f32(gdn_w_in)
    conv_w = f32(gdn_conv_w)
    a_log = f32(gdn_a_log)
    dtb = f32(gdn_dt_bias)
    onorm = f32(gdn_out_norm)
    g_w_out = f32(gdn_w_out)
    kvw = f32(kv_w)
    wq = f32(dsa_w_q)
    d_w_out = f32(dsa_w_out)
    rb = f32(rel_bias)

    def gdn_inputs(l, projs, bas):
        ba_full = np.concatenate(bas, axis=0)
        pjs = _to_heads(projs, lambda j: [kind * 16 + 2 * j + e for kind in range(4) for e in range(2)])
        maps = []
        for j in range(NCORES):
            hs = [2 * j, 2 * j + 1]
            cwn = np.zeros((128, 6, 4), np.float32)
            for kind in range(3):
                for e_, hd in enumerate(hs):
                    cwn[:, kind * 2 + e_, :] = conv_w[l][:, kind * 2048 + hd * 128: kind * 2048 + (hd + 1) * 128].T
            hpn = np.tile(np.array([a_log[l][hs[0]], a_log[l][hs[1]], dtb[l][hs[0]], dtb[l][hs[1]]], np.float32)[None], (128, 1))
            maps.append({"pj": np.ascontiguousarray(pjs[j].reshape(4, 2, 128, S)),
                         "ba": np.ascontiguousarray(ba_full[:, [hs[0], hs[1], 16 + hs[0], 16 + hs[1]]]),
                         "cw": cwn, "hp": hpn, "on": np.ascontiguousarray(onorm[l][:, None]),
                         "gc": gcs["gc"], "gs": gcs["gsel"]})
        return maps

    def dsa_inputs(qs_, kvs):
        qh = _to_heads(qs_, lambda j: [g * 16 + 2 * j + e for g in range(3) for e in range(2)])
        kh = _to_heads(kvs, lambda j: [g * 16 + 2 * j + e for g in range(3) for e in range(2)])
        vh = _to_heads(kvs, lambda j: [48 + g * 16 + 2 * j + e for g in range(3) for e in range(2)])
        return [{"q": np.ascontiguousarray(qh[j].reshape(3, 2, 128, S)), "k": np.ascontiguousarray(kh[j].reshape(3, 2, 128, S)),
                 "v": np.ascontiguousarray(vh[j].reshape(3, 2, 128, S)),
                 "bias": dsa_bias_tiles(rb, [2 * j, 2 * j + 1]), "ident": identb} for j in range(NCORES)]

    r = _run(build_ts(0, False, "gdn"), [{"x_in": xs[c], "gains": gains, "w_in": w_in[0]} for c in range(NCORES)])
    projs, bas = [r[c]["proj"] for c in range(NCORES)], [r[c]["ba"] for c in range(NCORES)]
    nc_gdn = build_gdn()
    nc_dsa = None
    kvs = None
    for l in range(4):
        if l < 2:
            ro = _run(nc_gdn if l == 0 else build_gdn(), gdn_inputs(l, projs, bas))
        else:
            ro = _run(build_dsa(), dsa_inputs(qs_, kvs))
        o_tok = _to_tokens([ro[j]["out"] for j in range(NCORES)])
        nxt = ["gdn", "dsa_kv", "dsa", None][l]
        w_o = g_w_out[l] if l < 2 else d_w_out[l - 2]
        maps = []
        for c in range(NCORES):
            m = {"x_in": xs[c], "gains": gains, "o_in": o_tok[c], "w_out": w_o, "w_gu": w_gu[l], "w_dn": w_dn[l]}
            if nxt == "gdn":
                m["w_in"] = w_in[l + 1]
            if nxt == "dsa_kv":
                m["kv_w"] = kvw
            if nxt in ("dsa", "dsa_kv"):
                m["w_q"] = wq[l - 1]
            maps.append(m)
        r = _run(build_ts(l, True, nxt), maps)
        xs = [r[c]["x_out"] for c in range(NCORES)]
        if nxt == "gdn":
            projs, bas = [r[c]["proj"] for c in range(NCORES)], [r[c]["ba"] for c in range(NCORES)]
        if nxt == "dsa_kv":
            kvs = [r[c]["kv"] for c in range(NCORES)]
        if nxt in ("dsa", "dsa_kv"):
            qs_ = [r[c]["q_out"] for c in range(NCORES)]
    out = np.empty((1, S, D), np.float32)
    for c in range(NCORES):
        out[0, c * T_:(c + 1) * T_, :] = xs[c].reshape(D, T_).T
    return out
```
